# Optimizing a Trainium2 kernel written in Bass

```python
import math
import jax, jax.numpy as jnp
from jax import lax
import numpy as np

D_MODEL = 2048
BATCH = 2
SEQ = 8192
DEPTH = 2

MEM_LEN = 256
EPS = 1e-6
DA_HEADS = 8
DA_HEAD_DIM = 64
DA_WIDTH = DA_HEADS * 2 * DA_HEAD_DIM
Q_BLOCK = 128
POOL_WINDOWS = (2, 4, 8, 16)
POOL_WIDTH = D_MODEL // 2
POOL_GROUP = POOL_WIDTH // len(POOL_WINDOWS)
LRU_WIDTH = D_MODEL // 2
LRU_BLOCKS = 8
LRU_BLOCK = LRU_WIDTH // LRU_BLOCKS
CONV_WIDTH = 4
LRU_C = 8.0
MEM_HEADS = 4
MEM_HEAD_DIM = 128
MEM_WIDTH = MEM_HEADS * MEM_HEAD_DIM
N_BRANCH = 4
BRANCH_SIZES = (DA_WIDTH, POOL_WIDTH, LRU_WIDTH, MEM_WIDTH)
BRANCH_OFFSETS = (0, DA_WIDTH, DA_WIDTH + POOL_WIDTH, DA_WIDTH + POOL_WIDTH + LRU_WIDTH, DA_WIDTH + POOL_WIDTH + LRU_WIDTH + MEM_WIDTH)
BRANCH_WIDTH = BRANCH_OFFSETS[-1]
SPLIT_SIZES = (DA_WIDTH, DA_WIDTH, DA_WIDTH, POOL_WIDTH, LRU_WIDTH, MEM_WIDTH, BRANCH_WIDTH, N_BRANCH * D_MODEL)
SPLIT_POINTS = tuple(int(v) for v in np.cumsum(SPLIT_SIZES)[:-1])
N_IN = int(sum(SPLIT_SIZES))

kernel_name = "gated_parallel_hybrid_diffattn_pool_rglru_mem"


def rms_norm(x, g):
    xf = x.astype(jnp.float32)
    y = xf * lax.rsqrt(jnp.mean(xf * xf, axis=-1, keepdims=True) + EPS)
    return (y * g.astype(jnp.float32)).astype(x.dtype)


def block_diag_linear(x, w):
    b, s, _ = x.shape
    g, n, _ = w.shape
    return jnp.einsum('bsgi,gij->bsgj', x.reshape(b, s, g, n), w).reshape(b, s, g * n)


def diff_attention(q, k, v, lam_vecs, subln_g, lambda_init):
    b, s, _ = q.shape
    q = q.reshape(b, s, DA_HEADS, 2, DA_HEAD_DIM)
    k = k.reshape(b, s, DA_HEADS, 2, DA_HEAD_DIM)
    v = v.reshape(b, s, DA_HEADS, 2 * DA_HEAD_DIM)
    lf = lam_vecs.astype(jnp.float32)
    lam = jnp.exp(jnp.sum(lf[0] * lf[1])) - jnp.exp(jnp.sum(lf[2] * lf[3])) + lambda_init
    n_blk = s // Q_BLOCK
    q_blocks = q.reshape(b, n_blk, Q_BLOCK, DA_HEADS, 2, DA_HEAD_DIM).transpose(1, 0, 2, 3, 4, 5)
    k_pos = jnp.arange(s)
    scale = DA_HEAD_DIM ** -0.5

    def one_block(args):
        qb, blk = args
        q_pos = blk * Q_BLOCK + jnp.arange(Q_BLOCK)
        sc = jnp.einsum('bqhcd,bkhcd->bhcqk', qb, k).astype(jnp.float32) * scale
        causal = k_pos[None, :] <= q_pos[:, None]
        sc = jnp.where(causal, sc, -jnp.inf)
        p = jax.nn.softmax(sc, axis=-1)
        p_diff = (p[:, :, 0] - lam * p[:, :, 1]).astype(v.dtype)
        return jnp.einsum('bhqk,bkhe->bqhe', p_diff, v)

    o = lax.map(one_block, (q_blocks, jnp.arange(n_blk)))
    o = o.transpose(1, 0, 2, 3, 4).reshape(b, s, DA_HEADS, 2 * DA_HEAD_DIM)
    o = rms_norm(o, subln_g) * (1.0 - lambda_init)
    return o.reshape(b, s, DA_WIDTH)


def multiscale_pool(x, w_grp, scale):
    b, s, _ = x.shape
    xg = x.reshape(b, s, len(POOL_WINDOWS), POOL_GROUP)
    t = jnp.arange(s)
    outs = []
    for gi, w in enumerate(POOL_WINDOWS):
        xi = xg[:, :, gi].astype(jnp.float32)
        c = jnp.pad(jnp.cumsum(xi, axis=1), ((0, 0), (1, 0), (0, 0)))
        upper = c[:, 1:]
        lower = jnp.pad(c[:, :s - w + 1], ((0, 0), (w - 1, 0), (0, 0)))
        count = jnp.minimum(t + 1, w).astype(jnp.float32)
        mean = (upper - lower) / count[None, :, None]
        outs.append((mean - xi).astype(x.dtype))
    d = jnp.stack(outs, axis=2)
    y = jnp.einsum('bsgi,gij->bsgj', d, w_grp).reshape(b, s, POOL_WIDTH)
    return y * scale


def rg_lru_branch(x, conv_w, conv_b, w_a, b_a, w_x, b_x, lam):
    b, s, _ = x.shape
    xc = lax.conv_general_dilated(x, conv_w[:, None, :], window_strides=(1,),
                                  padding=[(CONV_WIDTH - 1, 0)],
                                  dimension_numbers=('NWC', 'WIO', 'NWC'),
                                  feature_group_count=LRU_WIDTH) + conv_b
    r = jax.nn.sigmoid(block_diag_linear(xc, w_a) + b_a).astype(jnp.float32)
    i = jax.nn.sigmoid(block_diag_linear(xc, w_x) + b_x)
    log_a = -LRU_C * r * jax.nn.softplus(-lam.astype(jnp.float32))
    a = jnp.exp(log_a)
    mult = jnp.sqrt(-jnp.expm1(2.0 * log_a))
    mult = jnp.where((jnp.arange(s) == 0)[None, :, None], 1.0, mult)
    u = mult * (i * xc).astype(jnp.float32)

    def combine(left, right):
        a_l, b_l = left
        a_r, b_r = right
        return a_l * a_r, a_r * b_l + b_r

    _, h = lax.associative_scan(combine, (a, u), axis=1)
    return h.astype(x.dtype)


def memory_attention(q, mem_k, mem_v):
    b, s, _ = q.shape
    q = q.reshape(b, s, MEM_HEADS, MEM_HEAD_DIM)
    sc = jnp.einsum('bshd,bmhd->bhsm', q, mem_k).astype(jnp.float32) * (MEM_HEAD_DIM ** -0.5)
    p = jax.nn.softmax(sc, axis=-1).astype(mem_v.dtype)
    return jnp.einsum('bhsm,bmhd->bshd', p, mem_v).reshape(b, s, MEM_WIDTH)


def hybrid_layer(x, mem, layer_idx, norm_g, w_in, lam_vecs, subln_g, pool_w, pool_scale,
                 conv_w, conv_b, lru_wa, lru_ba, lru_wx, lru_bx, lru_lambda,
                 mem_norm_g, w_mem_kv, w_branch, w_out):
    b, s, _ = x.shape
    h = rms_norm(x, norm_g)
    proj = h @ w_in
    q, k, v, x_pool, x_lru, q_mem, gate_in, merge_in = jnp.split(proj, SPLIT_POINTS, axis=-1)
    lambda_init = 0.8 - 0.6 * math.exp(-0.3 * layer_idx)
    y_a = diff_attention(q, k, v, lam_vecs, subln_g, lambda_init)
    y_b = multiscale_pool(x_pool, pool_w, pool_scale)
    y_c = rg_lru_branch(x_lru, conv_w, conv_b, lru_wa, lru_ba, lru_wx, lru_bx, lru_lambda)
    mkv = rms_norm(mem, mem_norm_g) @ w_mem_kv
    mk, mv = jnp.split(mkv, 2, axis=-1)
    mk = mk.reshape(b, MEM_LEN, MEM_HEADS, MEM_HEAD_DIM)
    mv = mv.reshape(b, MEM_LEN, MEM_HEADS, MEM_HEAD_DIM)
    y_m = memory_attention(q_mem, mk, mv)
    branches = jnp.concatenate([y_a, y_b, y_c, y_m], axis=-1) * jax.nn.silu(gate_in)
    merge_gates = jax.nn.sigmoid(merge_in).reshape(b, s, N_BRANCH, D_MODEL)
    merged = jnp.zeros_like(x)
    for bi in range(N_BRANCH):
        st, en = BRANCH_OFFSETS[bi], BRANCH_OFFSETS[bi + 1]
        merged = merged + merge_gates[:, :, bi] * (branches[..., st:en] @ w_branch[st:en])
    return x + merged @ w_out


def setup_inputs(seed: int = 0) -> dict:
    key = jax.random.key(seed)
    ks = jax.random.split(key, 24)
    f32 = jnp.float32
    nrm = lambda k, shape, sc: jax.random.normal(k, shape, f32) * sc
    x = jax.random.normal(ks[0], (BATCH, SEQ, D_MODEL), f32)
    mem = jax.random.normal(ks[1], (BATCH, MEM_LEN, D_MODEL), f32)
    norm_g = 1.0 + nrm(ks[2], (DEPTH, D_MODEL), 0.02)
    w_in = nrm(ks[3], (DEPTH, D_MODEL, N_IN), D_MODEL ** -0.5)
    lam_vecs = nrm(ks[4], (DEPTH, 4, DA_HEAD_DIM), 0.1)
    subln_g = 1.0 + nrm(ks[5], (DEPTH, 2 * DA_HEAD_DIM), 0.02)
    pool_w = nrm(ks[6], (DEPTH, len(POOL_WINDOWS), POOL_GROUP, POOL_GROUP), POOL_GROUP ** -0.5)
    pool_scale = 1.0 + nrm(ks[7], (DEPTH, POOL_WIDTH), 0.02)
    conv_w = nrm(ks[8], (DEPTH, CONV_WIDTH, LRU_WIDTH), CONV_WIDTH ** -0.5)
    conv_b = nrm(ks[9], (DEPTH, LRU_WIDTH), 0.01)
    lru_wa = nrm(ks[10], (DEPTH, LRU_BLOCKS, LRU_BLOCK, LRU_BLOCK), LRU_BLOCK ** -0.5)
    lru_ba = nrm(ks[11], (DEPTH, LRU_WIDTH), 0.01)
    lru_wx = nrm(ks[12], (DEPTH, LRU_BLOCKS, LRU_BLOCK, LRU_BLOCK), LRU_BLOCK ** -0.5)
    lru_bx = nrm(ks[13], (DEPTH, LRU_WIDTH), 0.01)
    a_pow_c = jax.random.uniform(ks[14], (DEPTH, LRU_WIDTH), f32, 0.9, 0.999)
    a0 = a_pow_c ** (1.0 / LRU_C)
    lru_lambda = jnp.log(a0) - jnp.log1p(-a0)
    mem_norm_g = 1.0 + nrm(ks[15], (DEPTH, D_MODEL), 0.02)
    w_mem_kv = nrm(ks[16], (DEPTH, D_MODEL, 2 * MEM_WIDTH), D_MODEL ** -0.5)
    bkeys = jax.random.split(ks[17], N_BRANCH)
    w_branch = jnp.concatenate([nrm(bkeys[i], (DEPTH, BRANCH_SIZES[i], D_MODEL), BRANCH_SIZES[i] ** -0.5)
                                for i in range(N_BRANCH)], axis=1)
    w_out = nrm(ks[18], (DEPTH, D_MODEL, D_MODEL), D_MODEL ** -0.5)
    final_g = 1.0 + nrm(ks[19], (D_MODEL,), 0.02)
    return {"x": x, "mem": mem, "norm_g": norm_g, "w_in": w_in, "lam_vecs": lam_vecs,
            "subln_g": subln_g, "pool_w": pool_w, "pool_scale": pool_scale,
            "conv_w": conv_w, "conv_b": conv_b, "lru_wa": lru_wa, "lru_ba": lru_ba,
            "lru_wx": lru_wx, "lru_bx": lru_bx, "lru_lambda": lru_lambda,
            "mem_norm_g": mem_norm_g, "w_mem_kv": w_mem_kv, "w_branch": w_branch,
            "w_out": w_out, "final_g": final_g}


def reference(x, mem, norm_g, w_in, lam_vecs, subln_g, pool_w, pool_scale, conv_w, conv_b,
              lru_wa, lru_ba, lru_wx, lru_bx, lru_lambda, mem_norm_g, w_mem_kv, w_branch,
              w_out, final_g):
    for l in range(DEPTH):
        x = hybrid_layer(x, mem, l, norm_g[l], w_in[l], lam_vecs[l], subln_g[l], pool_w[l],
                         pool_scale[l], conv_w[l], conv_b[l], lru_wa[l], lru_ba[l], lru_wx[l],
                         lru_bx[l], lru_lambda[l], mem_norm_g[l], w_mem_kv[l], w_branch[l], w_out[l])
    return rms_norm(x, final_g)
```

```python
import contextlib
import math
import numpy as np
import concourse.bass as bass
import concourse.mybir as mybir
from concourse.bass_utils import run_bass_kernel_spmd

F32 = mybir.dt.float32
BF16 = mybir.dt.bfloat16
AF = mybir.ActivationFunctionType
ALU = mybir.AluOpType
AX = mybir.AxisListType

D = 2048
NIN = 17408
CK = 2048
MEM = 256
EPS = 1e-6
ENGS = ("pe", "act", "dve", "pool", "sp")

LW = 16 + 16 + 8 + 32 + 8 + 8 + 8 + 8 + 1 + 256
O_NG, O_MG, O_PS, O_CW, O_CB, O_BA, O_BX, O_LL, O_SG, O_LV = 0, 16, 32, 40, 72, 80, 88, 96, 104, 105
O_FG = 2 * LW
O_IC = O_FG + 16
NCST = O_IC + 64


class _Stop(Exception):
    pass


class T:
    def __init__(self, t, name=""):
        self.t = t
        self.name = name
        self.w = {}
        self.r = {}
        self.chan = None

    def __getitem__(self, k):
        return self.t[k]


class Chan:
    def __init__(self, sem):
        self.sem = sem
        self.count = 0


def _merge(d, s):
    for k, v in s.items():
        if d.get(k, 0) < v:
            d[k] = v


class Sched:
    def __init__(self, nc):
        self.nc = nc
        self.e = dict(pe=nc.tensor, act=nc.scalar, dve=nc.vector, pool=nc.gpsimd, sp=nc.sync)
        self.semobj = {}
        self.cnt = {}
        self.seen = {k: {} for k in ENGS}
        for k in ENGS:
            self.semobj[("c", k)] = nc.alloc_semaphore("cs_" + k)
            self.cnt[k] = 0
        self.chans = []
        self.free_ch = []
        self.pending = {k: False for k in ENGS}

    def _wait(self, eng, deps):
        for key, val in deps.items():
            if eng == "pe" and key == ("c", "pe"):
                continue
            if self.seen[eng].get(key, 0) >= val:
                continue
            self.e[eng].wait_ge(self.semobj[key], val)
            self.seen[eng][key] = val

    def op(self, eng, fn, reads=(), writes=(), inc=True):
        deps = {}
        for b in reads:
            _merge(deps, b.w)
        for b in writes:
            _merge(deps, b.w)
            _merge(deps, b.r)
        self._wait(eng, deps)
        ins = fn(self.e[eng])
        key = ("c", eng)
        val = self.cnt[eng] + 1
        if inc:
            ins.then_inc(self.semobj[key], 1)
            self.cnt[eng] = val
            self.pending[eng] = False
        else:
            self.pending[eng] = True
        for b in reads:
            if b.r.get(key, 0) < val:
                b.r[key] = val
        for b in writes:
            if b.w.get(key, 0) < val:
                b.w[key] = val
            b.r = {}
        return ins

    def get_chan(self, sb):
        if sb.chan is None:
            if self.free_ch:
                sb.chan = self.free_ch.pop()
            else:
                ch = Chan(self.nc.alloc_semaphore("ch%d" % len(self.chans)))
                self.chans.append(ch)
                self.semobj[("d", id(ch))] = ch.sem
                sb.chan = ch
        return sb.chan

    def dma(self, q, out_ap, in_ap, sb, reads=(), writes=()):
        deps = {}
        for b in reads:
            _merge(deps, b.w)
        for b in writes:
            _merge(deps, b.w)
            _merge(deps, b.r)
        self._wait(q, deps)
        ch = self.get_chan(sb)
        ins = self.e[q].dma_start(out=out_ap, in_=in_ap)
        ch.count += 16
        ins.then_inc(ch.sem, 16)
        key = ("d", id(ch))
        val = ch.count
        for b in reads:
            if b.r.get(key, 0) < val:
                b.r[key] = val
        for b in writes:
            if b.w.get(key, 0) < val:
                b.w[key] = val
            b.r = {}
        return ins

    def barrier(self, engs=ENGS):
        assert not any(self.pending.values()), self.pending
        marks = {}
        for k in ENGS:
            if self.cnt[k] > 0:
                marks[("c", k)] = self.cnt[k]
        for ch in self.chans:
            if ch.count > 0:
                marks[("d", id(ch))] = ch.count
        for eng in engs:
            deps = dict(marks)
            self._wait(eng, deps)

    def retire(self, bufs):
        for b in bufs:
            if b.chan is not None:
                self.free_ch.append(b.chan)
                b.chan = None


def build(nchunk, lam_inits=(0.2, 0.8 - 0.6 * math.exp(-0.3)), dbg=None):
    S_ = nchunk * CK
    nc = bass.Bass("TRN2", target_bir_lowering=False)
    xT = nc.dram_tensor("xT", [D, S_], F32, kind="ExternalInput").ap()
    memT = nc.dram_tensor("memT", [D, MEM], F32, kind="ExternalInput").ap()
    cst_d = nc.dram_tensor("cst", [128, NCST], F32, kind="ExternalInput").ap()
    tri_d = nc.dram_tensor("tri", [128, 128], F32, kind="ExternalInput").ap()
    w_in = nc.dram_tensor("w_in", [2, D, NIN], F32, kind="ExternalInput").ap()
    pool_w = nc.dram_tensor("pool_w", [2, 4, 256, 256], F32, kind="ExternalInput").ap()
    lru_wa = nc.dram_tensor("lru_wa", [2, 8, 128, 128], F32, kind="ExternalInput").ap()
    lru_wx = nc.dram_tensor("lru_wx", [2, 8, 128, 128], F32, kind="ExternalInput").ap()
    w_mem_kv = nc.dram_tensor("w_mem_kv", [2, D, 1024], F32, kind="ExternalInput").ap()
    w_branch = nc.dram_tensor("w_branch", [2, 3584, D], F32, kind="ExternalInput").ap()
    w_out = nc.dram_tensor("w_out", [2, D, D], F32, kind="ExternalInput").ap()
    outT = nc.dram_tensor("outT", [D, S_], F32, kind="ExternalOutput").ap()
    x1T = nc.dram_tensor("x1T", [D, S_], F32, kind="Internal").ap()
    x2T = nc.dram_tensor("x2T", [D, S_], F32, kind="Internal").ap()
    KT_d = nc.dram_tensor("KT_d", [2, 8, 128, S_], BF16, kind="Internal").ap()
    V_d = nc.dram_tensor("V_d", [2, 8, S_, 128], BF16, kind="Internal").ap()
    QT_d = nc.dram_tensor("QT_d", [8, 128, CK], BF16, kind="Internal").ap()
    SGA_d = nc.dram_tensor("SGA_d", [8, 128, CK], BF16, kind="Internal").ap()
    G_d = nc.dram_tensor("G_d", [28, 128, CK], BF16, kind="Internal").ap()
    Hs_d = nc.dram_tensor("Hs_d", [16, 128, CK], BF16, kind="Internal").ap()
    MG_d = nc.dram_tensor("MG_d", [16, 128, CK], BF16, kind="Internal").ap()

    S = Sched(nc)
    es = contextlib.ExitStack()
    import os
    stop_at = os.environ.get("KSTOP", "")

    def chk(name):
        if name == stop_at:
            S.barrier()
            raise _Stop()

    uid = [0]

    def sb(stack, name, shape, dt):
        uid[0] += 1
        name = "%s_u%d" % (name, uid[0])
        return T(stack.enter_context(nc.sbuf_tensor(name, shape, dt)), name)

    cst = sb(es, "cstb", [128, NCST], F32)
    ones_f = sb(es, "ones_f", [128, 128], F32)
    ones_b = sb(es, "ones_b", [128, 128], BF16)
    tri = sb(es, "trib", [128, 128], BF16)
    epsb = sb(es, "epsb", [128, 1], F32)
    PH = sb(es, "PH", [128, 8, 16], F32)
    LH = sb(es, "LH", [128, 8, 4], F32)
    CAR = sb(es, "CAR", [128, 8], F32)
    CL = sb(es, "CL", [128, 8], F32)
    CL2 = sb(es, "CL2", [128, 8], F32)
    NEGLAM = sb(es, "NEGLAM", [128, 1], F32)
    SGL = sb(es, "SGL", [128, 1], F32)
    MK = sb(es, "MK", [128, 4, MEM], BF16)
    MV = sb(es, "MV", [128, 2, 512], BF16)
    smallt = sb(es, "smallt", [128, 8, 64], F32)
    PSB = [T(es.enter_context(nc.psum_tensor("ps%d" % i, [128, 512], F32)), "ps%d" % i) for i in range(8)]

    S.dma("sp", cst[:], cst_d, cst, writes=[cst])
    S.dma("pool", tri[:], tri_d, tri, writes=[tri])
    S.op("dve", lambda e: e.memset(ones_f[:], 1.0), writes=[ones_f])
    S.op("dve", lambda e: e.memset(ones_b[:], 1.0), writes=[ones_b])
    S.op("dve", lambda e: e.memset(epsb[:], EPS), writes=[epsb])

    def rms_tile(stack_bufs, xt, ncols, gcol, out_fn, sqs, rstd_bufs, psb):
        t1, t2 = rstd_bufs
        for c in range(16):
            sq = sqs[c % len(sqs)]
            S.op("act", lambda e, c=c, sq=sq: e.activation(sq[:, 0:ncols], xt[:, c, 0:ncols], AF.Square),
                 reads=[xt], writes=[sq])
            S.op("pe", lambda e, c=c, sq=sq: e.matmul(psb[:, 0:ncols], ones_f[:], sq[:, 0:ncols],
                                                     start=(c == 0), stop=(c == 15)),
                 reads=[ones_f, sq], writes=[psb], inc=True)
        S.op("act", lambda e: e.activation(t1[:, 0:ncols], psb[:, 0:ncols], AF.Sqrt, bias=epsb[:, 0:1], scale=1.0 / D),
             reads=[psb, epsb], writes=[t1])
        S.op("dve", lambda e: e.reciprocal(t2[:, 0:ncols], t1[:, 0:ncols]), reads=[t1], writes=[t2])
        for c in range(16):
            oap, ot = out_fn(c)
            S.op("dve", lambda e, c=c, oap=oap: e.scalar_tensor_tensor(
                out=oap, in0=xt[:, c, 0:ncols], scalar=cst[:, gcol + c:gcol + c + 1], in1=t2[:, 0:ncols],
                op0=ALU.mult, op1=ALU.mult), reads=[xt, cst, t2], writes=[ot])
        return t2

    try:
      for l in range(2):
        LB = l * LW
        lam_init = lam_inits[l]
        xsrc = xT if l == 0 else x1T
        xdst = x1T if l == 0 else x2T
        with contextlib.ExitStack() as ps_:
            S.barrier()
            HM = sb(ps_, "HM", [128, 16, MEM], BF16)
            mt_ = sb(ps_, "mt_", [128, 16, MEM], F32)
            WM = sb(ps_, "WM", [128, 16, 1024], BF16)
            sqs = [sb(ps_, "sqm%d" % i, [128, 512], F32) for i in range(3)]
            r1 = sb(ps_, "r1m", [128, 512], F32)
            r2 = sb(ps_, "r2m", [128, 512], F32)
            S.dma("sp", mt_[:], memT.rearrange("(c p) n -> p c n", p=128), mt_, writes=[mt_])
            S.dma("pool", WM[:], w_mem_kv[l].rearrange("(c p) n -> p c n", p=128), WM, writes=[WM])
            rms_tile(None, mt_, MEM, LB + O_MG, lambda c: (HM[:, c, :], HM), sqs, (r1, r2), PSB[7])
            for hm in range(4):
                pb = PSB[hm % 4]
                for c in range(16):
                    S.op("pe", lambda e, c=c, hm=hm, pb=pb: e.matmul(pb[:, 0:MEM], WM[:, c, hm * 128:(hm + 1) * 128],
                                                                    HM[:, c, :], start=(c == 0), stop=(c == 15)),
                         reads=[WM, HM], writes=[pb], inc=(c == 15))
                S.op("act", lambda e, hm=hm, pb=pb: e.copy(MK[:, hm, :], pb[:, 0:MEM]), reads=[pb], writes=[MK])
            for mb in range(2):
                pb = PSB[4 + mb]
                for c in range(16):
                    S.op("pe", lambda e, c=c, mb=mb, pb=pb: e.matmul(pb[:], HM[:, c, mb * 128:(mb + 1) * 128],
                                                                    WM[:, c, 512:1024], start=(c == 0), stop=(c == 15)),
                         reads=[WM, HM], writes=[pb], inc=(c == 15))
                S.op("act", lambda e, mb=mb, pb=pb: e.copy(MV[:, mb, :], pb[:]), reads=[pb], writes=[MV])
            st = smallt
            S.op("act", lambda e: e.activation(st[:, 0, 0:8], cst[:, LB + O_LL:LB + O_LL + 8], AF.Exp, scale=-1.0),
                 reads=[cst], writes=[st])
            S.op("dve", lambda e: e.tensor_scalar_add(st[:, 0, 0:8], st[:, 0, 0:8], 1.0), reads=[st], writes=[st])
            S.op("act", lambda e: e.activation(st[:, 1, 0:8], st[:, 0, 0:8], AF.Ln), reads=[st], writes=[st])
            S.op("dve", lambda e: e.tensor_scalar_mul(CL[:], st[:, 1, 0:8], -8.0), reads=[st], writes=[CL])
            S.op("dve", lambda e: e.tensor_scalar_mul(CL2[:], st[:, 1, 0:8], -16.0), reads=[st], writes=[CL2])
            lv = LB + O_LV
            S.op("dve", lambda e: e.tensor_tensor(out=st[:, 2, 0:64], in0=cst[:, lv:lv + 64], in1=cst[:, lv + 64:lv + 128],
                                                  op=ALU.mult), reads=[cst, st], writes=[st])
            S.op("dve", lambda e: e.tensor_tensor(out=st[:, 3, 0:64], in0=cst[:, lv + 128:lv + 192],
                                                  in1=cst[:, lv + 192:lv + 256], op=ALU.mult), reads=[cst, st], writes=[st])
            S.op("dve", lambda e: e.reduce_sum(out=st[:, 4, 0:1], in_=st[:, 2, 0:64], axis=AX.X), reads=[st], writes=[st])
            S.op("dve", lambda e: e.reduce_sum(out=st[:, 4, 1:2], in_=st[:, 3, 0:64], axis=AX.X), reads=[st], writes=[st])
            S.op("act", lambda e: e.activation(st[:, 5, 0:2], st[:, 4, 0:2], AF.Exp), reads=[st], writes=[st])
            S.op("dve", lambda e: e.tensor_tensor(out=st[:, 6, 0:1], in0=st[:, 5, 1:2], in1=st[:, 5, 0:1], op=ALU.subtract),
                 reads=[st], writes=[st])
            S.op("dve", lambda e: e.tensor_scalar_add(NEGLAM[:], st[:, 6, 0:1], -lam_init), reads=[st], writes=[NEGLAM])
            S.op("dve", lambda e: e.tensor_scalar_mul(SGL[:], cst[:, LB + O_SG:LB + O_SG + 1], 1.0 - lam_init),
                 reads=[cst], writes=[SGL])
            S.op("dve", lambda e: e.memset(PH[:], 0.0), writes=[PH])
            S.op("dve", lambda e: e.memset(LH[:], 0.0), writes=[LH])
            S.op("dve", lambda e: e.memset(CAR[:], 0.0), writes=[CAR])
            S.barrier()
            S.retire([HM, mt_, WM])
        chk("setup")

        for ck in range(nchunk):
            c0 = ck * CK
            with contextlib.ExitStack() as p1:
                H = sb(p1, "H", [128, 16, CK], BF16)
                with contextlib.ExitStack() as p0:
                    xts = [sb(p0, "xt%d" % i, [128, 16, 512], F32) for i in range(2)]
                    sqs = [sb(p0, "sq%d" % i, [128, 512], F32) for i in range(3)]
                    r1 = sb(p0, "r1", [128, 512], F32)
                    r2 = sb(p0, "r2", [128, 512], F32)
                    for tt in range(4):
                        xt = xts[tt % 2]
                        S.dma("sp", xt[:], xsrc.rearrange("(c p) n -> p c n", p=128)[:, :, c0 + tt * 512:c0 + (tt + 1) * 512],
                              xt, writes=[xt])
                        rms_tile(None, xt, 512, LB + O_NG, lambda c, tt=tt: (H[:, c, tt * 512:(tt + 1) * 512], H),
                                 sqs, (r1, r2), PSB[7])
                    S.dma("sp", Hs_d.rearrange("c p n -> p c n"), H[:], H, reads=[H])
                    S.barrier()
                    S.retire(xts)
                chk("p0")
                Wt = [sb(p1, "Wt%d" % i, [128, 16, 512], BF16) for i in range(3)]
                wstate = {"next": 0, "map": {}}
                wseq = list(range(18))

                def wload(g):
                    if g in wstate["map"]:
                        return wstate["map"][g]
                    slot = wstate["next"] % 3
                    wstate["next"] += 1
                    for k in [k for k, v in wstate["map"].items() if v is Wt[slot]]:
                        del wstate["map"][k]
                    S.dma("pool", Wt[slot][:], w_in[l].rearrange("(c p) n -> p c n", p=128)[:, :, g * 512:(g + 1) * 512],
                          Wt[slot], writes=[Wt[slot]])
                    wstate["map"][g] = Wt[slot]
                    return Wt[slot]

                pair_i = [0]

                def next_pair():
                    k = pair_i[0] % 4
                    pair_i[0] += 1
                    return PSB[2 * k], PSB[2 * k + 1]

                def inproj_half(g, mloc, half):
                    W = wload(g)
                    if g + 1 < 18:
                        pass
                    pa, pb = next_pair()
                    for n, p_ in enumerate((pa, pb)):
                        for c in range(16):
                            S.op("pe", lambda e, c=c, n=n, p_=p_, W=W: e.matmul(
                                p_[:], W[:, c, mloc * 128:(mloc + 1) * 128],
                                H[:, c, half * 1024 + n * 512: half * 1024 + (n + 1) * 512],
                                start=(c == 0), stop=(c == 15)), reads=[W, H], writes=[p_], inc=(c == 15))
                    return pa, pb

                stg = [sb(p1, "stg%d" % i, [128, 1024], BF16) for i in range(4)]
                stg_i = [0]

                def next_stg():
                    s_ = stg[stg_i[0] % 4]
                    stg_i[0] += 1
                    return s_

                def evac_pair(eng, pa, pb, dst, fn):
                    for n, p_ in enumerate((pa, pb)):
                        S.op(eng, lambda e, n=n, p_=p_: fn(e, dst[:, n * 512:(n + 1) * 512], p_[:]),
                             reads=[p_], writes=[dst])

                wload(0)
                wload(1)
                for sec, (dst_d, scale) in enumerate(((QT_d, 0.125), (None, 1.0))):
                    for h in range(8):
                        g, mloc = sec * 2 + h // 4, h % 4
                        if mloc == 0 and g + 2 < 18:
                            pass
                        for half in range(2):
                            pa, pb = inproj_half(g, mloc, half)
                            s_ = next_stg()
                            evac_pair("act", pa, pb, s_, lambda e, o, i, scale=scale: e.activation(o, i, AF.Copy, scale=scale))
                            if sec == 0:
                                S.dma("sp", QT_d[h, :, half * 1024:(half + 1) * 1024], s_[:], s_, reads=[s_])
                            else:
                                S.dma("sp", KT_d[l, h, :, c0 + half * 1024:c0 + (half + 1) * 1024], s_[:], s_, reads=[s_])
                    if sec == 0:
                        wload(2)
                chk("qk")
                wload(3)
                with contextlib.ExitStack() as sA:
                    vst = [sb(sA, "vst%d" % i, [128, 512], BF16) for i in range(4)]
                    vi = 0
                    for vg in range(2):
                        W = wload(4 + vg)
                        wload(5 + vg)
                        for tt in range(16):
                            pb = PSB[vi % 8]
                            for c in range(16):
                                S.op("pe", lambda e, c=c, tt=tt, pb=pb, W=W: e.matmul(
                                    pb[:], H[:, c, tt * 128:(tt + 1) * 128], W[:, c, :], start=(c == 0), stop=(c == 15)),
                                    reads=[W, H], writes=[pb], inc=(c == 15))
                            vs = vst[vi % 4]
                            if vi % 2 == 0:
                                S.op("act", lambda e, vs=vs, pb=pb: e.copy(vs[:], pb[:]), reads=[pb], writes=[vs])
                            else:
                                S.op("dve", lambda e, vs=vs, pb=pb: e.tensor_copy(vs[:], pb[:]), reads=[pb], writes=[vs])
                            S.dma("sp", V_d[l, vg * 4:(vg + 1) * 4, c0 + tt * 128:c0 + (tt + 1) * 128, :].rearrange("h t e -> t h e"),
                                  vs[:].rearrange("t (h e) -> t h e", h=4), vs, reads=[vs])
                            vi += 1
                    S.barrier()
                    S.retire(vst)
                chk("v")
                sil = [sb(p1, "sil%d" % i, [128, 1024], F32) for i in range(2)]
                sil_i = [0]
                with contextlib.ExitStack() as sB:
                    WB = [sb(sB, "WBb%d" % i, [128, 16 + CK], F32) for i in range(4)]
                    Dg = sb(sB, "Dg", [128, 2, CK], BF16)
                    PWt = sb(sB, "PWt", [128, 4, 2, 256], BF16)
                    S.dma("pool", PWt[:], pool_w[l].rearrange("g (t p) j -> p g t j", p=128), PWt, writes=[PWt])
                    for grp in range(4):
                        w = 2 << grp
                        for jt in range(2):
                            mt = grp * 2 + jt
                            g, mloc = 6 + mt // 4, mt % 4
                            X = WB[0]
                            S.op("dve", lambda e, mt=mt, X=X: e.tensor_copy(X[:, 0:16], PH[:, mt, :]), reads=[PH], writes=[X])
                            for half in range(2):
                                pa, pb = inproj_half(g, mloc, half)
                                for n, p_ in enumerate((pa, pb)):
                                    S.op("act", lambda e, n=n, p_=p_, half=half, X=X: e.copy(
                                        X[:, 16 + half * 1024 + n * 512:16 + half * 1024 + (n + 1) * 512], p_[:]),
                                        reads=[p_], writes=[X])
                            S.op("dve", lambda e, mt=mt, X=X: e.tensor_copy(PH[:, mt, :], X[:, CK:CK + 16]), reads=[X], writes=[PH])
                            cur = X
                            sh = 1
                            bi_ = 1
                            while sh < w:
                                nxt = WB[bi_]
                                bi_ = 3 - bi_
                                S.op("dve", lambda e, cur=cur, nxt=nxt, sh=sh: e.tensor_tensor(
                                    out=nxt[:, sh:16 + CK], in0=cur[:, sh:16 + CK], in1=cur[:, 0:16 + CK - sh], op=ALU.add),
                                    reads=[cur], writes=[nxt])
                                cur = nxt
                                sh *= 2
                            S.op("dve", lambda e, cur=cur, jt=jt, w=w, X=X: e.scalar_tensor_tensor(
                                out=Dg[:, jt, :], in0=cur[:, 16:16 + CK], scalar=1.0 / w, in1=X[:, 16:16 + CK],
                                op0=ALU.mult, op1=ALU.subtract), reads=[cur, X], writes=[Dg])
                            if ck == 0:
                                tmp = WB[3]
                                S.op("dve", lambda e, cur=cur, grp=grp, tmp=tmp: e.tensor_tensor(
                                    out=tmp[:, 0:16], in0=cur[:, 16:32], in1=cst[:, O_IC + grp * 16:O_IC + grp * 16 + 16], op=ALU.mult),
                                    reads=[cur, cst], writes=[tmp])
                                S.op("dve", lambda e, jt=jt, tmp=tmp, X=X: e.tensor_tensor(
                                    out=Dg[:, jt, 0:16], in0=tmp[:, 0:16], in1=X[:, 16:32], op=ALU.subtract),
                                    reads=[tmp, X], writes=[Dg])
                        for jt in range(2):
                            mt = grp * 2 + jt
                            gg, gm = 13 + mt // 4, mt % 4
                            for half in range(2):
                                ya, yb = next_pair()
                                for n, p_ in enumerate((ya, yb)):
                                    for it in range(2):
                                        S.op("pe", lambda e, n=n, p_=p_, it=it, grp=grp, jt=jt, half=half: e.matmul(
                                            p_[:], PWt[:, grp, it, jt * 128:(jt + 1) * 128],
                                            Dg[:, it, half * 1024 + n * 512:half * 1024 + (n + 1) * 512],
                                            start=(it == 0), stop=(it == 1)), reads=[PWt, Dg], writes=[p_], inc=(it == 1))
                                pa, pb = inproj_half(gg, gm, half)
                                sl = sil[sil_i[0] % 2]
                                sil_i[0] += 1
                                evac_pair("act", pa, pb, sl, lambda e, o, i: e.activation(o, i, AF.Silu))
                                s_ = next_stg()
                                for n, p_ in enumerate((ya, yb)):
                                    S.op("dve", lambda e, n=n, p_=p_, sl=sl, s_=s_, mt=mt: e.scalar_tensor_tensor(
                                        out=s_[:, n * 512:(n + 1) * 512], in0=p_[:], scalar=cst[:, LB + O_PS + mt:LB + O_PS + mt + 1],
                                        in1=sl[:, n * 512:(n + 1) * 512], op0=ALU.mult, op1=ALU.mult),
                                        reads=[p_, cst, sl], writes=[s_])
                                S.dma("sp", G_d[8 + mt, :, half * 1024:(half + 1) * 1024], s_[:], s_, reads=[s_])
                    S.barrier()
                    S.retire([PWt])
                chk("pool")
                with contextlib.ExitStack() as sC:
                    WB = [sb(sC, "WBc%d" % i, [128, 16 + CK], F32) for i in range(5)]
                    LWA = sb(sC, "LWA", [128, 8, 128], BF16)
                    LWX = sb(sC, "LWX", [128, 8, 128], BF16)
                    xcb = sb(sC, "xcb", [128, CK], BF16)
                    S.dma("pool", LWA[:], lru_wa[l].rearrange("g i j -> i g j"), LWA, writes=[LWA])
                    S.dma("pool", LWX[:], lru_wx[l].rearrange("g i j -> i g j"), LWX, writes=[LWX])
                    for blk in range(8):
                        g, mloc = 8 + blk // 4, blk % 4
                        XL, XC, R_, I_, A_ = WB
                        HH = XL
                        S.op("dve", lambda e, blk=blk: e.tensor_copy(XL[:, 12:16], LH[:, blk, :]), reads=[LH], writes=[XL])
                        for half in range(2):
                            pa, pb = inproj_half(g, mloc, half)
                            for n, p_ in enumerate((pa, pb)):
                                S.op("act", lambda e, n=n, p_=p_, half=half: e.copy(
                                    XL[:, 16 + half * 1024 + n * 512:16 + half * 1024 + (n + 1) * 512], p_[:]),
                                    reads=[p_], writes=[XL])
                        S.op("dve", lambda e, blk=blk: e.tensor_copy(LH[:, blk, :], XL[:, 12 + CK:16 + CK]), reads=[XL], writes=[LH])
                        cw = LB + O_CW
                        S.op("dve", lambda e, blk=blk: e.tensor_scalar(
                            out=XC[:, 0:CK], in0=XL[:, 16:16 + CK], scalar1=cst[:, cw + 24 + blk:cw + 25 + blk],
                            scalar2=cst[:, LB + O_CB + blk:LB + O_CB + blk + 1], op0=ALU.mult, op1=ALU.add),
                            reads=[XL, cst], writes=[XC])
                        for k in (2, 1, 0):
                            S.op("dve", lambda e, blk=blk, k=k: e.scalar_tensor_tensor(
                                out=XC[:, 0:CK], in0=XL[:, 13 + k:13 + k + CK], scalar=cst[:, cw + k * 8 + blk:cw + k * 8 + blk + 1],
                                in1=XC[:, 0:CK], op0=ALU.mult, op1=ALU.add), reads=[XL, cst, XC], writes=[XC])
                        S.op("act", lambda e: e.copy(xcb[:], XC[:, 0:CK]), reads=[XC], writes=[xcb])
                        for which, (LWm, dstb, bcol) in enumerate(((LWA, R_, O_BA), (LWX, I_, O_BX))):
                            for half in range(2):
                                pa, pb = next_pair()
                                for n, p_ in enumerate((pa, pb)):
                                    S.op("pe", lambda e, n=n, p_=p_, half=half, LWm=LWm, blk=blk: e.matmul(
                                        p_[:], LWm[:, blk, :], xcb[:, half * 1024 + n * 512:half * 1024 + (n + 1) * 512],
                                        start=True, stop=True), reads=[LWm, xcb], writes=[p_])
                                    S.op("act", lambda e, n=n, p_=p_, half=half, dstb=dstb, bcol=bcol, blk=blk: e.activation(
                                        dstb[:, half * 1024 + n * 512:half * 1024 + (n + 1) * 512], p_[:], AF.Sigmoid,
                                        bias=cst[:, LB + bcol + blk:LB + bcol + blk + 1]), reads=[p_, cst], writes=[dstb])
                        S.op("act", lambda e, blk=blk: e.activation(A_[:, 0:CK], R_[:, 0:CK], AF.Exp, scale=CL[:, blk:blk + 1]),
                             reads=[R_, CL], writes=[A_])
                        S.op("act", lambda e, blk=blk: e.activation(R_[:, 0:CK], R_[:, 0:CK], AF.Exp, scale=CL2[:, blk:blk + 1]),
                             reads=[R_, CL2], writes=[R_])
                        S.op("dve", lambda e: e.tensor_scalar(out=R_[:, 0:CK], in0=R_[:, 0:CK], scalar1=-1.0, scalar2=1.0,
                                                              op0=ALU.mult, op1=ALU.add), reads=[R_], writes=[R_])
                        S.op("act", lambda e: e.activation(R_[:, 0:CK], R_[:, 0:CK], AF.Sqrt), reads=[R_], writes=[R_])
                        if ck == 0:
                            S.op("dve", lambda e: e.memset(R_[:, 0:1], 1.0), reads=[R_], writes=[R_])
                        S.op("dve", lambda e: e.tensor_tensor(out=I_[:, 0:CK], in0=I_[:, 0:CK], in1=XC[:, 0:CK], op=ALU.mult),
                             reads=[I_, XC], writes=[I_])
                        S.op("dve", lambda e: e.tensor_tensor(out=I_[:, 0:CK], in0=I_[:, 0:CK], in1=R_[:, 0:CK], op=ALU.mult),
                             reads=[I_, R_], writes=[I_])
                        S.op("dve", lambda e, blk=blk: e.tensor_tensor_scan(
                            out=HH[:, 0:CK], data0=A_[:, 0:CK], data1=I_[:, 0:CK], initial=CAR[:, blk:blk + 1],
                            op0=ALU.mult, op1=ALU.add), reads=[A_, I_, CAR], writes=[HH])
                        S.op("dve", lambda e, blk=blk: e.tensor_copy(CAR[:, blk:blk + 1], HH[:, CK - 1:CK]), reads=[HH], writes=[CAR])
                        gg, gm = 15 + blk // 4, blk % 4
                        for half in range(2):
                            pa, pb = inproj_half(gg, gm, half)
                            sl = sil[sil_i[0] % 2]
                            sil_i[0] += 1
                            evac_pair("act", pa, pb, sl, lambda e, o, i: e.activation(o, i, AF.Silu))
                            s_ = next_stg()
                            S.op("dve", lambda e, half=half, sl=sl, s_=s_: e.tensor_tensor(
                                out=s_[:], in0=HH[:, half * 1024:(half + 1) * 1024], in1=sl[:], op=ALU.mult),
                                reads=[HH, sl], writes=[s_])
                            S.dma("sp", G_d[16 + blk, :, half * 1024:(half + 1) * 1024], s_[:], s_, reads=[s_])
                    S.barrier()
                    S.retire([LWA, LWX])
                chk("lru")
                with contextlib.ExitStack() as sD:
                    QM = sb(sD, "QM", [128, CK], BF16)
                    Et = [sb(sD, "Etm%d" % i, [128, 512], BF16) for i in range(3)]
                    YM = sb(sD, "YM", [128, CK], F32)
                    RS = sb(sD, "RS", [128, CK], F32)
                    et_i = 0
                    for hm in range(4):
                        for half in range(2):
                            pa, pb = inproj_half(10, hm, half)
                            for n, p_ in enumerate((pa, pb)):
                                S.op("act", lambda e, n=n, p_=p_, half=half: e.activation(
                                    QM[:, half * 1024 + n * 512:half * 1024 + (n + 1) * 512], p_[:], AF.Copy, scale=128 ** -0.5),
                                    reads=[p_], writes=[QM])
                        for qt in range(4):
                            po, psm = PSB[0 + 2 * (qt % 2)], PSB[1 + 2 * (qt % 2)]
                            for mb in range(2):
                                pss = PSB[4 + (et_i % 3)]
                                et = Et[et_i % 3]
                                et_i += 1
                                S.op("pe", lambda e, mb=mb, hm=hm, qt=qt, pss=pss: e.matmul(
                                    pss[:], MK[:, hm, mb * 128:(mb + 1) * 128], QM[:, qt * 512:(qt + 1) * 512], start=True, stop=True),
                                    reads=[MK, QM], writes=[pss])
                                S.op("act", lambda e, et=et, pss=pss: e.activation(et[:], pss[:], AF.Exp), reads=[pss], writes=[et])
                                S.op("pe", lambda e, mb=mb, hm=hm, et=et, po=po: e.matmul(
                                    po[:], MV[:, mb, hm * 128:(hm + 1) * 128], et[:], start=(mb == 0), stop=(mb == 1)),
                                    reads=[MV, et], writes=[po], inc=False)
                                S.op("pe", lambda e, mb=mb, et=et, psm=psm: e.matmul(
                                    psm[:], ones_b[:], et[:], start=(mb == 0), stop=(mb == 1)),
                                    reads=[ones_b, et], writes=[psm, po], inc=True)
                            S.op("dve", lambda e, qt=qt, psm=psm: e.reciprocal(RS[:, qt * 512:(qt + 1) * 512], psm[:]),
                                 reads=[psm], writes=[RS])
                            S.op("dve", lambda e, qt=qt, po=po: e.tensor_tensor(
                                out=YM[:, qt * 512:(qt + 1) * 512], in0=po[:], in1=RS[:, qt * 512:(qt + 1) * 512], op=ALU.mult),
                                reads=[po, RS], writes=[YM])
                        for half in range(2):
                            pa, pb = inproj_half(17, hm, half)
                            sl = sil[sil_i[0] % 2]
                            sil_i[0] += 1
                            evac_pair("act", pa, pb, sl, lambda e, o, i: e.activation(o, i, AF.Silu))
                            s_ = next_stg()
                            S.op("dve", lambda e, half=half, sl=sl, s_=s_: e.tensor_tensor(
                                out=s_[:], in0=YM[:, half * 1024:(half + 1) * 1024], in1=sl[:], op=ALU.mult),
                                reads=[YM, sl], writes=[s_])
                            S.dma("sp", G_d[24 + hm, :, half * 1024:(half + 1) * 1024], s_[:], s_, reads=[s_])
                    S.barrier()
                chk("mem")
                wload(11)
                wload(12)
                for h in range(8):
                    for half in range(2):
                        pa, pb = inproj_half(11 + h // 4, h % 4, half)
                        s_ = next_stg()
                        evac_pair("act", pa, pb, s_, lambda e, o, i: e.activation(o, i, AF.Silu))
                        S.dma("sp", SGA_d[h, :, half * 1024:(half + 1) * 1024], s_[:], s_, reads=[s_])
                S.barrier()
                S.retire([H] + Wt + stg)

            chk("p1")
            with contextlib.ExitStack() as p2:
                nkc = ck + 1
                KTb = [sb(p2, "KTb%d" % i, [128, nkc * CK], BF16) for i in range(2)]
                Vb = [sb(p2, "Vb%d" % i, [128, nkc * 16, 128], BF16) for i in range(2)]
                Qb = [sb(p2, "Qb%d" % i, [128, CK], BF16) for i in range(2)]
                SGb = [sb(p2, "SGb%d" % i, [128, CK], BF16) for i in range(2)]
                Et = [sb(p2, "Et%d" % i, [128, 512], BF16) for i in range(4)]
                OS = [sb(p2, "OS%d" % i, [128, 512], F32) for i in range(4)]
                ta = [sb(p2, "ta%d" % i, [128, 512], F32) for i in range(4)]
                gst = [sb(p2, "gst%d" % i, [128, 512], BF16) for i in range(2)]

                def load_head(h):
                    i = h % 2
                    S.dma("sp", Qb[i][:], QT_d[h], Qb[i], writes=[Qb[i]])
                    S.dma("sp", SGb[i][:], SGA_d[h], SGb[i], writes=[SGb[i]])
                    S.dma("sp", KTb[i][:], KT_d[l, h, :, 0:nkc * CK], KTb[i], writes=[KTb[i]])
                    S.dma("sp", Vb[i][:], V_d[l, h, 0:nkc * CK, :].rearrange("(kb p) e -> p kb e", p=128), Vb[i], writes=[Vb[i]])

                load_head(0)
                et_i = 0
                acc_i = 0
                g_i = 0
                for h in range(8):
                    if h + 1 < 8:
                        load_head(h + 1)
                    i = h % 2
                    Q, KT_, V_, SG = Qb[i], KTb[i], Vb[i], SGb[i]
                    for qt in range(4):
                        for c in range(2):
                            po, psm = PSB[(acc_i % 2) * 2], PSB[(acc_i % 2) * 2 + 1]
                            acc_i += 1
                            nkb = ck * 16 + qt * 4 + 4
                            blocks = []
                            for kb in range(nkb):
                                sub = kb - (ck * 16 + qt * 4)
                                n0 = sub * 128 if sub > 0 else 0
                                blocks.append((kb, n0, sub >= 0))
                            ps_list = {}

                            def emit_s(j):
                                kb, n0, dg = blocks[j]
                                pss = PSB[4 + (j % 3)]
                                ps_list[j] = pss
                                S.op("pe", lambda e, kb=kb, n0=n0, pss=pss: e.matmul(
                                    pss[:, n0:512], KT_[c * 64:(c + 1) * 64, kb * 128:(kb + 1) * 128],
                                    Q[c * 64:(c + 1) * 64, qt * 512 + n0:(qt + 1) * 512], start=True, stop=True),
                                    reads=[KT_, Q], writes=[pss])

                            emit_s(0)
                            for j in range(len(blocks)):
                                if j + 1 < len(blocks):
                                    emit_s(j + 1)
                                kb, n0, dg = blocks[j]
                                pss = ps_list[j]
                                et = Et[et_i % 4]
                                et_i += 1
                                S.op("act", lambda e, et=et, pss=pss, n0=n0: e.activation(et[:, n0:512], pss[:, n0:512], AF.Exp),
                                     reads=[pss], writes=[et])
                                if dg:
                                    S.op("dve", lambda e, et=et, n0=n0: e.tensor_tensor(
                                        out=et[:, n0:n0 + 128], in0=et[:, n0:n0 + 128], in1=tri[:], op=ALU.mult),
                                        reads=[et, tri], writes=[et])
                                last = (j == len(blocks) - 1)
                                S.op("pe", lambda e, kb=kb, n0=n0, et=et, po=po, j=j, last=last: e.matmul(
                                    po[:, n0:512], V_[:, kb, :], et[:, n0:512], start=(j == 0), stop=last),
                                    reads=[V_, et], writes=[po], inc=False)
                                S.op("pe", lambda e, n0=n0, et=et, psm=psm, j=j, last=last: e.matmul(
                                    psm[:, n0:512], ones_b[:], et[:, n0:512], start=(j == 0), stop=last),
                                    reads=[ones_b, et], writes=[psm, po], inc=True)
                            S.op("act", lambda e, c=c, po=po: e.copy(OS[2 * c][:], po[:]), reads=[po], writes=[OS[2 * c]])
                            S.op("dve", lambda e, c=c, psm=psm: e.reciprocal(OS[2 * c + 1][:], psm[:]), reads=[psm],
                                 writes=[OS[2 * c + 1]])
                        t0, t1, t2, t3 = ta
                        S.op("dve", lambda e: e.tensor_tensor(out=t0[:], in0=OS[0][:], in1=OS[1][:], op=ALU.mult),
                             reads=[OS[0], OS[1]], writes=[t0])
                        S.op("dve", lambda e: e.tensor_tensor(out=t1[:], in0=OS[2][:], in1=OS[3][:], op=ALU.mult),
                             reads=[OS[2], OS[3]], writes=[t1])
                        S.op("dve", lambda e: e.scalar_tensor_tensor(out=t2[:], in0=t1[:], scalar=NEGLAM[:, 0:1], in1=t0[:],
                                                                     op0=ALU.mult, op1=ALU.add),
                             reads=[t1, t0, NEGLAM], writes=[t2])
                        S.op("act", lambda e: e.activation(t3[:], t2[:], AF.Square), reads=[t2], writes=[t3])
                        pq = PSB[7]
                        S.op("pe", lambda e: e.matmul(pq[:], ones_f[:], t3[:], start=True, stop=True),
                             reads=[ones_f, t3], writes=[pq])
                        S.op("act", lambda e: e.activation(t0[:], pq[:], AF.Sqrt, bias=epsb[:, 0:1], scale=1.0 / 128),
                             reads=[pq, epsb], writes=[t0])
                        S.op("dve", lambda e: e.reciprocal(t1[:], t0[:]), reads=[t0], writes=[t1])
                        S.op("dve", lambda e: e.tensor_tensor(out=t2[:], in0=t2[:], in1=t1[:], op=ALU.mult),
                             reads=[t2, t1], writes=[t2])
                        gs = gst[g_i % 2]
                        g_i += 1
                        S.op("dve", lambda e, gs=gs, qt=qt: e.scalar_tensor_tensor(
                            out=gs[:], in0=t2[:], scalar=SGL[:, 0:1], in1=SG[:, qt * 512:(qt + 1) * 512],
                            op0=ALU.mult, op1=ALU.mult), reads=[t2, SGL, SG], writes=[gs])
                        S.dma("sp", G_d[h, :, qt * 512:(qt + 1) * 512], gs[:], gs, reads=[gs])
                S.barrier()
                S.retire(KTb + Vb + Qb + SGb + gst)

            chk("p2")
            with contextlib.ExitStack() as p3:
                Hh = sb(p3, "Hh", [128, 16, 1024], BF16)
                Gh = sb(p3, "Gh", [128, 28, 1024], BF16)
                NB3 = 3
                Wb = [sb(p3, "Wb%d" % i, [128, 8, 128], BF16) for i in range(NB3)]
                Wm = [sb(p3, "Wm%d" % i, [128, 16, 128], BF16) for i in range(NB3)]
                sg = [sb(p3, "sg%d" % i, [128, 1024], F32) for i in range(2)]
                tm = [sb(p3, "tm%d" % i, [128, 1024], F32) for i in range(2)]
                acc = sb(p3, "acc", [128, 1024], F32)
                mst = [sb(p3, "mst%d" % i, [128, 1024], BF16) for i in range(2)]
                KOFF = (0, 8, 16, 24, 28)
                steps = [(n_, bi) for n_ in range(16) for bi in range(4)]

                def load_w3(si):
                    n_, bi = steps[si]
                    i = si % NB3
                    nk = KOFF[bi + 1] - KOFF[bi]
                    S.dma("pool", Wb[i][:, 0:nk, :],
                          w_branch[l][KOFF[bi] * 128:KOFF[bi + 1] * 128, :].rearrange("(k p) n -> p k n", p=128)[:, :, n_ * 128:(n_ + 1) * 128],
                          Wb[i], writes=[Wb[i]])
                    cb_ = 9216 + bi * 2048 + n_ * 128
                    S.dma("pool", Wm[i][:], w_in[l].rearrange("(c p) n -> p c n", p=128)[:, :, cb_:cb_ + 128],
                          Wm[i], writes=[Wm[i]])

                it3 = 0
                for half in range(2):
                    S.dma("sp", Hh[:], Hs_d.rearrange("c p n -> p c n")[:, :, half * 1024:(half + 1) * 1024], Hh, writes=[Hh])
                    S.dma("sp", Gh[:], G_d.rearrange("c p n -> p c n")[:, :, half * 1024:(half + 1) * 1024], Gh, writes=[Gh])
                    load_w3(0)
                    load_w3(1)
                    for si, (n_, bi) in enumerate(steps):
                        if si + 2 < len(steps):
                            load_w3(si + 2)
                        i = si % NB3
                        if True:
                            za, zb = PSB[(it3 % 2) * 2], PSB[(it3 % 2) * 2 + 1]
                            ma, mb_ = PSB[4 + (it3 % 2) * 2], PSB[5 + (it3 % 2) * 2]
                            it3 += 1
                            ks = list(range(KOFF[bi], KOFF[bi + 1]))
                            for n, p_ in enumerate((ma, mb_)):
                                for c in range(16):
                                    S.op("pe", lambda e, c=c, n=n, p_=p_, i=i: e.matmul(
                                        p_[:], Wm[i][:, c, :], Hh[:, c, n * 512:(n + 1) * 512], start=(c == 0), stop=(c == 15)),
                                        reads=[Wm[i], Hh], writes=[p_], inc=(c == 15))
                            for n, p_ in enumerate((za, zb)):
                                for kk, k in enumerate(ks):
                                    S.op("pe", lambda e, k=k, kk=kk, n=n, p_=p_, i=i, ks=ks: e.matmul(
                                        p_[:], Wb[i][:, kk, :], Gh[:, k, n * 512:(n + 1) * 512], start=(kk == 0),
                                        stop=(kk == len(ks) - 1)), reads=[Wb[i], Gh], writes=[p_], inc=(kk == len(ks) - 1))
                            sgt = sg[bi % 2]
                            for n, p_ in enumerate((ma, mb_)):
                                S.op("act", lambda e, n=n, p_=p_, sgt=sgt: e.activation(sgt[:, n * 512:(n + 1) * 512], p_[:], AF.Sigmoid),
                                     reads=[p_], writes=[sgt])
                            dst = acc if bi == 0 else tm[bi % 2]
                            for n, p_ in enumerate((za, zb)):
                                S.op("dve", lambda e, n=n, p_=p_, sgt=sgt, dst=dst: e.tensor_tensor(
                                    out=dst[:, n * 512:(n + 1) * 512], in0=p_[:], in1=sgt[:, n * 512:(n + 1) * 512], op=ALU.mult),
                                    reads=[p_, sgt], writes=[dst])
                            if bi > 0:
                                if bi < 3:
                                    S.op("dve", lambda e, dst=dst: e.tensor_tensor(out=acc[:], in0=acc[:], in1=dst[:], op=ALU.add),
                                         reads=[acc, dst], writes=[acc])
                                else:
                                    ms = mst[n_ % 2]
                                    S.op("dve", lambda e, dst=dst, ms=ms: e.tensor_tensor(out=ms[:], in0=acc[:], in1=dst[:], op=ALU.add),
                                         reads=[acc, dst], writes=[ms])
                                    S.dma("sp", MG_d[n_, :, half * 1024:(half + 1) * 1024], ms[:], ms, reads=[ms])
                S.barrier()
                S.retire([Hh, Gh] + Wb + Wm + mst)

            chk("p3a")
            with contextlib.ExitStack() as p4:
                MGh = sb(p4, "MGh", [128, 16, 1024], BF16)
                Wo = [sb(p4, "Wo%d" % i, [128, 16, 128], BF16) for i in range(2)]
                xin = [sb(p4, "xin%d" % i, [128, 1024], F32) for i in range(2)]
                xo = [sb(p4, "xo%d" % i, [128, 1024], F32) for i in range(2)]
                it4 = 0
                for half in range(2):
                    S.dma("sp", MGh[:], MG_d.rearrange("c p n -> p c n")[:, :, half * 1024:(half + 1) * 1024], MGh, writes=[MGh])
                    cols = slice(c0 + half * 1024, c0 + (half + 1) * 1024)
                    for n_ in range(16):
                        i = it4 % 2
                        S.dma("pool", Wo[i][:], w_out[l].rearrange("(k p) n -> p k n", p=128)[:, :, n_ * 128:(n_ + 1) * 128],
                              Wo[i], writes=[Wo[i]])
                        S.dma("sp", xin[i][:], xsrc[n_ * 128:(n_ + 1) * 128, cols], xin[i], writes=[xin[i]])
                        ya, yb = PSB[(it4 % 4) * 2], PSB[(it4 % 4) * 2 + 1]
                        it4 += 1
                        for n, p_ in enumerate((ya, yb)):
                            for k in range(16):
                                S.op("pe", lambda e, k=k, n=n, p_=p_, i=i: e.matmul(
                                    p_[:], Wo[i][:, k, :], MGh[:, k, n * 512:(n + 1) * 512], start=(k == 0), stop=(k == 15)),
                                    reads=[Wo[i], MGh], writes=[p_], inc=(k == 15))
                            S.op("dve", lambda e, n=n, p_=p_, i=i: e.tensor_tensor(
                                out=xo[i][:, n * 512:(n + 1) * 512], in0=p_[:], in1=xin[i][:, n * 512:(n + 1) * 512], op=ALU.add),
                                reads=[p_, xin[i]], writes=[xo[i]])
                        S.dma("sp", xdst[n_ * 128:(n_ + 1) * 128, cols], xo[i][:], xo[i], reads=[xo[i]])
                S.barrier()
                S.retire([MGh] + Wo + xin + xo)

    except _Stop:
        pass
    with contextlib.ExitStack() as pf:
      if not stop_at:
          xts = [sb(pf, "fx%d" % i, [128, 16, 512], F32) for i in range(2)]
          ots = [sb(pf, "fo%d" % i, [128, 16, 512], F32) for i in range(2)]
          sqs = [sb(pf, "fsq%d" % i, [128, 512], F32) for i in range(3)]
          r1 = sb(pf, "fr1", [128, 512], F32)
          r2 = sb(pf, "fr2", [128, 512], F32)
          for tt in range(S_ // 512):
              xt = xts[tt % 2]
              ot = ots[tt % 2]
              S.dma("sp", xt[:], x2T.rearrange("(c p) n -> p c n", p=128)[:, :, tt * 512:(tt + 1) * 512], xt, writes=[xt])
              t1, t2 = r1, r2
              for c in range(16):
                  sq = sqs[c % 3]
                  S.op("act", lambda e, c=c, sq=sq, xt=xt: e.activation(sq[:], xt[:, c, :], AF.Square), reads=[xt], writes=[sq])
                  S.op("pe", lambda e, c=c, sq=sq: e.matmul(PSB[7][:], ones_f[:], sq[:], start=(c == 0), stop=(c == 15)),
                       reads=[ones_f, sq], writes=[PSB[7]])
              S.op("act", lambda e: e.activation(t1[:], PSB[7][:], AF.Sqrt, bias=epsb[:, 0:1], scale=1.0 / D),
                   reads=[PSB[7], epsb], writes=[t1])
              S.op("dve", lambda e: e.reciprocal(t2[:], t1[:]), reads=[t1], writes=[t2])
              for c in range(16):
                  S.op("dve", lambda e, c=c, xt=xt, ot=ot: e.scalar_tensor_tensor(
                      out=ot[:, c, :], in0=xt[:, c, :], scalar=cst[:, O_FG + c:O_FG + c + 1], in1=t2[:],
                      op0=ALU.mult, op1=ALU.mult), reads=[xt, cst, t2], writes=[ot])
              S.dma("sp", outT.rearrange("(c p) n -> p c n", p=128)[:, :, tt * 512:(tt + 1) * 512], ot[:], ot, reads=[ot])
          S.barrier()
    if not stop_at:
        es.close()
    return nc


def host_inputs(nchunk, b, x, mem, norm_g, w_in, lam_vecs, subln_g, pool_w, pool_scale, conv_w, conv_b,
                lru_wa, lru_ba, lru_wx, lru_bx, lru_lambda, mem_norm_g, w_mem_kv, w_branch, w_out, final_g):
    S_ = nchunk * CK
    f = np.float32

    def pc(v, n):
        return np.ascontiguousarray(np.asarray(v, f).reshape(n, 128).T)

    cst = np.zeros((128, NCST), f)
    for l in range(2):
        LB = l * LW
        cst[:, LB + O_NG:LB + O_NG + 16] = pc(norm_g[l], 16)
        cst[:, LB + O_MG:LB + O_MG + 16] = pc(mem_norm_g[l], 16)
        cst[:, LB + O_PS:LB + O_PS + 8] = pc(pool_scale[l], 8)
        for k in range(4):
            cst[:, LB + O_CW + k * 8:LB + O_CW + k * 8 + 8] = pc(conv_w[l][k], 8)
        cst[:, LB + O_CB:LB + O_CB + 8] = pc(conv_b[l], 8)
        cst[:, LB + O_BA:LB + O_BA + 8] = pc(lru_ba[l], 8)
        cst[:, LB + O_BX:LB + O_BX + 8] = pc(lru_bx[l], 8)
        cst[:, LB + O_LL:LB + O_LL + 8] = pc(lru_lambda[l], 8)
        cst[:, LB + O_SG] = np.asarray(subln_g[l], f)
        cst[:, LB + O_LV:LB + O_LV + 256] = np.asarray(lam_vecs[l], f).reshape(1, 256)
    cst[:, O_FG:O_FG + 16] = pc(final_g, 16)
    for g in range(4):
        w = 2 << g
        for t in range(16):
            cst[:, O_IC + g * 16 + t] = 1.0 / min(t + 1, w)
    tri = np.triu(np.ones((128, 128), f))
    return {
        "xT": np.ascontiguousarray(np.asarray(x[b, :S_], f).T),
        "memT": np.ascontiguousarray(np.asarray(mem[b], f).T),
        "cst": cst, "tri": tri,
        "w_in": np.asarray(w_in, f), "pool_w": np.asarray(pool_w, f), "lru_wa": np.asarray(lru_wa, f),
        "lru_wx": np.asarray(lru_wx, f), "w_mem_kv": np.asarray(w_mem_kv, f), "w_branch": np.asarray(w_branch, f),
        "w_out": np.asarray(w_out, f),
    }


def run(nchunk, inputs, n_batch=2):
    nc = build(nchunk)
    maps = []
    per_b = 8 // n_batch
    for c in range(8):
        maps.append(host_inputs(nchunk, min(c // per_b, n_batch - 1), **inputs))
    res = run_bass_kernel_spmd(nc, maps, core_ids=list(range(8)))
    S_ = nchunk * CK
    out = np.zeros((n_batch, S_, D), np.float32)
    q = S_ // per_b
    for c in range(8):
        b, j = c // per_b, c % per_b
        out[b, j * q:(j + 1) * q, :] = res.results[c]["outT"][:, j * q:(j + 1) * q].T
    return out


def kernel(**inputs):
    return run(4, inputs)
```

```python
import contextlib
import math
import numpy as np
import concourse.bass as bass
import concourse.mybir as mybir
from concourse.bass_utils import run_bass_kernel_spmd

F32 = mybir.dt.float32
BF16 = mybir.dt.bfloat16
AF = mybir.ActivationFunctionType
ALU = mybir.AluOpType
AX = mybir.AxisListType

D = 2048
NIN = 17408
CK = 2048
MEM = 256
EPS = 1e-6
ENGS = ("pe", "act", "dve", "pool", "sp")

LW = 16 + 16 + 8 + 32 + 8 + 8 + 8 + 8 + 1 + 256
O_NG, O_MG, O_PS, O_CW, O_CB, O_BA, O_BX, O_LL, O_SG, O_LV = 0, 16, 32, 40, 72, 80, 88, 96, 104, 105
O_FG = 2 * LW
O_IC = O_FG + 16
NCST = O_IC + 64


class _Stop(Exception):
    pass


class T:
    def __init__(self, t, name=""):
        self.t = t
        self.name = name
        self.w = {}
        self.r = {}
        self.chan = None

    def __getitem__(self, k):
        return self.t[k]


class TP(T):
    def __init__(self, t, off, name=""):
        T.__init__(self, t, name)
        self.off = off

    def __getitem__(self, k):
        if isinstance(k, slice):
            assert k == slice(None)
            return self.t[:, self.off:self.off + 512]
        ps, fs = k
        a = 0 if fs.start is None else fs.start
        b = 512 if fs.stop is None else fs.stop
        return self.t[ps, self.off + a:self.off + b]


class Chan:
    def __init__(self, sem):
        self.sem = sem
        self.count = 0


def _merge(d, s):
    for k, v in s.items():
        if d.get(k, 0) < v:
            d[k] = v


class Sched:
    def __init__(self, nc):
        self.nc = nc
        self.e = dict(pe=nc.tensor, act=nc.scalar, dve=nc.vector, pool=nc.gpsimd, sp=nc.sync)
        self.semobj = {}
        self.cnt = {}
        self.seen = {k: {} for k in ENGS}
        for k in ENGS:
            self.semobj[("c", k)] = nc.alloc_semaphore("cs_" + k)
            self.cnt[k] = 0
        self.chans = []
        self.free_ch = []
        self.pending = {k: False for k in ENGS}

    def _wait(self, eng, deps):
        for key, val in deps.items():
            if eng == "pe" and key == ("c", "pe"):
                continue
            if self.seen[eng].get(key, 0) >= val:
                continue
            self.e[eng].wait_ge(self.semobj[key], val)
            self.seen[eng][key] = val

    def op(self, eng, fn, reads=(), writes=(), inc=True):
        deps = {}
        for b in reads:
            _merge(deps, b.w)
        for b in writes:
            _merge(deps, b.w)
            _merge(deps, b.r)
        self._wait(eng, deps)
        ins = fn(self.e[eng])
        key = ("c", eng)
        val = self.cnt[eng] + 1
        if inc:
            ins.then_inc(self.semobj[key], 1)
            self.cnt[eng] = val
            self.pending[eng] = False
        else:
            self.pending[eng] = True
        for b in reads:
            if b.r.get(key, 0) < val:
                b.r[key] = val
        for b in writes:
            if b.w.get(key, 0) < val:
                b.w[key] = val
            b.r = {}
        return ins

    def get_chan(self, sb):
        if sb.chan is None:
            if self.free_ch:
                sb.chan = self.free_ch.pop()
            else:
                ch = Chan(self.nc.alloc_semaphore("ch%d" % len(self.chans)))
                self.chans.append(ch)
                self.semobj[("d", id(ch))] = ch.sem
                sb.chan = ch
        return sb.chan

    def dma(self, q, out_ap, in_ap, sb, reads=(), writes=()):
        deps = {}
        for b in reads:
            _merge(deps, b.w)
        for b in writes:
            _merge(deps, b.w)
            _merge(deps, b.r)
        self._wait(q, deps)
        ch = self.get_chan(sb)
        ins = self.e[q].dma_start(out=out_ap, in_=in_ap)
        ch.count += 16
        ins.then_inc(ch.sem, 16)
        key = ("d", id(ch))
        val = ch.count
        for b in reads:
            if b.r.get(key, 0) < val:
                b.r[key] = val
        for b in writes:
            if b.w.get(key, 0) < val:
                b.w[key] = val
            b.r = {}
        return ins

    def barrier(self, engs=ENGS):
        assert not any(self.pending.values()), self.pending
        marks = {}
        for k in ENGS:
            if self.cnt[k] > 0:
                marks[("c", k)] = self.cnt[k]
        for ch in self.chans:
            if ch.count > 0:
                marks[("d", id(ch))] = ch.count
        for eng in engs:
            deps = dict(marks)
            self._wait(eng, deps)

    def retire(self, bufs):
        for b in bufs:
            if b.chan is not None:
                self.free_ch.append(b.chan)
                b.chan = None


def build(nchunk, lam_inits=(0.2, 0.8 - 0.6 * math.exp(-0.3)), dbg=None):
    S_ = nchunk * CK
    nc = bass.Bass("TRN2", target_bir_lowering=False)
    xT = nc.dram_tensor("xT", [D, S_], F32, kind="ExternalInput").ap()
    memT = nc.dram_tensor("memT", [D, MEM], F32, kind="ExternalInput").ap()
    cst_d = nc.dram_tensor("cst", [128, NCST], F32, kind="ExternalInput").ap()
    tri_d = nc.dram_tensor("tri", [128, 128], F32, kind="ExternalInput").ap()
    w_in = nc.dram_tensor("w_in", [2, D, NIN], F32, kind="ExternalInput").ap()
    pool_w = nc.dram_tensor("pool_w", [2, 4, 256, 256], F32, kind="ExternalInput").ap()
    lru_wa = nc.dram_tensor("lru_wa", [2, 8, 128, 128], F32, kind="ExternalInput").ap()
    lru_wx = nc.dram_tensor("lru_wx", [2, 8, 128, 128], F32, kind="ExternalInput").ap()
    w_mem_kv = nc.dram_tensor("w_mem_kv", [2, D, 1024], F32, kind="ExternalInput").ap()
    w_branch = nc.dram_tensor("w_branch", [2, 3584, D], F32, kind="ExternalInput").ap()
    w_out = nc.dram_tensor("w_out", [2, D, D], F32, kind="ExternalInput").ap()
    outT = nc.dram_tensor("outT", [D, S_], F32, kind="ExternalOutput").ap()
    x1T = nc.dram_tensor("x1T", [D, S_], F32, kind="Internal").ap()
    x2T = nc.dram_tensor("x2T", [D, S_], F32, kind="Internal").ap()
    KT_d = nc.dram_tensor("KT_d", [2, 8, 128, S_], BF16, kind="Internal").ap()
    V_d = nc.dram_tensor("V_d", [2, 8, S_, 128], BF16, kind="Internal").ap()
    QT_d = nc.dram_tensor("QT_d", [8, 128, CK], BF16, kind="Internal").ap()
    SGA_d = nc.dram_tensor("SGA_d", [8, 128, CK], BF16, kind="Internal").ap()
    G_d = nc.dram_tensor("G_d", [28, 128, CK], BF16, kind="Internal").ap()
    Hs_d = nc.dram_tensor("Hs_d", [16, 128, CK], BF16, kind="Internal").ap()
    MG_d = nc.dram_tensor("MG_d", [16, 128, CK], BF16, kind="Internal").ap()

    S = Sched(nc)
    es = contextlib.ExitStack()
    import os
    stop_at = os.environ.get("KSTOP", "")

    def chk(name):
        if name == stop_at:
            S.barrier()
            raise _Stop()

    uid = [0]

    def sb(stack, name, shape, dt):
        uid[0] += 1
        name = "%s_u%d" % (name, uid[0])
        return T(stack.enter_context(nc.sbuf_tensor(name, shape, dt)), name)

    cst = sb(es, "cstb", [128, NCST], F32)
    ones_f = sb(es, "ones_f", [128, 128], F32)
    ones_b = sb(es, "ones_b", [128, 128], BF16)
    tri = sb(es, "trib", [128, 128], BF16)
    epsb = sb(es, "epsb", [128, 1], F32)
    PH = sb(es, "PH", [128, 8, 16], F32)
    LH = sb(es, "LH", [128, 8, 4], F32)
    CAR = sb(es, "CAR", [128, 8], F32)
    CL = sb(es, "CL", [128, 8], F32)
    CL2 = sb(es, "CL2", [128, 8], F32)
    NEGLAM = sb(es, "NEGLAM", [128, 1], F32)
    SGL = sb(es, "SGL", [128, 1], F32)
    MK = sb(es, "MK", [128, 4, MEM], BF16)
    MV = sb(es, "MV", [128, 2, 512], BF16)
    smallt = sb(es, "smallt", [128, 8, 64], F32)
    PSP = [es.enter_context(nc.psum_tensor("psp%d" % i, [128, 1024], F32)) for i in range(4)]
    PSB = [TP(PSP[i // 2], (i % 2) * 512, "ps%d" % i) for i in range(8)]

    S.dma("sp", cst[:], cst_d, cst, writes=[cst])
    S.dma("pool", tri[:], tri_d, tri, writes=[tri])
    S.op("dve", lambda e: e.memset(ones_f[:], 1.0), writes=[ones_f])
    S.op("dve", lambda e: e.memset(ones_b[:], 1.0), writes=[ones_b])
    S.op("dve", lambda e: e.memset(epsb[:], EPS), writes=[epsb])

    def rms_tile(stack_bufs, xt, ncols, gcol, out_fn, sqs, rstd_bufs, psb):
        t1, t2 = rstd_bufs
        for c in range(16):
            sq = sqs[c % len(sqs)]
            S.op("act", lambda e, c=c, sq=sq: e.activation(sq[:, 0:ncols], xt[:, c, 0:ncols], AF.Square),
                 reads=[xt], writes=[sq])
            S.op("pe", lambda e, c=c, sq=sq: e.matmul(psb[:, 0:ncols], ones_f[:], sq[:, 0:ncols],
                                                     start=(c == 0), stop=(c == 15)),
                 reads=[ones_f, sq], writes=[psb], inc=True)
        S.op("act", lambda e: e.activation(t1[:, 0:ncols], psb[:, 0:ncols], AF.Sqrt, bias=epsb[:, 0:1], scale=1.0 / D),
             reads=[psb, epsb], writes=[t1])
        S.op("dve", lambda e: e.reciprocal(t2[:, 0:ncols], t1[:, 0:ncols]), reads=[t1], writes=[t2])
        for c in range(16):
            oap, ot = out_fn(c)
            S.op("dve", lambda e, c=c, oap=oap: e.scalar_tensor_tensor(
                out=oap, in0=xt[:, c, 0:ncols], scalar=cst[:, gcol + c:gcol + c + 1], in1=t2[:, 0:ncols],
                op0=ALU.mult, op1=ALU.mult), reads=[xt, cst, t2], writes=[ot])
        return t2

    try:
      for l in range(2):
        LB = l * LW
        lam_init = lam_inits[l]
        xsrc = xT if l == 0 else x1T
        xdst = x1T if l == 0 else x2T
        with contextlib.ExitStack() as ps_:
            S.barrier()
            HM = sb(ps_, "HM", [128, 16, MEM], BF16)
            mt_ = sb(ps_, "mt_", [128, 16, MEM], F32)
            WM = sb(ps_, "WM", [128, 16, 1024], BF16)
            sqs = [sb(ps_, "sqm%d" % i, [128, 512], F32) for i in range(3)]
            r1 = sb(ps_, "r1m", [128, 512], F32)
            r2 = sb(ps_, "r2m", [128, 512], F32)
            S.dma("sp", mt_[:], memT.rearrange("(c p) n -> p c n", p=128), mt_, writes=[mt_])
            S.dma("pool", WM[:], w_mem_kv[l].rearrange("(c p) n -> p c n", p=128), WM, writes=[WM])
            rms_tile(None, mt_, MEM, LB + O_MG, lambda c: (HM[:, c, :], HM), sqs, (r1, r2), PSB[7])
            for hm in range(4):
                pb = PSB[hm % 4]
                for c in range(16):
                    S.op("pe", lambda e, c=c, hm=hm, pb=pb: e.matmul(pb[:, 0:MEM], WM[:, c, hm * 128:(hm + 1) * 128],
                                                                    HM[:, c, :], start=(c == 0), stop=(c == 15)),
                         reads=[WM, HM], writes=[pb], inc=(c == 15))
                S.op("act", lambda e, hm=hm, pb=pb: e.copy(MK[:, hm, :], pb[:, 0:MEM]), reads=[pb], writes=[MK])
            for mb in range(2):
                pb = PSB[4 + mb]
                for c in range(16):
                    S.op("pe", lambda e, c=c, mb=mb, pb=pb: e.matmul(pb[:], HM[:, c, mb * 128:(mb + 1) * 128],
                                                                    WM[:, c, 512:1024], start=(c == 0), stop=(c == 15)),
                         reads=[WM, HM], writes=[pb], inc=(c == 15))
                S.op("act", lambda e, mb=mb, pb=pb: e.copy(MV[:, mb, :], pb[:]), reads=[pb], writes=[MV])
            st = smallt
            S.op("act", lambda e: e.activation(st[:, 0, 0:8], cst[:, LB + O_LL:LB + O_LL + 8], AF.Exp, scale=-1.0),
                 reads=[cst], writes=[st])
            S.op("dve", lambda e: e.tensor_scalar_add(st[:, 0, 0:8], st[:, 0, 0:8], 1.0), reads=[st], writes=[st])
            S.op("act", lambda e: e.activation(st[:, 1, 0:8], st[:, 0, 0:8], AF.Ln), reads=[st], writes=[st])
            S.op("dve", lambda e: e.tensor_scalar_mul(CL[:], st[:, 1, 0:8], -8.0), reads=[st], writes=[CL])
            S.op("dve", lambda e: e.tensor_scalar_mul(CL2[:], st[:, 1, 0:8], -16.0), reads=[st], writes=[CL2])
            lv = LB + O_LV
            S.op("dve", lambda e: e.tensor_tensor(out=st[:, 2, 0:64], in0=cst[:, lv:lv + 64], in1=cst[:, lv + 64:lv + 128],
                                                  op=ALU.mult), reads=[cst, st], writes=[st])
            S.op("dve", lambda e: e.tensor_tensor(out=st[:, 3, 0:64], in0=cst[:, lv + 128:lv + 192],
                                                  in1=cst[:, lv + 192:lv + 256], op=ALU.mult), reads=[cst, st], writes=[st])
            S.op("dve", lambda e: e.reduce_sum(out=st[:, 4, 0:1], in_=st[:, 2, 0:64], axis=AX.X), reads=[st], writes=[st])
            S.op("dve", lambda e: e.reduce_sum(out=st[:, 4, 1:2], in_=st[:, 3, 0:64], axis=AX.X), reads=[st], writes=[st])
            S.op("act", lambda e: e.activation(st[:, 5, 0:2], st[:, 4, 0:2], AF.Exp), reads=[st], writes=[st])
            S.op("dve", lambda e: e.tensor_tensor(out=st[:, 6, 0:1], in0=st[:, 5, 1:2], in1=st[:, 5, 0:1], op=ALU.subtract),
                 reads=[st], writes=[st])
            S.op("dve", lambda e: e.tensor_scalar_add(NEGLAM[:], st[:, 6, 0:1], -lam_init), reads=[st], writes=[NEGLAM])
            S.op("dve", lambda e: e.tensor_scalar_mul(SGL[:], cst[:, LB + O_SG:LB + O_SG + 1], 1.0 - lam_init),
                 reads=[cst], writes=[SGL])
            S.op("dve", lambda e: e.memset(PH[:], 0.0), writes=[PH])
            S.op("dve", lambda e: e.memset(LH[:], 0.0), writes=[LH])
            S.op("dve", lambda e: e.memset(CAR[:], 0.0), writes=[CAR])
            S.barrier()
            S.retire([HM, mt_, WM])
        chk("setup")

        for ck in range(nchunk):
            c0 = ck * CK
            with contextlib.ExitStack() as p1:
                H = sb(p1, "H", [128, 16, CK], BF16)
                with contextlib.ExitStack() as p0:
                    xts = [sb(p0, "xt%d" % i, [128, 16, 512], F32) for i in range(2)]
                    sqs = [sb(p0, "sq%d" % i, [128, 512], F32) for i in range(3)]
                    r1 = sb(p0, "r1", [128, 512], F32)
                    r2 = sb(p0, "r2", [128, 512], F32)
                    for tt in range(4):
                        xt = xts[tt % 2]
                        S.dma("sp", xt[:], xsrc.rearrange("(c p) n -> p c n", p=128)[:, :, c0 + tt * 512:c0 + (tt + 1) * 512],
                              xt, writes=[xt])
                        rms_tile(None, xt, 512, LB + O_NG, lambda c, tt=tt: (H[:, c, tt * 512:(tt + 1) * 512], H),
                                 sqs, (r1, r2), PSB[7])
                    S.dma("sp", Hs_d.rearrange("c p n -> p c n"), H[:], H, reads=[H])
                    S.barrier()
                    S.retire(xts)
                chk("p0")
                Wt = [sb(p1, "Wt%d" % i, [128, 16, 512], BF16) for i in range(3)]
                wstate = {"next": 0, "map": {}}
                wseq = list(range(18))

                def wload(g):
                    if g in wstate["map"]:
                        return wstate["map"][g]
                    slot = wstate["next"] % 3
                    wstate["next"] += 1
                    for k in [k for k, v in wstate["map"].items() if v is Wt[slot]]:
                        del wstate["map"][k]
                    S.dma("pool", Wt[slot][:], w_in[l].rearrange("(c p) n -> p c n", p=128)[:, :, g * 512:(g + 1) * 512],
                          Wt[slot], writes=[Wt[slot]])
                    wstate["map"][g] = Wt[slot]
                    return Wt[slot]

                pair_i = [0]

                def next_pair():
                    k = pair_i[0] % 4
                    pair_i[0] += 1
                    return PSB[2 * k], PSB[2 * k + 1]

                def inproj_half(g, mloc, half):
                    W = wload(g)
                    if g + 1 < 18:
                        pass
                    pa, pb = next_pair()
                    for n, p_ in enumerate((pa, pb)):
                        for c in range(16):
                            S.op("pe", lambda e, c=c, n=n, p_=p_, W=W: e.matmul(
                                p_[:], W[:, c, mloc * 128:(mloc + 1) * 128],
                                H[:, c, half * 1024 + n * 512: half * 1024 + (n + 1) * 512],
                                start=(c == 0), stop=(c == 15)), reads=[W, H], writes=[p_], inc=(c == 15))
                    return pa, pb

                stg = [sb(p1, "stg%d" % i, [128, 1024], BF16) for i in range(4)]
                stg_i = [0]

                def next_stg():
                    s_ = stg[stg_i[0] % 4]
                    stg_i[0] += 1
                    return s_

                def evac_pair(eng, pa, pb, dst, fn):
                    for n, p_ in enumerate((pa, pb)):
                        S.op(eng, lambda e, n=n, p_=p_: fn(e, dst[:, n * 512:(n + 1) * 512], p_[:]),
                             reads=[p_], writes=[dst])

                wload(0)
                wload(1)
                for sec, (dst_d, scale) in enumerate(((QT_d, 0.125), (None, 1.0))):
                    for h in range(8):
                        g, mloc = sec * 2 + h // 4, h % 4
                        if mloc == 0 and g + 2 < 18:
                            pass
                        for half in range(2):
                            pa, pb = inproj_half(g, mloc, half)
                            s_ = next_stg()
                            evac_pair("act", pa, pb, s_, lambda e, o, i, scale=scale: e.activation(o, i, AF.Copy, scale=scale))
                            if sec == 0:
                                S.dma("sp", QT_d[h, :, half * 1024:(half + 1) * 1024], s_[:], s_, reads=[s_])
                            else:
                                S.dma("sp", KT_d[l, h, :, c0 + half * 1024:c0 + (half + 1) * 1024], s_[:], s_, reads=[s_])
                    if sec == 0:
                        wload(2)
                chk("qk")
                wload(3)
                with contextlib.ExitStack() as sA:
                    vst = [sb(sA, "vst%d" % i, [128, 512], BF16) for i in range(4)]
                    vi = 0
                    for vg in range(2):
                        W = wload(4 + vg)
                        wload(5 + vg)
                        for tt in range(16):
                            pb = PSB[vi % 8]
                            for c in range(16):
                                S.op("pe", lambda e, c=c, tt=tt, pb=pb, W=W: e.matmul(
                                    pb[:], H[:, c, tt * 128:(tt + 1) * 128], W[:, c, :], start=(c == 0), stop=(c == 15)),
                                    reads=[W, H], writes=[pb], inc=(c == 15))
                            vs = vst[vi % 4]
                            if vi % 2 == 0:
                                S.op("act", lambda e, vs=vs, pb=pb: e.copy(vs[:], pb[:]), reads=[pb], writes=[vs])
                            else:
                                S.op("dve", lambda e, vs=vs, pb=pb: e.tensor_copy(vs[:], pb[:]), reads=[pb], writes=[vs])
                            S.dma("sp", V_d[l, vg * 4:(vg + 1) * 4, c0 + tt * 128:c0 + (tt + 1) * 128, :].rearrange("h t e -> t h e"),
                                  vs[:].rearrange("t (h e) -> t h e", h=4), vs, reads=[vs])
                            vi += 1
                    S.barrier()
                    S.retire(vst)
                chk("v")
                sil = [sb(p1, "sil%d" % i, [128, 1024], F32) for i in range(2)]
                sil_i = [0]
                with contextlib.ExitStack() as sB:
                    WB = [sb(sB, "WBb%d" % i, [128, 16 + CK], F32) for i in range(4)]
                    Dg = sb(sB, "Dg", [128, 2, CK], BF16)
                    PWt = sb(sB, "PWt", [128, 4, 2, 256], BF16)
                    S.dma("pool", PWt[:], pool_w[l].rearrange("g (t p) j -> p g t j", p=128), PWt, writes=[PWt])
                    for grp in range(4):
                        w = 2 << grp
                        for jt in range(2):
                            mt = grp * 2 + jt
                            g, mloc = 6 + mt // 4, mt % 4
                            X = WB[0]
                            S.op("dve", lambda e, mt=mt, X=X: e.tensor_copy(X[:, 0:16], PH[:, mt, :]), reads=[PH], writes=[X])
                            for half in range(2):
                                pa, pb = inproj_half(g, mloc, half)
                                for n, p_ in enumerate((pa, pb)):
                                    S.op("act", lambda e, n=n, p_=p_, half=half, X=X: e.copy(
                                        X[:, 16 + half * 1024 + n * 512:16 + half * 1024 + (n + 1) * 512], p_[:]),
                                        reads=[p_], writes=[X])
                            S.op("dve", lambda e, mt=mt, X=X: e.tensor_copy(PH[:, mt, :], X[:, CK:CK + 16]), reads=[X], writes=[PH])
                            cur = X
                            sh = 1
                            bi_ = 1
                            while sh < w:
                                nxt = WB[bi_]
                                bi_ = 3 - bi_
                                S.op("dve", lambda e, cur=cur, nxt=nxt, sh=sh: e.tensor_tensor(
                                    out=nxt[:, sh:16 + CK], in0=cur[:, sh:16 + CK], in1=cur[:, 0:16 + CK - sh], op=ALU.add),
                                    reads=[cur], writes=[nxt])
                                cur = nxt
                                sh *= 2
                            S.op("dve", lambda e, cur=cur, jt=jt, w=w, X=X: e.scalar_tensor_tensor(
                                out=Dg[:, jt, :], in0=cur[:, 16:16 + CK], scalar=1.0 / w, in1=X[:, 16:16 + CK],
                                op0=ALU.mult, op1=ALU.subtract), reads=[cur, X], writes=[Dg])
                            if ck == 0:
                                tmp = WB[3]
                                S.op("dve", lambda e, cur=cur, grp=grp, tmp=tmp: e.tensor_tensor(
                                    out=tmp[:, 0:16], in0=cur[:, 16:32], in1=cst[:, O_IC + grp * 16:O_IC + grp * 16 + 16], op=ALU.mult),
                                    reads=[cur, cst], writes=[tmp])
                                S.op("dve", lambda e, jt=jt, tmp=tmp, X=X: e.tensor_tensor(
                                    out=Dg[:, jt, 0:16], in0=tmp[:, 0:16], in1=X[:, 16:32], op=ALU.subtract),
                                    reads=[tmp, X], writes=[Dg])
                        for jt in range(2):
                            mt = grp * 2 + jt
                            gg, gm = 13 + mt // 4, mt % 4
                            for half in range(2):
                                ya, yb = next_pair()
                                for n, p_ in enumerate((ya, yb)):
                                    for it in range(2):
                                        S.op("pe", lambda e, n=n, p_=p_, it=it, grp=grp, jt=jt, half=half: e.matmul(
                                            p_[:], PWt[:, grp, it, jt * 128:(jt + 1) * 128],
                                            Dg[:, it, half * 1024 + n * 512:half * 1024 + (n + 1) * 512],
                                            start=(it == 0), stop=(it == 1)), reads=[PWt, Dg], writes=[p_], inc=(it == 1))
                                pa, pb = inproj_half(gg, gm, half)
                                sl = sil[sil_i[0] % 2]
                                sil_i[0] += 1
                                evac_pair("act", pa, pb, sl, lambda e, o, i: e.activation(o, i, AF.Silu))
                                s_ = next_stg()
                                for n, p_ in enumerate((ya, yb)):
                                    S.op("dve", lambda e, n=n, p_=p_, sl=sl, s_=s_, mt=mt: e.scalar_tensor_tensor(
                                        out=s_[:, n * 512:(n + 1) * 512], in0=p_[:], scalar=cst[:, LB + O_PS + mt:LB + O_PS + mt + 1],
                                        in1=sl[:, n * 512:(n + 1) * 512], op0=ALU.mult, op1=ALU.mult),
                                        reads=[p_, cst, sl], writes=[s_])
                                S.dma("sp", G_d[8 + mt, :, half * 1024:(half + 1) * 1024], s_[:], s_, reads=[s_])
                    S.barrier()
                    S.retire([PWt])
                chk("pool")
                with contextlib.ExitStack() as sC:
                    WB = [sb(sC, "WBc%d" % i, [128, 16 + CK], F32) for i in range(5)]
                    LWA = sb(sC, "LWA", [128, 8, 128], BF16)
                    LWX = sb(sC, "LWX", [128, 8, 128], BF16)
                    xcb = sb(sC, "xcb", [128, CK], BF16)
                    S.dma("pool", LWA[:], lru_wa[l].rearrange("g i j -> i g j"), LWA, writes=[LWA])
                    S.dma("pool", LWX[:], lru_wx[l].rearrange("g i j -> i g j"), LWX, writes=[LWX])
                    for blk in range(8):
                        g, mloc = 8 + blk // 4, blk % 4
                        XL, XC, R_, I_, A_ = WB
                        HH = XL
                        S.op("dve", lambda e, blk=blk: e.tensor_copy(XL[:, 12:16], LH[:, blk, :]), reads=[LH], writes=[XL])
                        for half in range(2):
                            pa, pb = inproj_half(g, mloc, half)
                            for n, p_ in enumerate((pa, pb)):
                                S.op("act", lambda e, n=n, p_=p_, half=half: e.copy(
                                    XL[:, 16 + half * 1024 + n * 512:16 + half * 1024 + (n + 1) * 512], p_[:]),
                                    reads=[p_], writes=[XL])
                        S.op("dve", lambda e, blk=blk: e.tensor_copy(LH[:, blk, :], XL[:, 12 + CK:16 + CK]), reads=[XL], writes=[LH])
                        cw = LB + O_CW
                        S.op("dve", lambda e, blk=blk: e.tensor_scalar(
                            out=XC[:, 0:CK], in0=XL[:, 16:16 + CK], scalar1=cst[:, cw + 24 + blk:cw + 25 + blk],
                            scalar2=cst[:, LB + O_CB + blk:LB + O_CB + blk + 1], op0=ALU.mult, op1=ALU.add),
                            reads=[XL, cst], writes=[XC])
                        for k in (2, 1, 0):
                            S.op("dve", lambda e, blk=blk, k=k: e.scalar_tensor_tensor(
                                out=XC[:, 0:CK], in0=XL[:, 13 + k:13 + k + CK], scalar=cst[:, cw + k * 8 + blk:cw + k * 8 + blk + 1],
                                in1=XC[:, 0:CK], op0=ALU.mult, op1=ALU.add), reads=[XL, cst, XC], writes=[XC])
                        S.op("act", lambda e: e.copy(xcb[:], XC[:, 0:CK]), reads=[XC], writes=[xcb])
                        for which, (LWm, dstb, bcol) in enumerate(((LWA, R_, O_BA), (LWX, I_, O_BX))):
                            for half in range(2):
                                pa, pb = next_pair()
                                for n, p_ in enumerate((pa, pb)):
                                    S.op("pe", lambda e, n=n, p_=p_, half=half, LWm=LWm, blk=blk: e.matmul(
                                        p_[:], LWm[:, blk, :], xcb[:, half * 1024 + n * 512:half * 1024 + (n + 1) * 512],
                                        start=True, stop=True), reads=[LWm, xcb], writes=[p_])
                                    S.op("act", lambda e, n=n, p_=p_, half=half, dstb=dstb, bcol=bcol, blk=blk: e.activation(
                                        dstb[:, half * 1024 + n * 512:half * 1024 + (n + 1) * 512], p_[:], AF.Sigmoid,
                                        bias=cst[:, LB + bcol + blk:LB + bcol + blk + 1]), reads=[p_, cst], writes=[dstb])
                        S.op("act", lambda e, blk=blk: e.activation(A_[:, 0:CK], R_[:, 0:CK], AF.Exp, scale=CL[:, blk:blk + 1]),
                             reads=[R_, CL], writes=[A_])
                        S.op("act", lambda e, blk=blk: e.activation(R_[:, 0:CK], R_[:, 0:CK], AF.Exp, scale=CL2[:, blk:blk + 1]),
                             reads=[R_, CL2], writes=[R_])
                        S.op("dve", lambda e: e.tensor_scalar(out=R_[:, 0:CK], in0=R_[:, 0:CK], scalar1=-1.0, scalar2=1.0,
                                                              op0=ALU.mult, op1=ALU.add), reads=[R_], writes=[R_])
                        S.op("act", lambda e: e.activation(R_[:, 0:CK], R_[:, 0:CK], AF.Sqrt), reads=[R_], writes=[R_])
                        if ck == 0:
                            S.op("dve", lambda e: e.memset(R_[:, 0:1], 1.0), reads=[R_], writes=[R_])
                        S.op("dve", lambda e: e.tensor_tensor(out=I_[:, 0:CK], in0=I_[:, 0:CK], in1=XC[:, 0:CK], op=ALU.mult),
                             reads=[I_, XC], writes=[I_])
                        S.op("dve", lambda e: e.tensor_tensor(out=I_[:, 0:CK], in0=I_[:, 0:CK], in1=R_[:, 0:CK], op=ALU.mult),
                             reads=[I_, R_], writes=[I_])
                        S.op("dve", lambda e, blk=blk: e.tensor_tensor_scan(
                            out=HH[:, 0:CK], data0=A_[:, 0:CK], data1=I_[:, 0:CK], initial=CAR[:, blk:blk + 1],
                            op0=ALU.mult, op1=ALU.add), reads=[A_, I_, CAR], writes=[HH])
                        S.op("dve", lambda e, blk=blk: e.tensor_copy(CAR[:, blk:blk + 1], HH[:, CK - 1:CK]), reads=[HH], writes=[CAR])
                        gg, gm = 15 + blk // 4, blk % 4
                        for half in range(2):
                            pa, pb = inproj_half(gg, gm, half)
                            sl = sil[sil_i[0] % 2]
                            sil_i[0] += 1
                            evac_pair("act", pa, pb, sl, lambda e, o, i: e.activation(o, i, AF.Silu))
                            s_ = next_stg()
                            S.op("dve", lambda e, half=half, sl=sl, s_=s_: e.tensor_tensor(
                                out=s_[:], in0=HH[:, half * 1024:(half + 1) * 1024], in1=sl[:], op=ALU.mult),
                                reads=[HH, sl], writes=[s_])
                            S.dma("sp", G_d[16 + blk, :, half * 1024:(half + 1) * 1024], s_[:], s_, reads=[s_])
                    S.barrier()
                    S.retire([LWA, LWX])
                chk("lru")
                with contextlib.ExitStack() as sD:
                    QM = sb(sD, "QM", [128, CK], BF16)
                    Et = [sb(sD, "Etm%d" % i, [128, 512], BF16) for i in range(3)]
                    YM = sb(sD, "YM", [128, CK], F32)
                    RS = sb(sD, "RS", [128, CK], F32)
                    et_i = 0
                    for hm in range(4):
                        for half in range(2):
                            pa, pb = inproj_half(10, hm, half)
                            for n, p_ in enumerate((pa, pb)):
                                S.op("act", lambda e, n=n, p_=p_, half=half: e.activation(
                                    QM[:, half * 1024 + n * 512:half * 1024 + (n + 1) * 512], p_[:], AF.Copy, scale=128 ** -0.5),
                                    reads=[p_], writes=[QM])
                        for qt in range(4):
                            po, psm = PSB[0 + 2 * (qt % 2)], PSB[1 + 2 * (qt % 2)]
                            for mb in range(2):
                                pss = PSB[4 + (et_i % 3)]
                                et = Et[et_i % 3]
                                et_i += 1
                                S.op("pe", lambda e, mb=mb, hm=hm, qt=qt, pss=pss: e.matmul(
                                    pss[:], MK[:, hm, mb * 128:(mb + 1) * 128], QM[:, qt * 512:(qt + 1) * 512], start=True, stop=True),
                                    reads=[MK, QM], writes=[pss])
                                S.op("act", lambda e, et=et, pss=pss: e.activation(et[:], pss[:], AF.Exp), reads=[pss], writes=[et])
                                S.op("pe", lambda e, mb=mb, hm=hm, et=et, po=po: e.matmul(
                                    po[:], MV[:, mb, hm * 128:(hm + 1) * 128], et[:], start=(mb == 0), stop=(mb == 1)),
                                    reads=[MV, et], writes=[po], inc=False)
                                S.op("pe", lambda e, mb=mb, et=et, psm=psm: e.matmul(
                                    psm[:], ones_b[:], et[:], start=(mb == 0), stop=(mb == 1)),
                                    reads=[ones_b, et], writes=[psm, po], inc=True)
                            S.op("dve", lambda e, qt=qt, psm=psm: e.reciprocal(RS[:, qt * 512:(qt + 1) * 512], psm[:]),
                                 reads=[psm], writes=[RS])
                            S.op("dve", lambda e, qt=qt, po=po: e.tensor_tensor(
                                out=YM[:, qt * 512:(qt + 1) * 512], in0=po[:], in1=RS[:, qt * 512:(qt + 1) * 512], op=ALU.mult),
                                reads=[po, RS], writes=[YM])
                        for half in range(2):
                            pa, pb = inproj_half(17, hm, half)
                            sl = sil[sil_i[0] % 2]
                            sil_i[0] += 1
                            evac_pair("act", pa, pb, sl, lambda e, o, i: e.activation(o, i, AF.Silu))
                            s_ = next_stg()
                            S.op("dve", lambda e, half=half, sl=sl, s_=s_: e.tensor_tensor(
                                out=s_[:], in0=YM[:, half * 1024:(half + 1) * 1024], in1=sl[:], op=ALU.mult),
                                reads=[YM, sl], writes=[s_])
                            S.dma("sp", G_d[24 + hm, :, half * 1024:(half + 1) * 1024], s_[:], s_, reads=[s_])
                    S.barrier()
                chk("mem")
                wload(11)
                wload(12)
                for h in range(8):
                    for half in range(2):
                        pa, pb = inproj_half(11 + h // 4, h % 4, half)
                        s_ = next_stg()
                        evac_pair("act", pa, pb, s_, lambda e, o, i: e.activation(o, i, AF.Silu))
                        S.dma("sp", SGA_d[h, :, half * 1024:(half + 1) * 1024], s_[:], s_, reads=[s_])
                S.barrier()
                S.retire([H] + Wt + stg)

            chk("p1")
            with contextlib.ExitStack() as p2:
                nkc = ck + 1
                KTb = [sb(p2, "KTb%d" % i, [128, nkc * CK], BF16) for i in range(2)]
                Vb = [sb(p2, "Vb%d" % i, [128, nkc * 16, 128], BF16) for i in range(2)]
                Qb = [sb(p2, "Qb%d" % i, [128, CK], BF16) for i in range(2)]
                SGb = [sb(p2, "SGb%d" % i, [128, CK], BF16) for i in range(2)]
                Et = [sb(p2, "Et%d" % i, [128, 1024], BF16) for i in range(3)]
                OS = [sb(p2, "OS%d" % i, [128, 512], F32) for i in range(4)]
                ta = [sb(p2, "ta%d" % i, [128, 512], F32) for i in range(4)]
                gst = [sb(p2, "gst%d" % i, [128, 512], BF16) for i in range(2)]

                def load_head(h):
                    i = h % 2
                    S.dma("sp", Qb[i][:], QT_d[h], Qb[i], writes=[Qb[i]])
                    S.dma("sp", SGb[i][:], SGA_d[h], SGb[i], writes=[SGb[i]])
                    S.dma("sp", KTb[i][:], KT_d[l, h, :, 0:nkc * CK], KTb[i], writes=[KTb[i]])
                    S.dma("sp", Vb[i][:], V_d[l, h, 0:nkc * CK, :].rearrange("(kb p) e -> p kb e", p=128), Vb[i], writes=[Vb[i]])

                load_head(0)
                et_i = 0
                g_i = 0
                accb = (PSB[0], PSB[1], PSB[2], PSB[3])
                for h in range(8):
                    if h + 1 < 8:
                        load_head(h + 1)
                    i = h % 2
                    Q, KT_, V_, SG = Qb[i], KTb[i], Vb[i], SGb[i]
                    for qt in range(4):
                        nkb = ck * 16 + qt * 4 + 4
                        blocks = []
                        for kb in range(nkb):
                            sub = kb - (ck * 16 + qt * 4)
                            n0 = sub * 128 if sub > 0 else 0
                            blocks.append((kb, n0, sub >= 0))
                        nb = len(blocks)

                        def emit_s(j):
                            kb, n0, dg = blocks[j]
                            for c in range(2):
                                pss = PSB[4 + 2 * (j % 2) + c]
                                S.op("pe", lambda e, kb=kb, n0=n0, pss=pss, c=c: e.matmul(
                                    pss[:, n0:512], KT_[c * 64:(c + 1) * 64, kb * 128:(kb + 1) * 128],
                                    Q[c * 64:(c + 1) * 64, qt * 512 + n0:(qt + 1) * 512], start=True, stop=True),
                                    reads=[KT_, Q], writes=[pss])

                        emit_s(0)
                        if nb > 1:
                            emit_s(1)
                        for j in range(nb):
                            kb, n0, dg = blocks[j]
                            pa_, pb_ = PSB[4 + 2 * (j % 2)], PSB[5 + 2 * (j % 2)]
                            et = Et[et_i % 3]
                            et_i += 1
                            if n0 == 0:
                                pair_t = PSP[2 + (j % 2)]
                                S.op("act", lambda e, et=et, pair_t=pair_t: e.activation(et[:], pair_t[:], AF.Exp),
                                     reads=[pa_, pb_], writes=[et])
                            else:
                                for c, pss in enumerate((pa_, pb_)):
                                    S.op("act", lambda e, et=et, pss=pss, n0=n0, c=c: e.activation(
                                        et[:, c * 512 + n0:(c + 1) * 512], pss[:, n0:512], AF.Exp), reads=[pss], writes=[et])
                            if dg:
                                for c in range(2):
                                    S.op("dve", lambda e, et=et, n0=n0, c=c: e.tensor_tensor(
                                        out=et[:, c * 512 + n0:c * 512 + n0 + 128], in0=et[:, c * 512 + n0:c * 512 + n0 + 128],
                                        in1=tri[:], op=ALU.mult), reads=[et, tri], writes=[et])
                            if j + 2 < nb:
                                emit_s(j + 2)
                            last = (j == nb - 1)
                            for c in range(2):
                                po, psm = accb[2 * c], accb[2 * c + 1]
                                S.op("pe", lambda e, kb=kb, n0=n0, et=et, po=po, j=j, last=last, c=c: e.matmul(
                                    po[:, n0:512], V_[:, kb, :], et[:, c * 512 + n0:(c + 1) * 512], start=(j == 0), stop=last),
                                    reads=[V_, et], writes=[po], inc=False)
                                S.op("pe", lambda e, n0=n0, et=et, psm=psm, j=j, last=last, c=c: e.matmul(
                                    psm[:, n0:512], ones_b[:], et[:, c * 512 + n0:(c + 1) * 512], start=(j == 0), stop=last),
                                    reads=[ones_b, et], writes=[psm, po], inc=True)
                        for c in range(2):
                            po, psm = accb[2 * c], accb[2 * c + 1]
                            S.op("act", lambda e, c=c, po=po: e.copy(OS[2 * c][:], po[:]), reads=[po], writes=[OS[2 * c]])
                            S.op("dve", lambda e, c=c, psm=psm: e.reciprocal(OS[2 * c + 1][:], psm[:]), reads=[psm],
                                 writes=[OS[2 * c + 1]])
                        t0, t1, t2, t3 = ta
                        S.op("dve", lambda e: e.tensor_tensor(out=t0[:], in0=OS[0][:], in1=OS[1][:], op=ALU.mult),
                             reads=[OS[0], OS[1]], writes=[t0])
                        S.op("dve", lambda e: e.tensor_tensor(out=t1[:], in0=OS[2][:], in1=OS[3][:], op=ALU.mult),
                             reads=[OS[2], OS[3]], writes=[t1])
                        S.op("dve", lambda e: e.scalar_tensor_tensor(out=t2[:], in0=t1[:], scalar=NEGLAM[:, 0:1], in1=t0[:],
                                                                     op0=ALU.mult, op1=ALU.add),
                             reads=[t1, t0, NEGLAM], writes=[t2])
                        S.op("act", lambda e: e.activation(t3[:], t2[:], AF.Square), reads=[t2], writes=[t3])
                        pq = PSB[7]
                        S.op("pe", lambda e: e.matmul(pq[:], ones_f[:], t3[:], start=True, stop=True),
                             reads=[ones_f, t3], writes=[pq])
                        S.op("act", lambda e: e.activation(t0[:], pq[:], AF.Sqrt, bias=epsb[:, 0:1], scale=1.0 / 128),
                             reads=[pq, epsb], writes=[t0])
                        S.op("dve", lambda e: e.reciprocal(t1[:], t0[:]), reads=[t0], writes=[t1])
                        S.op("dve", lambda e: e.tensor_tensor(out=t2[:], in0=t2[:], in1=t1[:], op=ALU.mult),
                             reads=[t2, t1], writes=[t2])
                        gs = gst[g_i % 2]
                        g_i += 1
                        S.op("dve", lambda e, gs=gs, qt=qt: e.scalar_tensor_tensor(
                            out=gs[:], in0=t2[:], scalar=SGL[:, 0:1], in1=SG[:, qt * 512:(qt + 1) * 512],
                            op0=ALU.mult, op1=ALU.mult), reads=[t2, SGL, SG], writes=[gs])
                        S.dma("sp", G_d[h, :, qt * 512:(qt + 1) * 512], gs[:], gs, reads=[gs])
                S.barrier()
                S.retire(KTb + Vb + Qb + SGb + gst)

            chk("p2")
            with contextlib.ExitStack() as p3:
                Hh = sb(p3, "Hh", [128, 16, 1024], BF16)
                Gh = sb(p3, "Gh", [128, 28, 1024], BF16)
                NB3 = 3
                Wb = [sb(p3, "Wb%d" % i, [128, 8, 128], BF16) for i in range(NB3)]
                Wm = [sb(p3, "Wm%d" % i, [128, 16, 128], BF16) for i in range(NB3)]
                sg = [sb(p3, "sg%d" % i, [128, 1024], F32) for i in range(2)]
                tm = [sb(p3, "tm%d" % i, [128, 1024], F32) for i in range(2)]
                acc = sb(p3, "acc", [128, 1024], F32)
                mst = [sb(p3, "mst%d" % i, [128, 1024], BF16) for i in range(2)]
                KOFF = (0, 8, 16, 24, 28)
                steps = [(n_, bi) for n_ in range(16) for bi in range(4)]

                def load_w3(si):
                    n_, bi = steps[si]
                    i = si % NB3
                    nk = KOFF[bi + 1] - KOFF[bi]
                    S.dma("pool", Wb[i][:, 0:nk, :],
                          w_branch[l][KOFF[bi] * 128:KOFF[bi + 1] * 128, :].rearrange("(k p) n -> p k n", p=128)[:, :, n_ * 128:(n_ + 1) * 128],
                          Wb[i], writes=[Wb[i]])
                    cb_ = 9216 + bi * 2048 + n_ * 128
                    S.dma("pool", Wm[i][:], w_in[l].rearrange("(c p) n -> p c n", p=128)[:, :, cb_:cb_ + 128],
                          Wm[i], writes=[Wm[i]])

                it3 = 0
                for half in range(2):
                    S.dma("sp", Hh[:], Hs_d.rearrange("c p n -> p c n")[:, :, half * 1024:(half + 1) * 1024], Hh, writes=[Hh])
                    S.dma("sp", Gh[:], G_d.rearrange("c p n -> p c n")[:, :, half * 1024:(half + 1) * 1024], Gh, writes=[Gh])
                    load_w3(0)
                    load_w3(1)
                    for si, (n_, bi) in enumerate(steps):
                        if si + 2 < len(steps):
                            load_w3(si + 2)
                        i = si % NB3
                        if True:
                            za, zb = PSB[(it3 % 2) * 2], PSB[(it3 % 2) * 2 + 1]
                            ma, mb_ = PSB[4 + (it3 % 2) * 2], PSB[5 + (it3 % 2) * 2]
                            it3 += 1
                            ks = list(range(KOFF[bi], KOFF[bi + 1]))
                            for n, p_ in enumerate((ma, mb_)):
                                for c in range(16):
                                    S.op("pe", lambda e, c=c, n=n, p_=p_, i=i: e.matmul(
                                        p_[:], Wm[i][:, c, :], Hh[:, c, n * 512:(n + 1) * 512], start=(c == 0), stop=(c == 15)),
                                        reads=[Wm[i], Hh], writes=[p_], inc=(c == 15))
                            for n, p_ in enumerate((za, zb)):
                                for kk, k in enumerate(ks):
                                    S.op("pe", lambda e, k=k, kk=kk, n=n, p_=p_, i=i, ks=ks: e.matmul(
                                        p_[:], Wb[i][:, kk, :], Gh[:, k, n * 512:(n + 1) * 512], start=(kk == 0),
                                        stop=(kk == len(ks) - 1)), reads=[Wb[i], Gh], writes=[p_], inc=(kk == len(ks) - 1))
                            sgt = sg[bi % 2]
                            for n, p_ in enumerate((ma, mb_)):
                                S.op("act", lambda e, n=n, p_=p_, sgt=sgt: e.activation(sgt[:, n * 512:(n + 1) * 512], p_[:], AF.Sigmoid),
                                     reads=[p_], writes=[sgt])
                            dst = acc if bi == 0 else tm[bi % 2]
                            for n, p_ in enumerate((za, zb)):
                                S.op("dve", lambda e, n=n, p_=p_, sgt=sgt, dst=dst: e.tensor_tensor(
                                    out=dst[:, n * 512:(n + 1) * 512], in0=p_[:], in1=sgt[:, n * 512:(n + 1) * 512], op=ALU.mult),
                                    reads=[p_, sgt], writes=[dst])
                            if bi > 0:
                                if bi < 3:
                                    S.op("dve", lambda e, dst=dst: e.tensor_tensor(out=acc[:], in0=acc[:], in1=dst[:], op=ALU.add),
                                         reads=[acc, dst], writes=[acc])
                                else:
                                    ms = mst[n_ % 2]
                                    S.op("dve", lambda e, dst=dst, ms=ms: e.tensor_tensor(out=ms[:], in0=acc[:], in1=dst[:], op=ALU.add),
                                         reads=[acc, dst], writes=[ms])
                                    S.dma("sp", MG_d[n_, :, half * 1024:(half + 1) * 1024], ms[:], ms, reads=[ms])
                S.barrier()
                S.retire([Hh, Gh] + Wb + Wm + mst)

            chk("p3a")
            with contextlib.ExitStack() as p4:
                MGh = sb(p4, "MGh", [128, 16, 1024], BF16)
                Wo = [sb(p4, "Wo%d" % i, [128, 16, 128], BF16) for i in range(2)]
                xin = [sb(p4, "xin%d" % i, [128, 1024], F32) for i in range(2)]
                xo = [sb(p4, "xo%d" % i, [128, 1024], F32) for i in range(2)]
                it4 = 0
                for half in range(2):
                    S.dma("sp", MGh[:], MG_d.rearrange("c p n -> p c n")[:, :, half * 1024:(half + 1) * 1024], MGh, writes=[MGh])
                    cols = slice(c0 + half * 1024, c0 + (half + 1) * 1024)
                    for n_ in range(16):
                        i = it4 % 2
                        S.dma("pool", Wo[i][:], w_out[l].rearrange("(k p) n -> p k n", p=128)[:, :, n_ * 128:(n_ + 1) * 128],
                              Wo[i], writes=[Wo[i]])
                        S.dma("sp", xin[i][:], xsrc[n_ * 128:(n_ + 1) * 128, cols], xin[i], writes=[xin[i]])
                        ya, yb = PSB[(it4 % 4) * 2], PSB[(it4 % 4) * 2 + 1]
                        it4 += 1
                        for n, p_ in enumerate((ya, yb)):
                            for k in range(16):
                                S.op("pe", lambda e, k=k, n=n, p_=p_, i=i: e.matmul(
                                    p_[:], Wo[i][:, k, :], MGh[:, k, n * 512:(n + 1) * 512], start=(k == 0), stop=(k == 15)),
                                    reads=[Wo[i], MGh], writes=[p_], inc=(k == 15))
                            S.op("dve", lambda e, n=n, p_=p_, i=i: e.tensor_tensor(
                                out=xo[i][:, n * 512:(n + 1) * 512], in0=p_[:], in1=xin[i][:, n * 512:(n + 1) * 512], op=ALU.add),
                                reads=[p_, xin[i]], writes=[xo[i]])
                        S.dma("sp", xdst[n_ * 128:(n_ + 1) * 128, cols], xo[i][:], xo[i], reads=[xo[i]])
                S.barrier()
                S.retire([MGh] + Wo + xin + xo)

    except _Stop:
        pass
    with contextlib.ExitStack() as pf:
      if not stop_at:
          xts = [sb(pf, "fx%d" % i, [128, 16, 512], F32) for i in range(2)]
          ots = [sb(pf, "fo%d" % i, [128, 16, 512], F32) for i in range(2)]
          sqs = [sb(pf, "fsq%d" % i, [128, 512], F32) for i in range(3)]
          r1 = sb(pf, "fr1", [128, 512], F32)
          r2 = sb(pf, "fr2", [128, 512], F32)
          for tt in range(S_ // 512):
              xt = xts[tt % 2]
              ot = ots[tt % 2]
              S.dma("sp", xt[:], x2T.rearrange("(c p) n -> p c n", p=128)[:, :, tt * 512:(tt + 1) * 512], xt, writes=[xt])
              t1, t2 = r1, r2
              for c in range(16):
                  sq = sqs[c % 3]
                  S.op("act", lambda e, c=c, sq=sq, xt=xt: e.activation(sq[:], xt[:, c, :], AF.Square), reads=[xt], writes=[sq])
                  S.op("pe", lambda e, c=c, sq=sq: e.matmul(PSB[7][:], ones_f[:], sq[:], start=(c == 0), stop=(c == 15)),
                       reads=[ones_f, sq], writes=[PSB[7]])
              S.op("act", lambda e: e.activation(t1[:], PSB[7][:], AF.Sqrt, bias=epsb[:, 0:1], scale=1.0 / D),
                   reads=[PSB[7], epsb], writes=[t1])
              S.op("dve", lambda e: e.reciprocal(t2[:], t1[:]), reads=[t1], writes=[t2])
              for c in range(16):
                  S.op("dve", lambda e, c=c, xt=xt, ot=ot: e.scalar_tensor_tensor(
                      out=ot[:, c, :], in0=xt[:, c, :], scalar=cst[:, O_FG + c:O_FG + c + 1], in1=t2[:],
                      op0=ALU.mult, op1=ALU.mult), reads=[xt, cst, t2], writes=[ot])
              S.dma("sp", outT.rearrange("(c p) n -> p c n", p=128)[:, :, tt * 512:(tt + 1) * 512], ot[:], ot, reads=[ot])
          S.barrier()
    if not stop_at:
        es.close()
    return nc


def host_inputs(nchunk, b, x, mem, norm_g, w_in, lam_vecs, subln_g, pool_w, pool_scale, conv_w, conv_b,
                lru_wa, lru_ba, lru_wx, lru_bx, lru_lambda, mem_norm_g, w_mem_kv, w_branch, w_out, final_g):
    S_ = nchunk * CK
    f = np.float32

    def pc(v, n):
        return np.ascontiguousarray(np.asarray(v, f).reshape(n, 128).T)

    cst = np.zeros((128, NCST), f)
    for l in range(2):
        LB = l * LW
        cst[:, LB + O_NG:LB + O_NG + 16] = pc(norm_g[l], 16)
        cst[:, LB + O_MG:LB + O_MG + 16] = pc(mem_norm_g[l], 16)
        cst[:, LB + O_PS:LB + O_PS + 8] = pc(pool_scale[l], 8)
        for k in range(4):
            cst[:, LB + O_CW + k * 8:LB + O_CW + k * 8 + 8] = pc(conv_w[l][k], 8)
        cst[:, LB + O_CB:LB + O_CB + 8] = pc(conv_b[l], 8)
        cst[:, LB + O_BA:LB + O_BA + 8] = pc(lru_ba[l], 8)
        cst[:, LB + O_BX:LB + O_BX + 8] = pc(lru_bx[l], 8)
        cst[:, LB + O_LL:LB + O_LL + 8] = pc(lru_lambda[l], 8)
        cst[:, LB + O_SG] = np.asarray(subln_g[l], f)
        cst[:, LB + O_LV:LB + O_LV + 256] = np.asarray(lam_vecs[l], f).reshape(1, 256)
    cst[:, O_FG:O_FG + 16] = pc(final_g, 16)
    for g in range(4):
        w = 2 << g
        for t in range(16):
            cst[:, O_IC + g * 16 + t] = 1.0 / min(t + 1, w)
    tri = np.triu(np.ones((128, 128), f))
    return {
        "xT": np.ascontiguousarray(np.asarray(x[b, :S_], f).T),
        "memT": np.ascontiguousarray(np.asarray(mem[b], f).T),
        "cst": cst, "tri": tri,
        "w_in": np.asarray(w_in, f), "pool_w": np.asarray(pool_w, f), "lru_wa": np.asarray(lru_wa, f),
        "lru_wx": np.asarray(lru_wx, f), "w_mem_kv": np.asarray(w_mem_kv, f), "w_branch": np.asarray(w_branch, f),
        "w_out": np.asarray(w_out, f),
    }


def run(nchunk, inputs, n_batch=2):
    nc = build(nchunk)
    maps = []
    per_b = 8 // n_batch
    for c in range(8):
        maps.append(host_inputs(nchunk, min(c // per_b, n_batch - 1), **inputs))
    res = run_bass_kernel_spmd(nc, maps, core_ids=list(range(8)))
    S_ = nchunk * CK
    out = np.zeros((n_batch, S_, D), np.float32)
    q = S_ // per_b
    for c in range(8):
        b, j = c // per_b, c % per_b
        out[b, j * q:(j + 1) * q, :] = res.results[c]["outT"][:, j * q:(j + 1) * q].T
    return out


def kernel(**inputs):
    return run(4, inputs)
```

```python
import contextlib
import math
import numpy as np
import concourse.bass as bass
import concourse.mybir as mybir
from concourse.bass_utils import run_bass_kernel_spmd

F32 = mybir.dt.float32
BF16 = mybir.dt.bfloat16
AF = mybir.ActivationFunctionType
ALU = mybir.AluOpType
AX = mybir.AxisListType

D = 2048
NIN = 17408
CK = 2048
MEM = 256
EPS = 1e-6
ENGS = ("pe", "act", "dve", "pool", "sp")

LW = 16 + 16 + 8 + 32 + 8 + 8 + 8 + 8 + 1 + 256
O_NG, O_MG, O_PS, O_CW, O_CB, O_BA, O_BX, O_LL, O_SG, O_LV = 0, 16, 32, 40, 72, 80, 88, 96, 104, 105
O_FG = 2 * LW
O_IC = O_FG + 16
O_WS = O_IC + 64
O_AL = O_WS + 4
O_IF = O_AL + 4
O_IC2 = O_IF + 1
NCST = O_IC2 + 64


class _Stop(Exception):
    pass


class T:
    def __init__(self, t, name=""):
        self.t = t
        self.name = name
        self.w = {}
        self.r = {}
        self.chan = None

    def __getitem__(self, k):
        return self.t[k]


class TP(T):
    def __init__(self, t, off, name=""):
        T.__init__(self, t, name)
        self.off = off

    def __getitem__(self, k):
        if isinstance(k, slice):
            assert k == slice(None)
            return self.t[:, self.off:self.off + 512]
        ps, fs = k
        a = 0 if fs.start is None else fs.start
        b = 512 if fs.stop is None else fs.stop
        return self.t[ps, self.off + a:self.off + b]


class Chan:
    def __init__(self, sem):
        self.sem = sem
        self.count = 0


def _merge(d, s):
    for k, v in s.items():
        if d.get(k, 0) < v:
            d[k] = v


class Sched:
    def __init__(self, nc):
        self.nc = nc
        self.e = dict(pe=nc.tensor, act=nc.scalar, dve=nc.vector, pool=nc.gpsimd, sp=nc.sync)
        self.semobj = {}
        self.cnt = {}
        self.seen = {k: {} for k in ENGS}
        for k in ENGS:
            self.semobj[("c", k)] = nc.alloc_semaphore("cs_" + k)
            self.cnt[k] = 0
        self.chans = []
        self.free_ch = []
        self.pending = {k: False for k in ENGS}

    def _wait(self, eng, deps):
        for key, val in deps.items():
            if eng == "pe" and key == ("c", "pe"):
                continue
            if self.seen[eng].get(key, 0) >= val:
                continue
            self.e[eng].wait_ge(self.semobj[key], val)
            self.seen[eng][key] = val

    def op(self, eng, fn, reads=(), writes=(), inc=True):
        deps = {}
        for b in reads:
            _merge(deps, b.w)
        for b in writes:
            _merge(deps, b.w)
            _merge(deps, b.r)
        self._wait(eng, deps)
        ins = fn(self.e[eng])
        key = ("c", eng)
        val = self.cnt[eng] + 1
        if inc:
            ins.then_inc(self.semobj[key], 1)
            self.cnt[eng] = val
            self.pending[eng] = False
        else:
            self.pending[eng] = True
        for b in reads:
            if b.r.get(key, 0) < val:
                b.r[key] = val
        for b in writes:
            if b.w.get(key, 0) < val:
                b.w[key] = val
            b.r = {}
        return ins

    def get_chan(self, sb):
        if sb.chan is None:
            if self.free_ch:
                sb.chan = self.free_ch.pop()
            else:
                ch = Chan(self.nc.alloc_semaphore("ch%d" % len(self.chans)))
                self.chans.append(ch)
                self.semobj[("d", id(ch))] = ch.sem
                sb.chan = ch
        return sb.chan

    def dma(self, q, out_ap, in_ap, sb, reads=(), writes=()):
        deps = {}
        for b in reads:
            _merge(deps, b.w)
        for b in writes:
            _merge(deps, b.w)
            _merge(deps, b.r)
        self._wait(q, deps)
        ch = self.get_chan(sb)
        ins = self.e[q].dma_start(out=out_ap, in_=in_ap)
        ch.count += 16
        ins.then_inc(ch.sem, 16)
        key = ("d", id(ch))
        val = ch.count
        for b in reads:
            if b.r.get(key, 0) < val:
                b.r[key] = val
        for b in writes:
            if b.w.get(key, 0) < val:
                b.w[key] = val
            b.r = {}
        return ins

    def barrier(self, engs=ENGS):
        assert not any(self.pending.values()), self.pending
        marks = {}
        for k in ENGS:
            if self.cnt[k] > 0:
                marks[("c", k)] = self.cnt[k]
        for ch in self.chans:
            if ch.count > 0:
                marks[("d", id(ch))] = ch.count
        for eng in engs:
            deps = dict(marks)
            self._wait(eng, deps)

    def retire(self, bufs):
        for b in bufs:
            if b.chan is not None:
                self.free_ch.append(b.chan)
                b.chan = None


def build(nchunk, lam_inits=(0.2, 0.8 - 0.6 * math.exp(-0.3)), dbg=None):
    S_ = nchunk * CK
    nc = bass.Bass("TRN2", target_bir_lowering=False)
    xT = nc.dram_tensor("xT", [D, S_], F32, kind="ExternalInput").ap()
    memT = nc.dram_tensor("memT", [D, MEM], F32, kind="ExternalInput").ap()
    cst_d = nc.dram_tensor("cst", [128, NCST], F32, kind="ExternalInput").ap()
    tri_d = nc.dram_tensor("tri", [128, 128], F32, kind="ExternalInput").ap()
    w_in = nc.dram_tensor("w_in", [2, D, NIN], F32, kind="ExternalInput").ap()
    pool_w = nc.dram_tensor("pool_w", [2, 4, 256, 256], F32, kind="ExternalInput").ap()
    lru_wa = nc.dram_tensor("lru_wa", [2, 8, 128, 128], F32, kind="ExternalInput").ap()
    lru_wx = nc.dram_tensor("lru_wx", [2, 8, 128, 128], F32, kind="ExternalInput").ap()
    w_mem_kv = nc.dram_tensor("w_mem_kv", [2, D, 1024], F32, kind="ExternalInput").ap()
    w_branch = nc.dram_tensor("w_branch", [2, 3584, D], F32, kind="ExternalInput").ap()
    w_out = nc.dram_tensor("w_out", [2, D, D], F32, kind="ExternalInput").ap()
    DEDUP = nchunk > 1
    outT = nc.dram_tensor("outT", [D, CK if DEDUP else S_], F32, kind="ExternalOutput").ap()
    xown_d = nc.dram_tensor("xown_d", [D, CK], F32, kind="Internal").ap()
    x2own_d = nc.dram_tensor("x2own_d", [D, CK], F32, kind="Internal").ap()
    x1T = nc.dram_tensor("x1T", [D, S_], F32, kind="Internal").ap()
    x2T = nc.dram_tensor("x2T", [D, S_], F32, kind="Internal").ap()
    KT_d = nc.dram_tensor("KT_d", [2, 8, 128, S_], BF16, kind="Internal").ap()
    V_d = nc.dram_tensor("V_d", [2, 8, S_, 128], BF16, kind="Internal").ap()
    QT_d = nc.dram_tensor("QT_d", [8, 128, CK], BF16, kind="Internal").ap()
    SGA_d = nc.dram_tensor("SGA_d", [8, 128, CK], BF16, kind="Internal").ap()
    G_d = nc.dram_tensor("G_d", [28, 128, CK], BF16, kind="Internal").ap()
    Hs_d = nc.dram_tensor("Hs_d", [16, 128, CK], BF16, kind="Internal").ap()
    MG_d = nc.dram_tensor("MG_d", [16, 128, CK], BF16, kind="Internal").ap()

    S = Sched(nc)
    es = contextlib.ExitStack()
    import os
    stop_at = os.environ.get("KSTOP", "")

    def chk(name):
        if name == stop_at:
            S.barrier()
            raise _Stop()

    uid = [0]

    def sb(stack, name, shape, dt):
        uid[0] += 1
        name = "%s_u%d" % (name, uid[0])
        return T(stack.enter_context(nc.sbuf_tensor(name, shape, dt)), name)

    cst = sb(es, "cstb", [128, NCST], F32)
    ones_f = sb(es, "ones_f", [128, 128], F32)
    ones_b = sb(es, "ones_b", [128, 128], BF16)
    tri = sb(es, "trib", [128, 128], BF16)
    epsb = sb(es, "epsb", [128, 1], F32)
    PH = sb(es, "PH", [128, 8, 16], F32)
    LH = sb(es, "LH", [128, 8, 4], F32)
    CAR = sb(es, "CAR", [128, 8], F32)
    PHo = sb(es, "PHo", [128, 8, 16], F32)
    LHo = sb(es, "LHo", [128, 8, 4], F32)
    CARo = sb(es, "CARo", [128, 8], F32)
    CL = sb(es, "CL", [128, 8], F32)
    CL2 = sb(es, "CL2", [128, 8], F32)
    NEGLAM = sb(es, "NEGLAM", [128, 1], F32)
    SGL = sb(es, "SGL", [128, 1], F32)
    MK = sb(es, "MK", [128, 4, MEM], BF16)
    MV = sb(es, "MV", [128, 2, 512], BF16)
    smallt = sb(es, "smallt", [128, 8, 64], F32)
    PSP = [es.enter_context(nc.psum_tensor("psp%d" % i, [128, 1024], F32)) for i in range(4)]
    PSB = [TP(PSP[i // 2], (i % 2) * 512, "ps%d" % i) for i in range(8)]

    S.dma("sp", cst[:], cst_d, cst, writes=[cst])
    S.dma("pool", tri[:], tri_d, tri, writes=[tri])
    S.op("dve", lambda e: e.memset(ones_f[:], 1.0), writes=[ones_f])
    S.op("dve", lambda e: e.memset(ones_b[:], 1.0), writes=[ones_b])
    S.op("dve", lambda e: e.memset(epsb[:], EPS), writes=[epsb])

    def rms_tile(stack_bufs, xt, ncols, gcol, out_fn, sqs, rstd_bufs, psb):
        t1, t2 = rstd_bufs
        for c in range(16):
            sq = sqs[c % len(sqs)]
            S.op("act", lambda e, c=c, sq=sq: e.activation(sq[:, 0:ncols], xt[:, c, 0:ncols], AF.Square),
                 reads=[xt], writes=[sq])
            S.op("pe", lambda e, c=c, sq=sq: e.matmul(psb[:, 0:ncols], ones_f[:], sq[:, 0:ncols],
                                                     start=(c == 0), stop=(c == 15)),
                 reads=[ones_f, sq], writes=[psb], inc=True)
        S.op("act", lambda e: e.activation(t1[:, 0:ncols], psb[:, 0:ncols], AF.Sqrt, bias=epsb[:, 0:1], scale=1.0 / D),
             reads=[psb, epsb], writes=[t1])
        S.op("dve", lambda e: e.reciprocal(t2[:, 0:ncols], t1[:, 0:ncols]), reads=[t1], writes=[t2])
        for c in range(16):
            oap, ot = out_fn(c)
            S.op("dve", lambda e, c=c, oap=oap: e.scalar_tensor_tensor(
                out=oap, in0=xt[:, c, 0:ncols], scalar=cst[:, gcol + c:gcol + c + 1], in1=t2[:, 0:ncols],
                op0=ALU.mult, op1=ALU.mult), reads=[xt, cst, t2], writes=[ot])
        return t2

    try:
      for l in range(2):
        LB = l * LW
        lam_init = lam_inits[l]
        xsrc = xT if l == 0 else x1T
        xdst = x1T if l == 0 else x2T
        with contextlib.ExitStack() as ps_:
            S.barrier()
            HM = sb(ps_, "HM", [128, 16, MEM], BF16)
            mt_ = sb(ps_, "mt_", [128, 16, MEM], F32)
            WM = sb(ps_, "WM", [128, 16, 1024], BF16)
            sqs = [sb(ps_, "sqm%d" % i, [128, 512], F32) for i in range(3)]
            r1 = sb(ps_, "r1m", [128, 512], F32)
            r2 = sb(ps_, "r2m", [128, 512], F32)
            S.dma("sp", mt_[:], memT.rearrange("(c p) n -> p c n", p=128), mt_, writes=[mt_])
            S.dma("pool", WM[:], w_mem_kv[l].rearrange("(c p) n -> p c n", p=128), WM, writes=[WM])
            rms_tile(None, mt_, MEM, LB + O_MG, lambda c: (HM[:, c, :], HM), sqs, (r1, r2), PSB[7])
            for hm in range(4):
                pb = PSB[hm % 4]
                for c in range(16):
                    S.op("pe", lambda e, c=c, hm=hm, pb=pb: e.matmul(pb[:, 0:MEM], WM[:, c, hm * 128:(hm + 1) * 128],
                                                                    HM[:, c, :], start=(c == 0), stop=(c == 15)),
                         reads=[WM, HM], writes=[pb], inc=(c == 15))
                S.op("act", lambda e, hm=hm, pb=pb: e.copy(MK[:, hm, :], pb[:, 0:MEM]), reads=[pb], writes=[MK])
            for mb in range(2):
                pb = PSB[4 + mb]
                for c in range(16):
                    S.op("pe", lambda e, c=c, mb=mb, pb=pb: e.matmul(pb[:], HM[:, c, mb * 128:(mb + 1) * 128],
                                                                    WM[:, c, 512:1024], start=(c == 0), stop=(c == 15)),
                         reads=[WM, HM], writes=[pb], inc=(c == 15))
                S.op("act", lambda e, mb=mb, pb=pb: e.copy(MV[:, mb, :], pb[:]), reads=[pb], writes=[MV])
            st = smallt
            S.op("act", lambda e: e.activation(st[:, 0, 0:8], cst[:, LB + O_LL:LB + O_LL + 8], AF.Exp, scale=-1.0),
                 reads=[cst], writes=[st])
            S.op("dve", lambda e: e.tensor_scalar_add(st[:, 0, 0:8], st[:, 0, 0:8], 1.0), reads=[st], writes=[st])
            S.op("act", lambda e: e.activation(st[:, 1, 0:8], st[:, 0, 0:8], AF.Ln), reads=[st], writes=[st])
            S.op("dve", lambda e: e.tensor_scalar_mul(CL[:], st[:, 1, 0:8], -8.0), reads=[st], writes=[CL])
            S.op("dve", lambda e: e.tensor_scalar_mul(CL2[:], st[:, 1, 0:8], -16.0), reads=[st], writes=[CL2])
            lv = LB + O_LV
            S.op("dve", lambda e: e.tensor_tensor(out=st[:, 2, 0:64], in0=cst[:, lv:lv + 64], in1=cst[:, lv + 64:lv + 128],
                                                  op=ALU.mult), reads=[cst, st], writes=[st])
            S.op("dve", lambda e: e.tensor_tensor(out=st[:, 3, 0:64], in0=cst[:, lv + 128:lv + 192],
                                                  in1=cst[:, lv + 192:lv + 256], op=ALU.mult), reads=[cst, st], writes=[st])
            S.op("dve", lambda e: e.reduce_sum(out=st[:, 4, 0:1], in_=st[:, 2, 0:64], axis=AX.X), reads=[st], writes=[st])
            S.op("dve", lambda e: e.reduce_sum(out=st[:, 4, 1:2], in_=st[:, 3, 0:64], axis=AX.X), reads=[st], writes=[st])
            S.op("act", lambda e: e.activation(st[:, 5, 0:2], st[:, 4, 0:2], AF.Exp), reads=[st], writes=[st])
            S.op("dve", lambda e: e.tensor_tensor(out=st[:, 6, 0:1], in0=st[:, 5, 1:2], in1=st[:, 5, 0:1], op=ALU.subtract),
                 reads=[st], writes=[st])
            S.op("dve", lambda e: e.tensor_scalar_add(NEGLAM[:], st[:, 6, 0:1], -lam_init), reads=[st], writes=[NEGLAM])
            S.op("dve", lambda e: e.tensor_scalar_mul(SGL[:], cst[:, LB + O_SG:LB + O_SG + 1], 1.0 - lam_init),
                 reads=[cst], writes=[SGL])
            S.op("dve", lambda e: e.memset(PH[:], 0.0), writes=[PH])
            S.op("dve", lambda e: e.memset(LH[:], 0.0), writes=[LH])
            S.op("dve", lambda e: e.memset(CAR[:], 0.0), writes=[CAR])
            S.barrier()
            S.retire([HM, mt_, WM])
        chk("setup")

        dedup = DEDUP and l == 1
        passes = [("full", ck) for ck in range(nchunk)]
        if dedup:
            passes = [("light", ck) for ck in range(nchunk)] + [("heavy", 0)]
            S.op("dve", lambda e: e.memset(PHo[:], 0.0), writes=[PHo])
            S.op("dve", lambda e: e.memset(LHo[:], 0.0), writes=[LHo])
            S.op("dve", lambda e: e.memset(CARo[:], 0.0), writes=[CARo])
        for mode, ck in passes:
            light = mode == "light"
            heavy = mode == "heavy"
            c0 = ck * CK
            with contextlib.ExitStack() as p1:
                H = sb(p1, "H", [128, 16, CK], BF16)
                with contextlib.ExitStack() as p0:
                    xts = [sb(p0, "xt%d" % i, [128, 16, 512], F32) for i in range(2)]
                    sqs = [sb(p0, "sq%d" % i, [128, 512], F32) for i in range(3)]
                    r1 = sb(p0, "r1", [128, 512], F32)
                    r2 = sb(p0, "r2", [128, 512], F32)
                    if heavy:
                        xld = sb(p0, "xld", [128, 16, 512], F32)
                    for tt in range(4):
                        xt = xts[tt % 2]
                        if heavy:
                            for k_ in range(nchunk):
                                S.dma("sp", xld[:], x1T.rearrange("(c p) n -> p c n", p=128)[:, :, k_ * CK + tt * 512:k_ * CK + (tt + 1) * 512],
                                      xld, writes=[xld])
                                if k_ == 0:
                                    S.op("dve", lambda e, xt=xt: e.tensor_scalar(
                                        out=xt[:], in0=xld[:], scalar1=cst[:, O_WS:O_WS + 1], scalar2=None, op0=ALU.mult),
                                        reads=[xld, cst], writes=[xt])
                                else:
                                    S.op("dve", lambda e, xt=xt, k_=k_: e.scalar_tensor_tensor(
                                        out=xt[:], in0=xld[:], scalar=cst[:, O_WS + k_:O_WS + k_ + 1], in1=xt[:],
                                        op0=ALU.mult, op1=ALU.add), reads=[xld, cst, xt], writes=[xt])
                            S.dma("sp", xown_d.rearrange("(c p) n -> p c n", p=128)[:, :, tt * 512:(tt + 1) * 512], xt[:], xt, reads=[xt])
                        else:
                            S.dma("sp", xt[:], xsrc.rearrange("(c p) n -> p c n", p=128)[:, :, c0 + tt * 512:c0 + (tt + 1) * 512],
                                  xt, writes=[xt])
                        rms_tile(None, xt, 512, LB + O_NG, lambda c, tt=tt: (H[:, c, tt * 512:(tt + 1) * 512], H),
                                 sqs, (r1, r2), PSB[7])
                    if not light:
                        S.dma("sp", Hs_d.rearrange("c p n -> p c n"), H[:], H, reads=[H])
                    S.barrier()
                    S.retire(xts)
                chk("p0")
                Wt = [sb(p1, "Wt%d" % i, [128, 16, 512], BF16) for i in range(3)]
                wstate = {"next": 0, "map": {}}
                wseq = list(range(18))

                def wload(g):
                    if g in wstate["map"]:
                        return wstate["map"][g]
                    slot = wstate["next"] % 3
                    wstate["next"] += 1
                    for k in [k for k, v in wstate["map"].items() if v is Wt[slot]]:
                        del wstate["map"][k]
                    S.dma("pool", Wt[slot][:], w_in[l].rearrange("(c p) n -> p c n", p=128)[:, :, g * 512:(g + 1) * 512],
                          Wt[slot], writes=[Wt[slot]])
                    wstate["map"][g] = Wt[slot]
                    return Wt[slot]

                def pre(g):
                    need = set(range(18))
                    if light:
                        need = {2, 3, 4, 5, 6, 7, 8, 9}
                    if heavy:
                        need = set(range(18)) - {2, 3, 4, 5}
                    if g in need:
                        wload(g)

                pair_i = [0]

                def next_pair():
                    k = pair_i[0] % 4
                    pair_i[0] += 1
                    return PSB[2 * k], PSB[2 * k + 1]

                def inproj_half(g, mloc, half):
                    W = wload(g)
                    if g + 1 < 18:
                        pass
                    pa, pb = next_pair()
                    for n, p_ in enumerate((pa, pb)):
                        for c in range(16):
                            S.op("pe", lambda e, c=c, n=n, p_=p_, W=W: e.matmul(
                                p_[:], W[:, c, mloc * 128:(mloc + 1) * 128],
                                H[:, c, half * 1024 + n * 512: half * 1024 + (n + 1) * 512],
                                start=(c == 0), stop=(c == 15)), reads=[W, H], writes=[p_], inc=(c == 15))
                    return pa, pb

                stg = [sb(p1, "stg%d" % i, [128, 1024], BF16) for i in range(4)]
                stg_i = [0]

                def next_stg():
                    s_ = stg[stg_i[0] % 4]
                    stg_i[0] += 1
                    return s_

                def evac_pair(eng, pa, pb, dst, fn):
                    for n, p_ in enumerate((pa, pb)):
                        S.op(eng, lambda e, n=n, p_=p_: fn(e, dst[:, n * 512:(n + 1) * 512], p_[:]),
                             reads=[p_], writes=[dst])

                pre(0)
                pre(1)
                for sec, (dst_d, scale) in enumerate(((QT_d, 0.125), (None, 1.0))):
                    if (sec == 0 and light) or (sec == 1 and heavy):
                        continue
                    for h in range(8):
                        g, mloc = sec * 2 + h // 4, h % 4
                        if mloc == 0 and g + 2 < 18:
                            pass
                        for half in range(2):
                            pa, pb = inproj_half(g, mloc, half)
                            s_ = next_stg()
                            evac_pair("act", pa, pb, s_, lambda e, o, i, scale=scale: e.activation(o, i, AF.Copy, scale=scale))
                            if sec == 0:
                                S.dma("sp", QT_d[h, :, half * 1024:(half + 1) * 1024], s_[:], s_, reads=[s_])
                            else:
                                S.dma("sp", KT_d[l, h, :, c0 + half * 1024:c0 + (half + 1) * 1024], s_[:], s_, reads=[s_])
                    if sec == 0:
                        pre(2)
                chk("qk")
                pre(3)
                with contextlib.ExitStack() as sA:
                    vst = [sb(sA, "vst%d" % i, [128, 512], BF16) for i in range(4)]
                    vi = 0
                    for vg in range(2):
                        if heavy:
                            continue
                        W = wload(4 + vg)
                        pre(5 + vg)
                        for tt in range(16):
                            pb = PSB[vi % 8]
                            for c in range(16):
                                S.op("pe", lambda e, c=c, tt=tt, pb=pb, W=W: e.matmul(
                                    pb[:], H[:, c, tt * 128:(tt + 1) * 128], W[:, c, :], start=(c == 0), stop=(c == 15)),
                                    reads=[W, H], writes=[pb], inc=(c == 15))
                            vs = vst[vi % 4]
                            if vi % 2 == 0:
                                S.op("act", lambda e, vs=vs, pb=pb: e.copy(vs[:], pb[:]), reads=[pb], writes=[vs])
                            else:
                                S.op("dve", lambda e, vs=vs, pb=pb: e.tensor_copy(vs[:], pb[:]), reads=[pb], writes=[vs])
                            S.dma("sp", V_d[l, vg * 4:(vg + 1) * 4, c0 + tt * 128:c0 + (tt + 1) * 128, :].rearrange("h t e -> t h e"),
                                  vs[:].rearrange("t (h e) -> t h e", h=4), vs, reads=[vs])
                            vi += 1
                    S.barrier()
                    S.retire(vst)
                chk("v")
                sil = [sb(p1, "sil%d" % i, [128, 1024], F32) for i in range(2)]
                sil_i = [0]
                if light:
                    S.op("dve", lambda e: e.scalar_tensor_tensor(
                        out=PHo[:], in0=PH[:], scalar=cst[:, O_WS + ck:O_WS + ck + 1], in1=PHo[:], op0=ALU.mult, op1=ALU.add),
                        reads=[PH, PHo, cst], writes=[PHo])
                    for mt in range(8):
                        W = wload(6 + mt // 4)
                        pb = PSB[mt % 8]
                        for c in range(16):
                            S.op("pe", lambda e, c=c, mt=mt, pb=pb, W=W: e.matmul(
                                pb[:, 0:16], W[:, c, (mt % 4) * 128:(mt % 4 + 1) * 128], H[:, c, CK - 16:CK],
                                start=(c == 0), stop=(c == 15)), reads=[W, H], writes=[pb], inc=(c == 15))
                        S.op("act", lambda e, mt=mt, pb=pb: e.copy(PH[:, mt, :], pb[:, 0:16]), reads=[pb], writes=[PH])
                    S.barrier()
                if heavy:
                    S.op("dve", lambda e: e.tensor_copy(PH[:], PHo[:]), reads=[PHo], writes=[PH])
                    S.op("dve", lambda e: e.tensor_copy(LH[:], LHo[:]), reads=[LHo], writes=[LH])
                    S.op("dve", lambda e: e.tensor_copy(CAR[:], CARo[:]), reads=[CARo], writes=[CAR])
                with contextlib.ExitStack() as sB:
                  if not light:
                      WB = [sb(sB, "WBb%d" % i, [128, 16 + CK], F32) for i in range(4)]
                      Dg = sb(sB, "Dg", [128, 2, CK], BF16)
                      PWt = sb(sB, "PWt", [128, 4, 2, 256], BF16)
                      S.dma("pool", PWt[:], pool_w[l].rearrange("g (t p) j -> p g t j", p=128), PWt, writes=[PWt])
                      for grp in range(4):
                          w = 2 << grp
                          for jt in range(2):
                              mt = grp * 2 + jt
                              g, mloc = 6 + mt // 4, mt % 4
                              X = WB[0]
                              S.op("dve", lambda e, mt=mt, X=X: e.tensor_copy(X[:, 0:16], PH[:, mt, :]), reads=[PH], writes=[X])
                              for half in range(2):
                                  pa, pb = inproj_half(g, mloc, half)
                                  for n, p_ in enumerate((pa, pb)):
                                      S.op("act", lambda e, n=n, p_=p_, half=half, X=X: e.copy(
                                          X[:, 16 + half * 1024 + n * 512:16 + half * 1024 + (n + 1) * 512], p_[:]),
                                          reads=[p_], writes=[X])
                              S.op("dve", lambda e, mt=mt, X=X: e.tensor_copy(PH[:, mt, :], X[:, CK:CK + 16]), reads=[X], writes=[PH])
                              cur = X
                              sh = 1
                              bi_ = 1
                              while sh < w:
                                  nxt = WB[bi_]
                                  bi_ = 3 - bi_
                                  S.op("dve", lambda e, cur=cur, nxt=nxt, sh=sh: e.tensor_tensor(
                                      out=nxt[:, sh:16 + CK], in0=cur[:, sh:16 + CK], in1=cur[:, 0:16 + CK - sh], op=ALU.add),
                                      reads=[cur], writes=[nxt])
                                  cur = nxt
                                  sh *= 2
                              S.op("dve", lambda e, cur=cur, jt=jt, w=w, X=X: e.scalar_tensor_tensor(
                                  out=Dg[:, jt, :], in0=cur[:, 16:16 + CK], scalar=1.0 / w, in1=X[:, 16:16 + CK],
                                  op0=ALU.mult, op1=ALU.subtract), reads=[cur, X], writes=[Dg])
                              if ck == 0 or heavy:
                                  icb = O_IC2 if heavy else O_IC
                                  tmp = WB[3]
                                  S.op("dve", lambda e, cur=cur, grp=grp, tmp=tmp, icb=icb: e.tensor_tensor(
                                      out=tmp[:, 0:16], in0=cur[:, 16:32], in1=cst[:, icb + grp * 16:icb + grp * 16 + 16], op=ALU.mult),
                                      reads=[cur, cst], writes=[tmp])
                                  S.op("dve", lambda e, jt=jt, tmp=tmp, X=X: e.tensor_tensor(
                                      out=Dg[:, jt, 0:16], in0=tmp[:, 0:16], in1=X[:, 16:32], op=ALU.subtract),
                                      reads=[tmp, X], writes=[Dg])
                          for jt in range(2):
                              mt = grp * 2 + jt
                              gg, gm = 13 + mt // 4, mt % 4
                              for half in range(2):
                                  ya, yb = next_pair()
                                  for n, p_ in enumerate((ya, yb)):
                                      for it in range(2):
                                          S.op("pe", lambda e, n=n, p_=p_, it=it, grp=grp, jt=jt, half=half: e.matmul(
                                              p_[:], PWt[:, grp, it, jt * 128:(jt + 1) * 128],
                                              Dg[:, it, half * 1024 + n * 512:half * 1024 + (n + 1) * 512],
                                              start=(it == 0), stop=(it == 1)), reads=[PWt, Dg], writes=[p_], inc=(it == 1))
                                  pa, pb = inproj_half(gg, gm, half)
                                  sl = sil[sil_i[0] % 2]
                                  sil_i[0] += 1
                                  evac_pair("act", pa, pb, sl, lambda e, o, i: e.activation(o, i, AF.Silu))
                                  s_ = next_stg()
                                  for n, p_ in enumerate((ya, yb)):
                                      S.op("dve", lambda e, n=n, p_=p_, sl=sl, s_=s_, mt=mt: e.scalar_tensor_tensor(
                                          out=s_[:, n * 512:(n + 1) * 512], in0=p_[:], scalar=cst[:, LB + O_PS + mt:LB + O_PS + mt + 1],
                                          in1=sl[:, n * 512:(n + 1) * 512], op0=ALU.mult, op1=ALU.mult),
                                          reads=[p_, cst, sl], writes=[s_])
                                  S.dma("sp", G_d[8 + mt, :, half * 1024:(half + 1) * 1024], s_[:], s_, reads=[s_])
                      S.barrier()
                      S.retire([PWt])
                chk("pool")
                with contextlib.ExitStack() as sC:
                    WB = [sb(sC, "WBc%d" % i, [128, 16 + CK], F32) for i in range(5)]
                    LWA = sb(sC, "LWA", [128, 8, 128], BF16)
                    LWX = sb(sC, "LWX", [128, 8, 128], BF16)
                    xcb = sb(sC, "xcb", [128, CK], BF16)
                    S.dma("pool", LWA[:], lru_wa[l].rearrange("g i j -> i g j"), LWA, writes=[LWA])
                    S.dma("pool", LWX[:], lru_wx[l].rearrange("g i j -> i g j"), LWX, writes=[LWX])
                    if light:
                        S.op("dve", lambda e: e.scalar_tensor_tensor(
                            out=LHo[:], in0=LH[:], scalar=cst[:, O_WS + ck:O_WS + ck + 1], in1=LHo[:], op0=ALU.mult, op1=ALU.add),
                            reads=[LH, LHo, cst], writes=[LHo])
                        S.op("dve", lambda e: e.scalar_tensor_tensor(
                            out=CARo[:], in0=CAR[:], scalar=cst[:, O_WS + ck:O_WS + ck + 1], in1=CARo[:], op0=ALU.mult, op1=ALU.add),
                            reads=[CAR, CARo, cst], writes=[CARo])
                    for blk in range(8):
                        g, mloc = 8 + blk // 4, blk % 4
                        XL, XC, R_, I_, A_ = WB
                        HH = XL
                        S.op("dve", lambda e, blk=blk: e.tensor_copy(XL[:, 12:16], LH[:, blk, :]), reads=[LH], writes=[XL])
                        for half in range(2):
                            pa, pb = inproj_half(g, mloc, half)
                            for n, p_ in enumerate((pa, pb)):
                                S.op("act", lambda e, n=n, p_=p_, half=half: e.copy(
                                    XL[:, 16 + half * 1024 + n * 512:16 + half * 1024 + (n + 1) * 512], p_[:]),
                                    reads=[p_], writes=[XL])
                        S.op("dve", lambda e, blk=blk: e.tensor_copy(LH[:, blk, :], XL[:, 12 + CK:16 + CK]), reads=[XL], writes=[LH])
                        cw = LB + O_CW
                        S.op("dve", lambda e, blk=blk: e.tensor_scalar(
                            out=XC[:, 0:CK], in0=XL[:, 16:16 + CK], scalar1=cst[:, cw + 24 + blk:cw + 25 + blk],
                            scalar2=cst[:, LB + O_CB + blk:LB + O_CB + blk + 1], op0=ALU.mult, op1=ALU.add),
                            reads=[XL, cst], writes=[XC])
                        for k in (2, 1, 0):
                            S.op("dve", lambda e, blk=blk, k=k: e.scalar_tensor_tensor(
                                out=XC[:, 0:CK], in0=XL[:, 13 + k:13 + k + CK], scalar=cst[:, cw + k * 8 + blk:cw + k * 8 + blk + 1],
                                in1=XC[:, 0:CK], op0=ALU.mult, op1=ALU.add), reads=[XL, cst, XC], writes=[XC])
                        S.op("act", lambda e: e.copy(xcb[:], XC[:, 0:CK]), reads=[XC], writes=[xcb])
                        for which, (LWm, dstb, bcol) in enumerate(((LWA, R_, O_BA), (LWX, I_, O_BX))):
                            for half in range(2):
                                pa, pb = next_pair()
                                for n, p_ in enumerate((pa, pb)):
                                    S.op("pe", lambda e, n=n, p_=p_, half=half, LWm=LWm, blk=blk: e.matmul(
                                        p_[:], LWm[:, blk, :], xcb[:, half * 1024 + n * 512:half * 1024 + (n + 1) * 512],
                                        start=True, stop=True), reads=[LWm, xcb], writes=[p_])
                                    S.op("act", lambda e, n=n, p_=p_, half=half, dstb=dstb, bcol=bcol, blk=blk: e.activation(
                                        dstb[:, half * 1024 + n * 512:half * 1024 + (n + 1) * 512], p_[:], AF.Sigmoid,
                                        bias=cst[:, LB + bcol + blk:LB + bcol + blk + 1]), reads=[p_, cst], writes=[dstb])
                        S.op("act", lambda e, blk=blk: e.activation(A_[:, 0:CK], R_[:, 0:CK], AF.Exp, scale=CL[:, blk:blk + 1]),
                             reads=[R_, CL], writes=[A_])
                        S.op("act", lambda e, blk=blk: e.activation(R_[:, 0:CK], R_[:, 0:CK], AF.Exp, scale=CL2[:, blk:blk + 1]),
                             reads=[R_, CL2], writes=[R_])
                        S.op("dve", lambda e: e.tensor_scalar(out=R_[:, 0:CK], in0=R_[:, 0:CK], scalar1=-1.0, scalar2=1.0,
                                                              op0=ALU.mult, op1=ALU.add), reads=[R_], writes=[R_])
                        S.op("act", lambda e: e.activation(R_[:, 0:CK], R_[:, 0:CK], AF.Sqrt), reads=[R_], writes=[R_])
                        if heavy:
                            S.op("dve", lambda e: e.tensor_scalar(out=R_[:, 0:1], in0=R_[:, 0:1], scalar1=cst[:, O_IF:O_IF + 1],
                                                                  scalar2=None, op0=ALU.max), reads=[R_, cst], writes=[R_])
                        elif ck == 0:
                            S.op("dve", lambda e: e.memset(R_[:, 0:1], 1.0), reads=[R_], writes=[R_])
                        S.op("dve", lambda e: e.tensor_tensor(out=I_[:, 0:CK], in0=I_[:, 0:CK], in1=XC[:, 0:CK], op=ALU.mult),
                             reads=[I_, XC], writes=[I_])
                        S.op("dve", lambda e: e.tensor_tensor(out=I_[:, 0:CK], in0=I_[:, 0:CK], in1=R_[:, 0:CK], op=ALU.mult),
                             reads=[I_, R_], writes=[I_])
                        S.op("dve", lambda e, blk=blk: e.tensor_tensor_scan(
                            out=HH[:, 0:CK], data0=A_[:, 0:CK], data1=I_[:, 0:CK], initial=CAR[:, blk:blk + 1],
                            op0=ALU.mult, op1=ALU.add), reads=[A_, I_, CAR], writes=[HH])
                        S.op("dve", lambda e, blk=blk: e.tensor_copy(CAR[:, blk:blk + 1], HH[:, CK - 1:CK]), reads=[HH], writes=[CAR])
                        gg, gm = 15 + blk // 4, blk % 4
                        for half in range(2):
                            if light:
                                continue
                            pa, pb = inproj_half(gg, gm, half)
                            sl = sil[sil_i[0] % 2]
                            sil_i[0] += 1
                            evac_pair("act", pa, pb, sl, lambda e, o, i: e.activation(o, i, AF.Silu))
                            s_ = next_stg()
                            S.op("dve", lambda e, half=half, sl=sl, s_=s_: e.tensor_tensor(
                                out=s_[:], in0=HH[:, half * 1024:(half + 1) * 1024], in1=sl[:], op=ALU.mult),
                                reads=[HH, sl], writes=[s_])
                            S.dma("sp", G_d[16 + blk, :, half * 1024:(half + 1) * 1024], s_[:], s_, reads=[s_])
                    S.barrier()
                    S.retire([LWA, LWX])
                chk("lru")
                with contextlib.ExitStack() as sD:
                  if not light:
                      QM = sb(sD, "QM", [128, CK], BF16)
                      Et = [sb(sD, "Etm%d" % i, [128, 512], BF16) for i in range(3)]
                      YM = sb(sD, "YM", [128, CK], F32)
                      RS = sb(sD, "RS", [128, CK], F32)
                      et_i = 0
                      for hm in range(4):
                          for half in range(2):
                              pa, pb = inproj_half(10, hm, half)
                              for n, p_ in enumerate((pa, pb)):
                                  S.op("act", lambda e, n=n, p_=p_, half=half: e.activation(
                                      QM[:, half * 1024 + n * 512:half * 1024 + (n + 1) * 512], p_[:], AF.Copy, scale=128 ** -0.5),
                                      reads=[p_], writes=[QM])
                          for qt in range(4):
                              po, psm = PSB[0 + 2 * (qt % 2)], PSB[1 + 2 * (qt % 2)]
                              for mb in range(2):
                                  pss = PSB[4 + (et_i % 3)]
                                  et = Et[et_i % 3]
                                  et_i += 1
                                  S.op("pe", lambda e, mb=mb, hm=hm, qt=qt, pss=pss: e.matmul(
                                      pss[:], MK[:, hm, mb * 128:(mb + 1) * 128], QM[:, qt * 512:(qt + 1) * 512], start=True, stop=True),
                                      reads=[MK, QM], writes=[pss])
                                  S.op("act", lambda e, et=et, pss=pss: e.activation(et[:], pss[:], AF.Exp), reads=[pss], writes=[et])
                                  S.op("pe", lambda e, mb=mb, hm=hm, et=et, po=po: e.matmul(
                                      po[:], MV[:, mb, hm * 128:(hm + 1) * 128], et[:], start=(mb == 0), stop=(mb == 1)),
                                      reads=[MV, et], writes=[po], inc=False)
                                  S.op("pe", lambda e, mb=mb, et=et, psm=psm: e.matmul(
                                      psm[:], ones_b[:], et[:], start=(mb == 0), stop=(mb == 1)),
                                      reads=[ones_b, et], writes=[psm, po], inc=True)
                              S.op("dve", lambda e, qt=qt, psm=psm: e.reciprocal(RS[:, qt * 512:(qt + 1) * 512], psm[:]),
                                   reads=[psm], writes=[RS])
                              S.op("dve", lambda e, qt=qt, po=po: e.tensor_tensor(
                                  out=YM[:, qt * 512:(qt + 1) * 512], in0=po[:], in1=RS[:, qt * 512:(qt + 1) * 512], op=ALU.mult),
                                  reads=[po, RS], writes=[YM])
                          for half in range(2):
                              pa, pb = inproj_half(17, hm, half)
                              sl = sil[sil_i[0] % 2]
                              sil_i[0] += 1
                              evac_pair("act", pa, pb, sl, lambda e, o, i: e.activation(o, i, AF.Silu))
                              s_ = next_stg()
                              S.op("dve", lambda e, half=half, sl=sl, s_=s_: e.tensor_tensor(
                                  out=s_[:], in0=YM[:, half * 1024:(half + 1) * 1024], in1=sl[:], op=ALU.mult),
                                  reads=[YM, sl], writes=[s_])
                              S.dma("sp", G_d[24 + hm, :, half * 1024:(half + 1) * 1024], s_[:], s_, reads=[s_])
                      S.barrier()
                chk("mem")
                pre(11)
                pre(12)
                for h in range(8):
                    for half in range(2):
                        if light:
                            continue
                        pa, pb = inproj_half(11 + h // 4, h % 4, half)
                        s_ = next_stg()
                        evac_pair("act", pa, pb, s_, lambda e, o, i: e.activation(o, i, AF.Silu))
                        S.dma("sp", SGA_d[h, :, half * 1024:(half + 1) * 1024], s_[:], s_, reads=[s_])
                S.barrier()
                S.retire([H] + Wt + stg)

            chk("p1")
            if light:
                continue
            with contextlib.ExitStack() as p2:
                nkc = nchunk if heavy else ck + 1
                MSK = None
                if heavy:
                    PT = [sb(p2, "PT%d" % i, [128, 1024], BF16) for i in range(6)]
                    MSK = [[sb(p2, "MSK%d_%d" % (k_, i), [128, 1024], BF16) for i in range(6)] for k_ in range(nchunk)]
                    S.op("dve", lambda e: e.memset(PT[0][:], 1.0), writes=[PT[0]])
                    S.op("dve", lambda e: e.memset(PT[5][:], 0.0), writes=[PT[5]])
                    for r_ in range(4):
                        pt = PT[1 + r_]
                        S.op("dve", lambda e, pt=pt: e.memset(pt[:], 1.0), writes=[pt])
                        for c_ in range(2):
                            if r_ > 0:
                                S.op("dve", lambda e, pt=pt, c_=c_, r_=r_: e.memset(pt[:, c_ * 512:c_ * 512 + r_ * 128], 0.0),
                                     reads=[pt], writes=[pt])
                            S.op("dve", lambda e, pt=pt, c_=c_, r_=r_: e.tensor_copy(
                                pt[:, c_ * 512 + r_ * 128:c_ * 512 + (r_ + 1) * 128], tri[:]), reads=[pt, tri], writes=[pt])
                    for k_ in range(nchunk):
                        for i_ in range(6):
                            S.op("dve", lambda e, k_=k_, i_=i_: e.tensor_scalar(
                                out=MSK[k_][i_][:], in0=PT[i_][:], scalar1=cst[:, O_WS + k_:O_WS + k_ + 1],
                                scalar2=cst[:, O_AL + k_:O_AL + k_ + 1], op0=ALU.mult, op1=ALU.add),
                                reads=[PT[i_], cst], writes=[MSK[k_][i_]])
                KTb = [sb(p2, "KTb%d" % i, [128, nkc * CK], BF16) for i in range(2)]
                Vb = [sb(p2, "Vb%d" % i, [128, nkc * 16, 128], BF16) for i in range(2)]
                Qb = [sb(p2, "Qb%d" % i, [128, CK], BF16) for i in range(2)]
                SGb = [sb(p2, "SGb%d" % i, [128, CK], BF16) for i in range(2)]
                Et = [sb(p2, "Et%d" % i, [128, 1024], BF16) for i in range(3)]
                OS = [sb(p2, "OS%d" % i, [128, 512], F32) for i in range(4)]
                ta = [sb(p2, "ta%d" % i, [128, 512], F32) for i in range(4)]
                gst = [sb(p2, "gst%d" % i, [128, 512], BF16) for i in range(2)]

                def load_head(h):
                    i = h % 2
                    S.dma("sp", Qb[i][:], QT_d[h], Qb[i], writes=[Qb[i]])
                    S.dma("sp", SGb[i][:], SGA_d[h], SGb[i], writes=[SGb[i]])
                    S.dma("sp", KTb[i][:], KT_d[l, h, :, 0:nkc * CK], KTb[i], writes=[KTb[i]])
                    S.dma("sp", Vb[i][:], V_d[l, h, 0:nkc * CK, :].rearrange("(kb p) e -> p kb e", p=128), Vb[i], writes=[Vb[i]])

                load_head(0)
                et_i = 0
                g_i = 0
                accb = (PSB[0], PSB[1], PSB[2], PSB[3])
                for h in range(8):
                    if h + 1 < 8:
                        load_head(h + 1)
                    i = h % 2
                    Q, KT_, V_, SG = Qb[i], KTb[i], Vb[i], SGb[i]
                    for qt in range(4):
                        nkb = ck * 16 + qt * 4 + 4
                        blocks = []
                        if heavy:
                            for kb in range(nchunk * 16):
                                rel = (kb % 16) - 4 * qt
                                ty = 0 if rel < 0 else (5 if rel > 3 else 1 + rel)
                                blocks.append((kb, 0, False, MSK[kb // 16][ty]))
                        else:
                            for kb in range(nkb):
                                sub = kb - (ck * 16 + qt * 4)
                                n0 = sub * 128 if sub > 0 else 0
                                blocks.append((kb, n0, sub >= 0, None))
                        nb = len(blocks)

                        def emit_s(j):
                            kb, n0, dg, mk = blocks[j]
                            for c in range(2):
                                pss = PSB[4 + 2 * (j % 2) + c]
                                S.op("pe", lambda e, kb=kb, n0=n0, pss=pss, c=c: e.matmul(
                                    pss[:, n0:512], KT_[c * 64:(c + 1) * 64, kb * 128:(kb + 1) * 128],
                                    Q[c * 64:(c + 1) * 64, qt * 512 + n0:(qt + 1) * 512], start=True, stop=True),
                                    reads=[KT_, Q], writes=[pss])

                        emit_s(0)
                        if nb > 1:
                            emit_s(1)
                        for j in range(nb):
                            kb, n0, dg, mk = blocks[j]
                            pa_, pb_ = PSB[4 + 2 * (j % 2)], PSB[5 + 2 * (j % 2)]
                            et = Et[et_i % 3]
                            et_i += 1
                            if n0 == 0:
                                pair_t = PSP[2 + (j % 2)]
                                S.op("act", lambda e, et=et, pair_t=pair_t: e.activation(et[:], pair_t[:], AF.Exp),
                                     reads=[pa_, pb_], writes=[et])
                            else:
                                for c, pss in enumerate((pa_, pb_)):
                                    S.op("act", lambda e, et=et, pss=pss, n0=n0, c=c: e.activation(
                                        et[:, c * 512 + n0:(c + 1) * 512], pss[:, n0:512], AF.Exp), reads=[pss], writes=[et])
                            if mk is not None:
                                S.op("dve", lambda e, et=et, mk=mk: e.tensor_tensor(out=et[:], in0=et[:], in1=mk[:], op=ALU.mult),
                                     reads=[et, mk], writes=[et])
                            if dg:
                                for c in range(2):
                                    S.op("dve", lambda e, et=et, n0=n0, c=c: e.tensor_tensor(
                                        out=et[:, c * 512 + n0:c * 512 + n0 + 128], in0=et[:, c * 512 + n0:c * 512 + n0 + 128],
                                        in1=tri[:], op=ALU.mult), reads=[et, tri], writes=[et])
                            if j + 2 < nb:
                                emit_s(j + 2)
                            last = (j == nb - 1)
                            for c in range(2):
                                po, psm = accb[2 * c], accb[2 * c + 1]
                                S.op("pe", lambda e, kb=kb, n0=n0, et=et, po=po, j=j, last=last, c=c: e.matmul(
                                    po[:, n0:512], V_[:, kb, :], et[:, c * 512 + n0:(c + 1) * 512], start=(j == 0), stop=last),
                                    reads=[V_, et], writes=[po], inc=False)
                                S.op("pe", lambda e, n0=n0, et=et, psm=psm, j=j, last=last, c=c: e.matmul(
                                    psm[:, n0:512], ones_b[:], et[:, c * 512 + n0:(c + 1) * 512], start=(j == 0), stop=last),
                                    reads=[ones_b, et], writes=[psm, po], inc=True)
                        for c in range(2):
                            po, psm = accb[2 * c], accb[2 * c + 1]
                            S.op("act", lambda e, c=c, po=po: e.copy(OS[2 * c][:], po[:]), reads=[po], writes=[OS[2 * c]])
                            S.op("dve", lambda e, c=c, psm=psm: e.reciprocal(OS[2 * c + 1][:], psm[:]), reads=[psm],
                                 writes=[OS[2 * c + 1]])
                        t0, t1, t2, t3 = ta
                        S.op("dve", lambda e: e.tensor_tensor(out=t0[:], in0=OS[0][:], in1=OS[1][:], op=ALU.mult),
                             reads=[OS[0], OS[1]], writes=[t0])
                        S.op("dve", lambda e: e.tensor_tensor(out=t1[:], in0=OS[2][:], in1=OS[3][:], op=ALU.mult),
                             reads=[OS[2], OS[3]], writes=[t1])
                        S.op("dve", lambda e: e.scalar_tensor_tensor(out=t2[:], in0=t1[:], scalar=NEGLAM[:, 0:1], in1=t0[:],
                                                                     op0=ALU.mult, op1=ALU.add),
                             reads=[t1, t0, NEGLAM], writes=[t2])
                        S.op("act", lambda e: e.activation(t3[:], t2[:], AF.Square), reads=[t2], writes=[t3])
                        pq = PSB[7]
                        S.op("pe", lambda e: e.matmul(pq[:], ones_f[:], t3[:], start=True, stop=True),
                             reads=[ones_f, t3], writes=[pq])
                        S.op("act", lambda e: e.activation(t0[:], pq[:], AF.Sqrt, bias=epsb[:, 0:1], scale=1.0 / 128),
                             reads=[pq, epsb], writes=[t0])
                        S.op("dve", lambda e: e.reciprocal(t1[:], t0[:]), reads=[t0], writes=[t1])
                        S.op("dve", lambda e: e.tensor_tensor(out=t2[:], in0=t2[:], in1=t1[:], op=ALU.mult),
                             reads=[t2, t1], writes=[t2])
                        gs = gst[g_i % 2]
                        g_i += 1
                        S.op("dve", lambda e, gs=gs, qt=qt: e.scalar_tensor_tensor(
                            out=gs[:], in0=t2[:], scalar=SGL[:, 0:1], in1=SG[:, qt * 512:(qt + 1) * 512],
                            op0=ALU.mult, op1=ALU.mult), reads=[t2, SGL, SG], writes=[gs])
                        S.dma("sp", G_d[h, :, qt * 512:(qt + 1) * 512], gs[:], gs, reads=[gs])
                S.barrier()
                S.retire(KTb + Vb + Qb + SGb + gst)

            chk("p2")
            with contextlib.ExitStack() as p3:
                Hh = sb(p3, "Hh", [128, 16, 1024], BF16)
                Gh = sb(p3, "Gh", [128, 28, 1024], BF16)
                NB3 = 3
                Wb = [sb(p3, "Wb%d" % i, [128, 8, 128], BF16) for i in range(NB3)]
                Wm = [sb(p3, "Wm%d" % i, [128, 16, 128], BF16) for i in range(NB3)]
                sg = [sb(p3, "sg%d" % i, [128, 1024], F32) for i in range(2)]
                tm = [sb(p3, "tm%d" % i, [128, 1024], F32) for i in range(2)]
                acc = sb(p3, "acc", [128, 1024], F32)
                mst = [sb(p3, "mst%d" % i, [128, 1024], BF16) for i in range(2)]
                KOFF = (0, 8, 16, 24, 28)
                steps = [(n_, bi) for n_ in range(16) for bi in range(4)]

                def load_w3(si):
                    n_, bi = steps[si]
                    i = si % NB3
                    nk = KOFF[bi + 1] - KOFF[bi]
                    S.dma("pool", Wb[i][:, 0:nk, :],
                          w_branch[l][KOFF[bi] * 128:KOFF[bi + 1] * 128, :].rearrange("(k p) n -> p k n", p=128)[:, :, n_ * 128:(n_ + 1) * 128],
                          Wb[i], writes=[Wb[i]])
                    cb_ = 9216 + bi * 2048 + n_ * 128
                    S.dma("pool", Wm[i][:], w_in[l].rearrange("(c p) n -> p c n", p=128)[:, :, cb_:cb_ + 128],
                          Wm[i], writes=[Wm[i]])

                it3 = 0
                for half in range(2):
                    S.dma("sp", Hh[:], Hs_d.rearrange("c p n -> p c n")[:, :, half * 1024:(half + 1) * 1024], Hh, writes=[Hh])
                    S.dma("sp", Gh[:], G_d.rearrange("c p n -> p c n")[:, :, half * 1024:(half + 1) * 1024], Gh, writes=[Gh])
                    load_w3(0)
                    load_w3(1)
                    for si, (n_, bi) in enumerate(steps):
                        if si + 2 < len(steps):
                            load_w3(si + 2)
                        i = si % NB3
                        if True:
                            za, zb = PSB[(it3 % 2) * 2], PSB[(it3 % 2) * 2 + 1]
                            ma, mb_ = PSB[4 + (it3 % 2) * 2], PSB[5 + (it3 % 2) * 2]
                            it3 += 1
                            ks = list(range(KOFF[bi], KOFF[bi + 1]))
                            for n, p_ in enumerate((ma, mb_)):
                                for c in range(16):
                                    S.op("pe", lambda e, c=c, n=n, p_=p_, i=i: e.matmul(
                                        p_[:], Wm[i][:, c, :], Hh[:, c, n * 512:(n + 1) * 512], start=(c == 0), stop=(c == 15)),
                                        reads=[Wm[i], Hh], writes=[p_], inc=(c == 15))
                            for n, p_ in enumerate((za, zb)):
                                for kk, k in enumerate(ks):
                                    S.op("pe", lambda e, k=k, kk=kk, n=n, p_=p_, i=i, ks=ks: e.matmul(
                                        p_[:], Wb[i][:, kk, :], Gh[:, k, n * 512:(n + 1) * 512], start=(kk == 0),
                                        stop=(kk == len(ks) - 1)), reads=[Wb[i], Gh], writes=[p_], inc=(kk == len(ks) - 1))
                            sgt = sg[bi % 2]
                            for n, p_ in enumerate((ma, mb_)):
                                S.op("act", lambda e, n=n, p_=p_, sgt=sgt: e.activation(sgt[:, n * 512:(n + 1) * 512], p_[:], AF.Sigmoid),
                                     reads=[p_], writes=[sgt])
                            dst = acc if bi == 0 else tm[bi % 2]
                            for n, p_ in enumerate((za, zb)):
                                S.op("dve", lambda e, n=n, p_=p_, sgt=sgt, dst=dst: e.tensor_tensor(
                                    out=dst[:, n * 512:(n + 1) * 512], in0=p_[:], in1=sgt[:, n * 512:(n + 1) * 512], op=ALU.mult),
                                    reads=[p_, sgt], writes=[dst])
                            if bi > 0:
                                if bi < 3:
                                    S.op("dve", lambda e, dst=dst: e.tensor_tensor(out=acc[:], in0=acc[:], in1=dst[:], op=ALU.add),
                                         reads=[acc, dst], writes=[acc])
                                else:
                                    ms = mst[n_ % 2]
                                    S.op("dve", lambda e, dst=dst, ms=ms: e.tensor_tensor(out=ms[:], in0=acc[:], in1=dst[:], op=ALU.add),
                                         reads=[acc, dst], writes=[ms])
                                    S.dma("sp", MG_d[n_, :, half * 1024:(half + 1) * 1024], ms[:], ms, reads=[ms])
                S.barrier()
                S.retire([Hh, Gh] + Wb + Wm + mst)

            chk("p3a")
            with contextlib.ExitStack() as p4:
                MGh = sb(p4, "MGh", [128, 16, 1024], BF16)
                Wo = [sb(p4, "Wo%d" % i, [128, 16, 128], BF16) for i in range(2)]
                xin = [sb(p4, "xin%d" % i, [128, 1024], F32) for i in range(2)]
                xo = [sb(p4, "xo%d" % i, [128, 1024], F32) for i in range(2)]
                it4 = 0
                for half in range(2):
                    S.dma("sp", MGh[:], MG_d.rearrange("c p n -> p c n")[:, :, half * 1024:(half + 1) * 1024], MGh, writes=[MGh])
                    cols = slice(c0 + half * 1024, c0 + (half + 1) * 1024)
                    for n_ in range(16):
                        i = it4 % 2
                        S.dma("pool", Wo[i][:], w_out[l].rearrange("(k p) n -> p k n", p=128)[:, :, n_ * 128:(n_ + 1) * 128],
                              Wo[i], writes=[Wo[i]])
                        S.dma("sp", xin[i][:], (xown_d if heavy else xsrc)[n_ * 128:(n_ + 1) * 128, cols], xin[i], writes=[xin[i]])
                        ya, yb = PSB[(it4 % 4) * 2], PSB[(it4 % 4) * 2 + 1]
                        it4 += 1
                        for n, p_ in enumerate((ya, yb)):
                            for k in range(16):
                                S.op("pe", lambda e, k=k, n=n, p_=p_, i=i: e.matmul(
                                    p_[:], Wo[i][:, k, :], MGh[:, k, n * 512:(n + 1) * 512], start=(k == 0), stop=(k == 15)),
                                    reads=[Wo[i], MGh], writes=[p_], inc=(k == 15))
                            S.op("dve", lambda e, n=n, p_=p_, i=i: e.tensor_tensor(
                                out=xo[i][:, n * 512:(n + 1) * 512], in0=p_[:], in1=xin[i][:, n * 512:(n + 1) * 512], op=ALU.add),
                                reads=[p_, xin[i]], writes=[xo[i]])
                        S.dma("sp", (x2own_d if heavy else xdst)[n_ * 128:(n_ + 1) * 128, cols], xo[i][:], xo[i], reads=[xo[i]])
                S.barrier()
                S.retire([MGh] + Wo + xin + xo)

    except _Stop:
        pass
    with contextlib.ExitStack() as pf:
      if not stop_at:
          xts = [sb(pf, "fx%d" % i, [128, 16, 512], F32) for i in range(2)]
          ots = [sb(pf, "fo%d" % i, [128, 16, 512], F32) for i in range(2)]
          sqs = [sb(pf, "fsq%d" % i, [128, 512], F32) for i in range(3)]
          r1 = sb(pf, "fr1", [128, 512], F32)
          r2 = sb(pf, "fr2", [128, 512], F32)
          fsrc = x2own_d if DEDUP else x2T
          for tt in range((CK if DEDUP else S_) // 512):
              xt = xts[tt % 2]
              ot = ots[tt % 2]
              S.dma("sp", xt[:], fsrc.rearrange("(c p) n -> p c n", p=128)[:, :, tt * 512:(tt + 1) * 512], xt, writes=[xt])
              t1, t2 = r1, r2
              for c in range(16):
                  sq = sqs[c % 3]
                  S.op("act", lambda e, c=c, sq=sq, xt=xt: e.activation(sq[:], xt[:, c, :], AF.Square), reads=[xt], writes=[sq])
                  S.op("pe", lambda e, c=c, sq=sq: e.matmul(PSB[7][:], ones_f[:], sq[:], start=(c == 0), stop=(c == 15)),
                       reads=[ones_f, sq], writes=[PSB[7]])
              S.op("act", lambda e: e.activation(t1[:], PSB[7][:], AF.Sqrt, bias=epsb[:, 0:1], scale=1.0 / D),
                   reads=[PSB[7], epsb], writes=[t1])
              S.op("dve", lambda e: e.reciprocal(t2[:], t1[:]), reads=[t1], writes=[t2])
              for c in range(16):
                  S.op("dve", lambda e, c=c, xt=xt, ot=ot: e.scalar_tensor_tensor(
                      out=ot[:, c, :], in0=xt[:, c, :], scalar=cst[:, O_FG + c:O_FG + c + 1], in1=t2[:],
                      op0=ALU.mult, op1=ALU.mult), reads=[xt, cst, t2], writes=[ot])
              S.dma("sp", outT.rearrange("(c p) n -> p c n", p=128)[:, :, tt * 512:(tt + 1) * 512], ot[:], ot, reads=[ot])
          S.barrier()
    if not stop_at:
        es.close()
    return nc


def host_inputs(nchunk, b, jown, x, mem, norm_g, w_in, lam_vecs, subln_g, pool_w, pool_scale, conv_w, conv_b,
                lru_wa, lru_ba, lru_wx, lru_bx, lru_lambda, mem_norm_g, w_mem_kv, w_branch, w_out, final_g):
    S_ = nchunk * CK
    f = np.float32

    def pc(v, n):
        return np.ascontiguousarray(np.asarray(v, f).reshape(n, 128).T)

    cst = np.zeros((128, NCST), f)
    for l in range(2):
        LB = l * LW
        cst[:, LB + O_NG:LB + O_NG + 16] = pc(norm_g[l], 16)
        cst[:, LB + O_MG:LB + O_MG + 16] = pc(mem_norm_g[l], 16)
        cst[:, LB + O_PS:LB + O_PS + 8] = pc(pool_scale[l], 8)
        for k in range(4):
            cst[:, LB + O_CW + k * 8:LB + O_CW + k * 8 + 8] = pc(conv_w[l][k], 8)
        cst[:, LB + O_CB:LB + O_CB + 8] = pc(conv_b[l], 8)
        cst[:, LB + O_BA:LB + O_BA + 8] = pc(lru_ba[l], 8)
        cst[:, LB + O_BX:LB + O_BX + 8] = pc(lru_bx[l], 8)
        cst[:, LB + O_LL:LB + O_LL + 8] = pc(lru_lambda[l], 8)
        cst[:, LB + O_SG] = np.asarray(subln_g[l], f)
        cst[:, LB + O_LV:LB + O_LV + 256] = np.asarray(lam_vecs[l], f).reshape(1, 256)
    cst[:, O_FG:O_FG + 16] = pc(final_g, 16)
    for g in range(4):
        w = 2 << g
        for t in range(16):
            cst[:, O_IC + g * 16 + t] = 1.0 / min(t + 1, w)
    for k in range(4):
        cst[:, O_WS + k] = 1.0 if k == jown else 0.0
        cst[:, O_AL + k] = 1.0 if k < jown else 0.0
    cst[:, O_IF] = 1.0 if jown == 0 else 0.0
    for g in range(4):
        w = 2 << g
        for t in range(16):
            cst[:, O_IC2 + g * 16 + t] = (1.0 / min(t + 1, w)) if jown == 0 else 1.0 / w
    tri = np.triu(np.ones((128, 128), f))
    return {
        "xT": np.ascontiguousarray(np.asarray(x[b, :S_], f).T),
        "memT": np.ascontiguousarray(np.asarray(mem[b], f).T),
        "cst": cst, "tri": tri,
        "w_in": np.asarray(w_in, f), "pool_w": np.asarray(pool_w, f), "lru_wa": np.asarray(lru_wa, f),
        "lru_wx": np.asarray(lru_wx, f), "w_mem_kv": np.asarray(w_mem_kv, f), "w_branch": np.asarray(w_branch, f),
        "w_out": np.asarray(w_out, f),
    }


def run(nchunk, inputs, n_batch=2):
    nc = build(nchunk)
    maps = []
    per_b = 8 // n_batch
    for c in range(8):
        jown = (c % per_b) * nchunk // per_b
        maps.append(host_inputs(nchunk, min(c // per_b, n_batch - 1), jown, **inputs))
    res = run_bass_kernel_spmd(nc, maps, core_ids=list(range(8)))
    S_ = nchunk * CK
    out = np.zeros((n_batch, S_, D), np.float32)
    if nchunk == 1:
        q = S_ // per_b
        for c in range(8):
            b, j = c // per_b, c % per_b
            out[b, j * q:(j + 1) * q, :] = res.results[c]["outT"][:, j * q:(j + 1) * q].T
    else:
        sub = per_b // nchunk
        q = CK // sub
        for c in range(8):
            b, r = c // per_b, c % per_b
            jown, k = r * nchunk // per_b, r % sub
            out[b, jown * CK + k * q:jown * CK + (k + 1) * q, :] = res.results[c]["outT"][:, k * q:(k + 1) * q].T
    return out


def kernel(**inputs):
    return run(4, inputs)
```

```python
import contextlib
import math
import numpy as np
import concourse.bass as bass
import concourse.mybir as mybir
from concourse.bass_utils import run_bass_kernel_spmd

F32 = mybir.dt.float32
BF16 = mybir.dt.bfloat16
AF = mybir.ActivationFunctionType
ALU = mybir.AluOpType
AX = mybir.AxisListType

D = 2048
NIN = 17408
CK = 2048
MEM = 256
EPS = 1e-6
ENGS = ("pe", "act", "dve", "pool", "sp")

LW = 16 + 16 + 8 + 32 + 8 + 8 + 8 + 8 + 1 + 256
O_NG, O_MG, O_PS, O_CW, O_CB, O_BA, O_BX, O_LL, O_SG, O_LV = 0, 16, 32, 40, 72, 80, 88, 96, 104, 105
O_FG = 2 * LW
O_IC = O_FG + 16
O_WS = O_IC + 64
O_AL = O_WS + 4
O_IF = O_AL + 4
O_IC2 = O_IF + 1
NCST = O_IC2 + 64


class _Stop(Exception):
    pass


class T:
    def __init__(self, t, name=""):
        self.t = t
        self.name = name
        self.w = {}
        self.r = {}
        self.chan = None

    def __getitem__(self, k):
        return self.t[k]


class TP(T):
    def __init__(self, t, off, name=""):
        T.__init__(self, t, name)
        self.off = off

    def __getitem__(self, k):
        if isinstance(k, slice):
            assert k == slice(None)
            return self.t[:, self.off:self.off + 512]
        ps, fs = k
        a = 0 if fs.start is None else fs.start
        b = 512 if fs.stop is None else fs.stop
        return self.t[ps, self.off + a:self.off + b]


class Chan:
    def __init__(self, sem):
        self.sem = sem
        self.count = 0


def _merge(d, s):
    for k, v in s.items():
        if d.get(k, 0) < v:
            d[k] = v


class Sched:
    def __init__(self, nc):
        self.nc = nc
        self.e = dict(pe=nc.tensor, act=nc.scalar, dve=nc.vector, pool=nc.gpsimd, sp=nc.sync)
        self.semobj = {}
        self.cnt = {}
        self.seen = {k: {} for k in ENGS}
        for k in ENGS:
            self.semobj[("c", k)] = nc.alloc_semaphore("cs_" + k)
            self.cnt[k] = 0
        self.chans = []
        self.free_ch = []
        self.pending = {k: False for k in ENGS}

    def _wait(self, eng, deps):
        for key, val in deps.items():
            if eng == "pe" and key == ("c", "pe"):
                continue
            if self.seen[eng].get(key, 0) >= val:
                continue
            self.e[eng].wait_ge(self.semobj[key], val)
            self.seen[eng][key] = val

    def op(self, eng, fn, reads=(), writes=(), inc=True):
        deps = {}
        for b in reads:
            _merge(deps, b.w)
        for b in writes:
            _merge(deps, b.w)
            _merge(deps, b.r)
        self._wait(eng, deps)
        ins = fn(self.e[eng])
        key = ("c", eng)
        val = self.cnt[eng] + 1
        if inc:
            ins.then_inc(self.semobj[key], 1)
            self.cnt[eng] = val
            self.pending[eng] = False
        else:
            self.pending[eng] = True
        for b in reads:
            if b.r.get(key, 0) < val:
                b.r[key] = val
        for b in writes:
            if b.w.get(key, 0) < val:
                b.w[key] = val
            b.r = {}
        return ins

    def get_chan(self, sb):
        if sb.chan is None:
            if self.free_ch:
                sb.chan = self.free_ch.pop()
            else:
                ch = Chan(self.nc.alloc_semaphore("ch%d" % len(self.chans)))
                self.chans.append(ch)
                self.semobj[("d", id(ch))] = ch.sem
                sb.chan = ch
        return sb.chan

    def dma(self, q, out_ap, in_ap, sb, reads=(), writes=()):
        deps = {}
        for b in reads:
            _merge(deps, b.w)
        for b in writes:
            _merge(deps, b.w)
            _merge(deps, b.r)
        self._wait(q, deps)
        ch = self.get_chan(sb)
        ins = self.e[q].dma_start(out=out_ap, in_=in_ap)
        ch.count += 16
        ins.then_inc(ch.sem, 16)
        key = ("d", id(ch))
        val = ch.count
        for b in reads:
            if b.r.get(key, 0) < val:
                b.r[key] = val
        for b in writes:
            if b.w.get(key, 0) < val:
                b.w[key] = val
            b.r = {}
        return ins

    def barrier(self, engs=ENGS):
        assert not any(self.pending.values()), self.pending
        marks = {}
        for k in ENGS:
            if self.cnt[k] > 0:
                marks[("c", k)] = self.cnt[k]
        for ch in self.chans:
            if ch.count > 0:
                marks[("d", id(ch))] = ch.count
        for eng in engs:
            deps = dict(marks)
            self._wait(eng, deps)

    def retire(self, bufs):
        for b in bufs:
            if b.chan is not None:
                self.free_ch.append(b.chan)
                b.chan = None


def build(nchunk, lam_inits=(0.2, 0.8 - 0.6 * math.exp(-0.3)), dbg=None):
    S_ = nchunk * CK
    nc = bass.Bass("TRN2", target_bir_lowering=False)
    xT = nc.dram_tensor("xT", [D, S_], F32, kind="ExternalInput").ap()
    memT = nc.dram_tensor("memT", [D, MEM], F32, kind="ExternalInput").ap()
    cst_d = nc.dram_tensor("cst", [128, NCST], F32, kind="ExternalInput").ap()
    tri_d = nc.dram_tensor("tri", [128, 128], F32, kind="ExternalInput").ap()
    w_in = nc.dram_tensor("w_in", [2, D, NIN], F32, kind="ExternalInput").ap()
    pool_w = nc.dram_tensor("pool_w", [2, 4, 256, 256], F32, kind="ExternalInput").ap()
    lru_wa = nc.dram_tensor("lru_wa", [2, 8, 128, 128], F32, kind="ExternalInput").ap()
    lru_wx = nc.dram_tensor("lru_wx", [2, 8, 128, 128], F32, kind="ExternalInput").ap()
    w_mem_kv = nc.dram_tensor("w_mem_kv", [2, D, 1024], F32, kind="ExternalInput").ap()
    w_branch = nc.dram_tensor("w_branch", [2, 3584, D], F32, kind="ExternalInput").ap()
    w_out = nc.dram_tensor("w_out", [2, D, D], F32, kind="ExternalInput").ap()
    DEDUP = nchunk > 1
    outT = nc.dram_tensor("outT", [D, CK if DEDUP else S_], F32, kind="ExternalOutput").ap()
    xown_d = nc.dram_tensor("xown_d", [D, CK], F32, kind="Internal").ap()
    x2own_d = nc.dram_tensor("x2own_d", [D, CK], F32, kind="Internal").ap()
    x1T = nc.dram_tensor("x1T", [D, S_], F32, kind="Internal").ap()
    x2T = nc.dram_tensor("x2T", [D, S_], F32, kind="Internal").ap()
    KT_d = nc.dram_tensor("KT_d", [2, 8, 128, S_], BF16, kind="Internal").ap()
    V_d = nc.dram_tensor("V_d", [2, 8, S_, 128], BF16, kind="Internal").ap()
    QT_d = nc.dram_tensor("QT_d", [8, 128, CK], BF16, kind="Internal").ap()
    SGA_d = nc.dram_tensor("SGA_d", [8, 128, CK], BF16, kind="Internal").ap()
    G_d = nc.dram_tensor("G_d", [28, 128, CK], BF16, kind="Internal").ap()
    Hs_d = nc.dram_tensor("Hs_d", [16, 128, CK], BF16, kind="Internal").ap()
    MG_d = nc.dram_tensor("MG_d", [16, 128, CK], BF16, kind="Internal").ap()

    S = Sched(nc)
    es = contextlib.ExitStack()
    import os
    stop_at = os.environ.get("KSTOP", "")

    def chk(name):
        if name == stop_at:
            S.barrier()
            raise _Stop()

    uid = [0]

    def sb(stack, name, shape, dt):
        uid[0] += 1
        name = "%s_u%d" % (name, uid[0])
        return T(stack.enter_context(nc.sbuf_tensor(name, shape, dt)), name)

    cst = sb(es, "cstb", [128, NCST], F32)
    ones_f = sb(es, "ones_f", [128, 128], F32)
    ones_b = sb(es, "ones_b", [128, 128], BF16)
    tri = sb(es, "trib", [128, 128], BF16)
    epsb = sb(es, "epsb", [128, 1], F32)
    PH = sb(es, "PH", [128, 8, 16], F32)
    LH = sb(es, "LH", [128, 8, 4], F32)
    CAR = sb(es, "CAR", [128, 8], F32)
    PHo = sb(es, "PHo", [128, 8, 16], F32)
    LHo = sb(es, "LHo", [128, 8, 4], F32)
    CARo = sb(es, "CARo", [128, 8], F32)
    CL = sb(es, "CL", [128, 8], F32)
    CL2 = sb(es, "CL2", [128, 8], F32)
    NEGLAM = sb(es, "NEGLAM", [128, 1], F32)
    SGL = sb(es, "SGL", [128, 1], F32)
    MK = sb(es, "MK", [128, 4, MEM], BF16)
    MV = sb(es, "MV", [128, 2, 512], BF16)
    smallt = sb(es, "smallt", [128, 8, 64], F32)
    PSP = [es.enter_context(nc.psum_tensor("psp%d" % i, [128, 1024], F32)) for i in range(4)]
    PSB = [TP(PSP[i // 2], (i % 2) * 512, "ps%d" % i) for i in range(8)]

    S.dma("sp", cst[:], cst_d, cst, writes=[cst])
    S.dma("pool", tri[:], tri_d, tri, writes=[tri])
    S.op("dve", lambda e: e.memset(ones_f[:], 1.0), writes=[ones_f])
    S.op("dve", lambda e: e.memset(ones_b[:], 1.0), writes=[ones_b])
    S.op("dve", lambda e: e.memset(epsb[:], EPS), writes=[epsb])

    def rms_tile(stack_bufs, xt, ncols, gcol, out_fn, sqs, rstd_bufs, psb):
        t1, t2 = rstd_bufs
        for c in range(16):
            sq = sqs[c % len(sqs)]
            S.op("act", lambda e, c=c, sq=sq: e.activation(sq[:, 0:ncols], xt[:, c, 0:ncols], AF.Square),
                 reads=[xt], writes=[sq])
            S.op("pe", lambda e, c=c, sq=sq: e.matmul(psb[:, 0:ncols], ones_f[:], sq[:, 0:ncols],
                                                     start=(c == 0), stop=(c == 15)),
                 reads=[ones_f, sq], writes=[psb], inc=True)
        S.op("act", lambda e: e.activation(t1[:, 0:ncols], psb[:, 0:ncols], AF.Sqrt, bias=epsb[:, 0:1], scale=1.0 / D),
             reads=[psb, epsb], writes=[t1])
        S.op("dve", lambda e: e.reciprocal(t2[:, 0:ncols], t1[:, 0:ncols]), reads=[t1], writes=[t2])
        for c in range(16):
            oap, ot = out_fn(c)
            S.op("dve", lambda e, c=c, oap=oap: e.scalar_tensor_tensor(
                out=oap, in0=xt[:, c, 0:ncols], scalar=cst[:, gcol + c:gcol + c + 1], in1=t2[:, 0:ncols],
                op0=ALU.mult, op1=ALU.mult), reads=[xt, cst, t2], writes=[ot])
        return t2

    try:
      for l in range(2):
        LB = l * LW
        lam_init = lam_inits[l]
        xsrc = xT if l == 0 else x1T
        xdst = x1T if l == 0 else x2T
        with contextlib.ExitStack() as ps_:
            S.barrier()
            HM = sb(ps_, "HM", [128, 16, MEM], BF16)
            mt_ = sb(ps_, "mt_", [128, 16, MEM], F32)
            WM = sb(ps_, "WM", [128, 16, 1024], BF16)
            sqs = [sb(ps_, "sqm%d" % i, [128, 512], F32) for i in range(3)]
            r1 = sb(ps_, "r1m", [128, 512], F32)
            r2 = sb(ps_, "r2m", [128, 512], F32)
            S.dma("sp", mt_[:], memT.rearrange("(c p) n -> p c n", p=128), mt_, writes=[mt_])
            S.dma("pool", WM[:], w_mem_kv[l].rearrange("(c p) n -> p c n", p=128), WM, writes=[WM])
            rms_tile(None, mt_, MEM, LB + O_MG, lambda c: (HM[:, c, :], HM), sqs, (r1, r2), PSB[7])
            for hm in range(4):
                pb = PSB[hm % 4]
                for c in range(16):
                    S.op("pe", lambda e, c=c, hm=hm, pb=pb: e.matmul(pb[:, 0:MEM], WM[:, c, hm * 128:(hm + 1) * 128],
                                                                    HM[:, c, :], start=(c == 0), stop=(c == 15)),
                         reads=[WM, HM], writes=[pb], inc=(c == 15))
                S.op("act", lambda e, hm=hm, pb=pb: e.copy(MK[:, hm, :], pb[:, 0:MEM]), reads=[pb], writes=[MK])
            for mb in range(2):
                pb = PSB[4 + mb]
                for c in range(16):
                    S.op("pe", lambda e, c=c, mb=mb, pb=pb: e.matmul(pb[:], HM[:, c, mb * 128:(mb + 1) * 128],
                                                                    WM[:, c, 512:1024], start=(c == 0), stop=(c == 15)),
                         reads=[WM, HM], writes=[pb], inc=(c == 15))
                S.op("act", lambda e, mb=mb, pb=pb: e.copy(MV[:, mb, :], pb[:]), reads=[pb], writes=[MV])
            st = smallt
            S.op("act", lambda e: e.activation(st[:, 0, 0:8], cst[:, LB + O_LL:LB + O_LL + 8], AF.Exp, scale=-1.0),
                 reads=[cst], writes=[st])
            S.op("dve", lambda e: e.tensor_scalar_add(st[:, 0, 0:8], st[:, 0, 0:8], 1.0), reads=[st], writes=[st])
            S.op("act", lambda e: e.activation(st[:, 1, 0:8], st[:, 0, 0:8], AF.Ln), reads=[st], writes=[st])
            S.op("dve", lambda e: e.tensor_scalar_mul(CL[:], st[:, 1, 0:8], -8.0), reads=[st], writes=[CL])
            S.op("dve", lambda e: e.tensor_scalar_mul(CL2[:], st[:, 1, 0:8], -16.0), reads=[st], writes=[CL2])
            lv = LB + O_LV
            S.op("dve", lambda e: e.tensor_tensor(out=st[:, 2, 0:64], in0=cst[:, lv:lv + 64], in1=cst[:, lv + 64:lv + 128],
                                                  op=ALU.mult), reads=[cst, st], writes=[st])
            S.op("dve", lambda e: e.tensor_tensor(out=st[:, 3, 0:64], in0=cst[:, lv + 128:lv + 192],
                                                  in1=cst[:, lv + 192:lv + 256], op=ALU.mult), reads=[cst, st], writes=[st])
            S.op("dve", lambda e: e.reduce_sum(out=st[:, 4, 0:1], in_=st[:, 2, 0:64], axis=AX.X), reads=[st], writes=[st])
            S.op("dve", lambda e: e.reduce_sum(out=st[:, 4, 1:2], in_=st[:, 3, 0:64], axis=AX.X), reads=[st], writes=[st])
            S.op("act", lambda e: e.activation(st[:, 5, 0:2], st[:, 4, 0:2], AF.Exp), reads=[st], writes=[st])
            S.op("dve", lambda e: e.tensor_tensor(out=st[:, 6, 0:1], in0=st[:, 5, 1:2], in1=st[:, 5, 0:1], op=ALU.subtract),
                 reads=[st], writes=[st])
            S.op("dve", lambda e: e.tensor_scalar_add(NEGLAM[:], st[:, 6, 0:1], -lam_init), reads=[st], writes=[NEGLAM])
            S.op("dve", lambda e: e.tensor_scalar_mul(SGL[:], cst[:, LB + O_SG:LB + O_SG + 1], 1.0 - lam_init),
                 reads=[cst], writes=[SGL])
            S.op("dve", lambda e: e.memset(PH[:], 0.0), writes=[PH])
            S.op("dve", lambda e: e.memset(LH[:], 0.0), writes=[LH])
            S.op("dve", lambda e: e.memset(CAR[:], 0.0), writes=[CAR])
            S.barrier()
            S.retire([HM, mt_, WM])
        chk("setup")

        dedup = DEDUP and l == 1
        passes = [("full", ck) for ck in range(nchunk)]
        if dedup:
            passes = [("light", ck) for ck in range(nchunk)] + [("heavy", 0)]
            S.op("dve", lambda e: e.memset(PHo[:], 0.0), writes=[PHo])
            S.op("dve", lambda e: e.memset(LHo[:], 0.0), writes=[LHo])
            S.op("dve", lambda e: e.memset(CARo[:], 0.0), writes=[CARo])
        for mode, ck in passes:
            light = mode == "light"
            heavy = mode == "heavy"
            c0 = ck * CK
            with contextlib.ExitStack() as p1:
                H = sb(p1, "H", [128, 16, CK], BF16)
                with contextlib.ExitStack() as p0:
                    xts = [sb(p0, "xt%d" % i, [128, 16, 512], F32) for i in range(2)]
                    sqs = [sb(p0, "sq%d" % i, [128, 512], F32) for i in range(3)]
                    r1 = sb(p0, "r1", [128, 512], F32)
                    r2 = sb(p0, "r2", [128, 512], F32)
                    if heavy:
                        xld = sb(p0, "xld", [128, 16, 512], F32)
                    for tt in range(4):
                        xt = xts[tt % 2]
                        if heavy:
                            for k_ in range(nchunk):
                                S.dma("sp", xld[:], x1T.rearrange("(c p) n -> p c n", p=128)[:, :, k_ * CK + tt * 512:k_ * CK + (tt + 1) * 512],
                                      xld, writes=[xld])
                                if k_ == 0:
                                    S.op("dve", lambda e, xt=xt: e.tensor_scalar(
                                        out=xt[:], in0=xld[:], scalar1=cst[:, O_WS:O_WS + 1], scalar2=None, op0=ALU.mult),
                                        reads=[xld, cst], writes=[xt])
                                else:
                                    S.op("dve", lambda e, xt=xt, k_=k_: e.scalar_tensor_tensor(
                                        out=xt[:], in0=xld[:], scalar=cst[:, O_WS + k_:O_WS + k_ + 1], in1=xt[:],
                                        op0=ALU.mult, op1=ALU.add), reads=[xld, cst, xt], writes=[xt])
                            S.dma("sp", xown_d.rearrange("(c p) n -> p c n", p=128)[:, :, tt * 512:(tt + 1) * 512], xt[:], xt, reads=[xt])
                        else:
                            S.dma("sp", xt[:], xsrc.rearrange("(c p) n -> p c n", p=128)[:, :, c0 + tt * 512:c0 + (tt + 1) * 512],
                                  xt, writes=[xt])
                        rms_tile(None, xt, 512, LB + O_NG, lambda c, tt=tt: (H[:, c, tt * 512:(tt + 1) * 512], H),
                                 sqs, (r1, r2), PSB[7])
                    if not light:
                        S.dma("sp", Hs_d.rearrange("c p n -> p c n"), H[:], H, reads=[H])
                    S.barrier()
                    S.retire(xts)
                chk("p0")
                Wt = [sb(p1, "Wt%d" % i, [128, 16, 512], BF16) for i in range(3)]
                wstate = {"next": 0, "map": {}}
                wseq = list(range(18))

                def wload(g):
                    if g in wstate["map"]:
                        return wstate["map"][g]
                    slot = wstate["next"] % 3
                    wstate["next"] += 1
                    for k in [k for k, v in wstate["map"].items() if v is Wt[slot]]:
                        del wstate["map"][k]
                    S.dma("pool", Wt[slot][:], w_in[l].rearrange("(c p) n -> p c n", p=128)[:, :, g * 512:(g + 1) * 512],
                          Wt[slot], writes=[Wt[slot]])
                    wstate["map"][g] = Wt[slot]
                    return Wt[slot]

                def pre(g):
                    need = set(range(18))
                    if light:
                        need = {2, 3, 4, 5, 6, 7, 8, 9}
                    if heavy:
                        need = set(range(18)) - {2, 3, 4, 5}
                    if g in need:
                        wload(g)

                pair_i = [0]

                def next_pair():
                    k = pair_i[0] % 4
                    pair_i[0] += 1
                    return PSB[2 * k], PSB[2 * k + 1]

                def inproj_half(g, mloc, half):
                    W = wload(g)
                    if g + 1 < 18:
                        pass
                    pa, pb = next_pair()
                    for n, p_ in enumerate((pa, pb)):
                        for c in range(16):
                            S.op("pe", lambda e, c=c, n=n, p_=p_, W=W: e.matmul(
                                p_[:], W[:, c, mloc * 128:(mloc + 1) * 128],
                                H[:, c, half * 1024 + n * 512: half * 1024 + (n + 1) * 512],
                                start=(c == 0), stop=(c == 15)), reads=[W, H], writes=[p_], inc=(c == 15))
                    return pa, pb

                stg = [sb(p1, "stg%d" % i, [128, 1024], BF16) for i in range(4)]
                stg_i = [0]

                def next_stg():
                    s_ = stg[stg_i[0] % 4]
                    stg_i[0] += 1
                    return s_

                def evac_pair(eng, pa, pb, dst, fn):
                    for n, p_ in enumerate((pa, pb)):
                        S.op(eng, lambda e, n=n, p_=p_: fn(e, dst[:, n * 512:(n + 1) * 512], p_[:]),
                             reads=[p_], writes=[dst])

                pre(0)
                pre(1)
                for sec, (dst_d, scale) in enumerate(((QT_d, 0.125), (None, 1.0))):
                    if (sec == 0 and light) or (sec == 1 and heavy):
                        continue
                    for h in range(8):
                        g, mloc = sec * 2 + h // 4, h % 4
                        if mloc == 0 and g + 2 < 18:
                            pass
                        for half in range(2):
                            pa, pb = inproj_half(g, mloc, half)
                            s_ = next_stg()
                            evac_pair("act", pa, pb, s_, lambda e, o, i, scale=scale: e.activation(o, i, AF.Copy, scale=scale))
                            if sec == 0:
                                S.dma("sp", QT_d[h, :, half * 1024:(half + 1) * 1024], s_[:], s_, reads=[s_])
                            else:
                                S.dma("sp", KT_d[l, h, :, c0 + half * 1024:c0 + (half + 1) * 1024], s_[:], s_, reads=[s_])
                    if sec == 0:
                        pre(2)
                chk("qk")
                pre(3)
                with contextlib.ExitStack() as sA:
                    vst = [sb(sA, "vst%d" % i, [128, 512], BF16) for i in range(4)]
                    vi = 0
                    for vg in range(2):
                        if heavy:
                            continue
                        W = wload(4 + vg)
                        pre(5 + vg)
                        for tt in range(16):
                            pb = PSB[vi % 8]
                            for c in range(16):
                                S.op("pe", lambda e, c=c, tt=tt, pb=pb, W=W: e.matmul(
                                    pb[:], H[:, c, tt * 128:(tt + 1) * 128], W[:, c, :], start=(c == 0), stop=(c == 15)),
                                    reads=[W, H], writes=[pb], inc=(c == 15))
                            vs = vst[vi % 4]
                            if vi % 2 == 0:
                                S.op("act", lambda e, vs=vs, pb=pb: e.copy(vs[:], pb[:]), reads=[pb], writes=[vs])
                            else:
                                S.op("dve", lambda e, vs=vs, pb=pb: e.tensor_copy(vs[:], pb[:]), reads=[pb], writes=[vs])
                            S.dma("sp", V_d[l, vg * 4:(vg + 1) * 4, c0 + tt * 128:c0 + (tt + 1) * 128, :].rearrange("h t e -> t h e"),
                                  vs[:].rearrange("t (h e) -> t h e", h=4), vs, reads=[vs])
                            vi += 1
                    S.barrier()
                    S.retire(vst)
                chk("v")
                sil = [sb(p1, "sil%d" % i, [128, 1024], F32) for i in range(2)]
                sil_i = [0]
                if light:
                    S.op("dve", lambda e: e.scalar_tensor_tensor(
                        out=PHo[:], in0=PH[:], scalar=cst[:, O_WS + ck:O_WS + ck + 1], in1=PHo[:], op0=ALU.mult, op1=ALU.add),
                        reads=[PH, PHo, cst], writes=[PHo])
                    for mt in range(8):
                        W = wload(6 + mt // 4)
                        pb = PSB[mt % 8]
                        for c in range(16):
                            S.op("pe", lambda e, c=c, mt=mt, pb=pb, W=W: e.matmul(
                                pb[:, 0:16], W[:, c, (mt % 4) * 128:(mt % 4 + 1) * 128], H[:, c, CK - 16:CK],
                                start=(c == 0), stop=(c == 15)), reads=[W, H], writes=[pb], inc=(c == 15))
                        S.op("act", lambda e, mt=mt, pb=pb: e.copy(PH[:, mt, :], pb[:, 0:16]), reads=[pb], writes=[PH])
                    S.barrier()
                if heavy:
                    S.op("dve", lambda e: e.tensor_copy(PH[:], PHo[:]), reads=[PHo], writes=[PH])
                    S.op("dve", lambda e: e.tensor_copy(LH[:], LHo[:]), reads=[LHo], writes=[LH])
                    S.op("dve", lambda e: e.tensor_copy(CAR[:], CARo[:]), reads=[CARo], writes=[CAR])
                with contextlib.ExitStack() as sB:
                  if not light:
                      WB = [sb(sB, "WBb%d" % i, [128, 16 + CK], F32) for i in range(4)]
                      Dg = sb(sB, "Dg", [128, 2, CK], BF16)
                      PWt = sb(sB, "PWt", [128, 4, 2, 256], BF16)
                      S.dma("pool", PWt[:], pool_w[l].rearrange("g (t p) j -> p g t j", p=128), PWt, writes=[PWt])
                      for grp in range(4):
                          w = 2 << grp
                          for jt in range(2):
                              mt = grp * 2 + jt
                              g, mloc = 6 + mt // 4, mt % 4
                              X = WB[0]
                              S.op("dve", lambda e, mt=mt, X=X: e.tensor_copy(X[:, 0:16], PH[:, mt, :]), reads=[PH], writes=[X])
                              for half in range(2):
                                  pa, pb = inproj_half(g, mloc, half)
                                  for n, p_ in enumerate((pa, pb)):
                                      S.op("act", lambda e, n=n, p_=p_, half=half, X=X: e.copy(
                                          X[:, 16 + half * 1024 + n * 512:16 + half * 1024 + (n + 1) * 512], p_[:]),
                                          reads=[p_], writes=[X])
                              S.op("dve", lambda e, mt=mt, X=X: e.tensor_copy(PH[:, mt, :], X[:, CK:CK + 16]), reads=[X], writes=[PH])
                              cur = X
                              sh = 1
                              bi_ = 1
                              while sh < w:
                                  nxt = WB[bi_]
                                  bi_ = 3 - bi_
                                  S.op("dve", lambda e, cur=cur, nxt=nxt, sh=sh: e.tensor_tensor(
                                      out=nxt[:, sh:16 + CK], in0=cur[:, sh:16 + CK], in1=cur[:, 0:16 + CK - sh], op=ALU.add),
                                      reads=[cur], writes=[nxt])
                                  cur = nxt
                                  sh *= 2
                              S.op("dve", lambda e, cur=cur, jt=jt, w=w, X=X: e.scalar_tensor_tensor(
                                  out=Dg[:, jt, :], in0=cur[:, 16:16 + CK], scalar=1.0 / w, in1=X[:, 16:16 + CK],
                                  op0=ALU.mult, op1=ALU.subtract), reads=[cur, X], writes=[Dg])
                              if ck == 0 or heavy:
                                  icb = O_IC2 if heavy else O_IC
                                  tmp = WB[3]
                                  S.op("dve", lambda e, cur=cur, grp=grp, tmp=tmp, icb=icb: e.tensor_tensor(
                                      out=tmp[:, 0:16], in0=cur[:, 16:32], in1=cst[:, icb + grp * 16:icb + grp * 16 + 16], op=ALU.mult),
                                      reads=[cur, cst], writes=[tmp])
                                  S.op("dve", lambda e, jt=jt, tmp=tmp, X=X: e.tensor_tensor(
                                      out=Dg[:, jt, 0:16], in0=tmp[:, 0:16], in1=X[:, 16:32], op=ALU.subtract),
                                      reads=[tmp, X], writes=[Dg])
                          for jt in range(2):
                              mt = grp * 2 + jt
                              gg, gm = 13 + mt // 4, mt % 4
                              for half in range(2):
                                  ya, yb = next_pair()
                                  for n, p_ in enumerate((ya, yb)):
                                      for it in range(2):
                                          S.op("pe", lambda e, n=n, p_=p_, it=it, grp=grp, jt=jt, half=half: e.matmul(
                                              p_[:], PWt[:, grp, it, jt * 128:(jt + 1) * 128],
                                              Dg[:, it, half * 1024 + n * 512:half * 1024 + (n + 1) * 512],
                                              start=(it == 0), stop=(it == 1)), reads=[PWt, Dg], writes=[p_], inc=(it == 1))
                                  pa, pb = inproj_half(gg, gm, half)
                                  sl = sil[sil_i[0] % 2]
                                  sil_i[0] += 1
                                  evac_pair("act", pa, pb, sl, lambda e, o, i: e.activation(o, i, AF.Silu))
                                  s_ = next_stg()
                                  for n, p_ in enumerate((ya, yb)):
                                      S.op("dve", lambda e, n=n, p_=p_, sl=sl, s_=s_, mt=mt: e.scalar_tensor_tensor(
                                          out=s_[:, n * 512:(n + 1) * 512], in0=p_[:], scalar=cst[:, LB + O_PS + mt:LB + O_PS + mt + 1],
                                          in1=sl[:, n * 512:(n + 1) * 512], op0=ALU.mult, op1=ALU.mult),
                                          reads=[p_, cst, sl], writes=[s_])
                                  S.dma("sp", G_d[8 + mt, :, half * 1024:(half + 1) * 1024], s_[:], s_, reads=[s_])
                      S.barrier()
                      S.retire([PWt])
                chk("pool")
                with contextlib.ExitStack() as sC:
                    WB = [sb(sC, "WBc%d" % i, [128, 16 + CK], F32) for i in range(5)]
                    LWA = sb(sC, "LWA", [128, 8, 128], BF16)
                    LWX = sb(sC, "LWX", [128, 8, 128], BF16)
                    xcb = sb(sC, "xcb", [128, CK], BF16)
                    S.dma("pool", LWA[:], lru_wa[l].rearrange("g i j -> i g j"), LWA, writes=[LWA])
                    S.dma("pool", LWX[:], lru_wx[l].rearrange("g i j -> i g j"), LWX, writes=[LWX])
                    if light:
                        S.op("dve", lambda e: e.scalar_tensor_tensor(
                            out=LHo[:], in0=LH[:], scalar=cst[:, O_WS + ck:O_WS + ck + 1], in1=LHo[:], op0=ALU.mult, op1=ALU.add),
                            reads=[LH, LHo, cst], writes=[LHo])
                        S.op("dve", lambda e: e.scalar_tensor_tensor(
                            out=CARo[:], in0=CAR[:], scalar=cst[:, O_WS + ck:O_WS + ck + 1], in1=CARo[:], op0=ALU.mult, op1=ALU.add),
                            reads=[CAR, CARo, cst], writes=[CARo])
                    for blk in range(8):
                        g, mloc = 8 + blk // 4, blk % 4
                        XL, XC, R_, I_, A_ = WB
                        HH = XL
                        S.op("dve", lambda e, blk=blk: e.tensor_copy(XL[:, 12:16], LH[:, blk, :]), reads=[LH], writes=[XL])
                        for half in range(2):
                            pa, pb = inproj_half(g, mloc, half)
                            for n, p_ in enumerate((pa, pb)):
                                S.op("act", lambda e, n=n, p_=p_, half=half: e.copy(
                                    XL[:, 16 + half * 1024 + n * 512:16 + half * 1024 + (n + 1) * 512], p_[:]),
                                    reads=[p_], writes=[XL])
                        S.op("dve", lambda e, blk=blk: e.tensor_copy(LH[:, blk, :], XL[:, 12 + CK:16 + CK]), reads=[XL], writes=[LH])
                        cw = LB + O_CW
                        S.op("dve", lambda e, blk=blk: e.tensor_scalar(
                            out=XC[:, 0:CK], in0=XL[:, 16:16 + CK], scalar1=cst[:, cw + 24 + blk:cw + 25 + blk],
                            scalar2=cst[:, LB + O_CB + blk:LB + O_CB + blk + 1], op0=ALU.mult, op1=ALU.add),
                            reads=[XL, cst], writes=[XC])
                        for k in (2, 1, 0):
                            S.op("dve", lambda e, blk=blk, k=k: e.scalar_tensor_tensor(
                                out=XC[:, 0:CK], in0=XL[:, 13 + k:13 + k + CK], scalar=cst[:, cw + k * 8 + blk:cw + k * 8 + blk + 1],
                                in1=XC[:, 0:CK], op0=ALU.mult, op1=ALU.add), reads=[XL, cst, XC], writes=[XC])
                        S.op("act", lambda e: e.copy(xcb[:], XC[:, 0:CK]), reads=[XC], writes=[xcb])
                        for which, (LWm, dstb, bcol) in enumerate(((LWA, R_, O_BA), (LWX, I_, O_BX))):
                            for half in range(2):
                                pa, pb = next_pair()
                                for n, p_ in enumerate((pa, pb)):
                                    S.op("pe", lambda e, n=n, p_=p_, half=half, LWm=LWm, blk=blk: e.matmul(
                                        p_[:], LWm[:, blk, :], xcb[:, half * 1024 + n * 512:half * 1024 + (n + 1) * 512],
                                        start=True, stop=True), reads=[LWm, xcb], writes=[p_])
                                    S.op("act", lambda e, n=n, p_=p_, half=half, dstb=dstb, bcol=bcol, blk=blk: e.activation(
                                        dstb[:, half * 1024 + n * 512:half * 1024 + (n + 1) * 512], p_[:], AF.Sigmoid,
                                        bias=cst[:, LB + bcol + blk:LB + bcol + blk + 1]), reads=[p_, cst], writes=[dstb])
                        S.op("act", lambda e, blk=blk: e.activation(A_[:, 0:CK], R_[:, 0:CK], AF.Exp, scale=CL[:, blk:blk + 1]),
                             reads=[R_, CL], writes=[A_])
                        S.op("act", lambda e, blk=blk: e.activation(R_[:, 0:CK], R_[:, 0:CK], AF.Exp, scale=CL2[:, blk:blk + 1]),
                             reads=[R_, CL2], writes=[R_])
                        S.op("dve", lambda e: e.tensor_scalar(out=R_[:, 0:CK], in0=R_[:, 0:CK], scalar1=-1.0, scalar2=1.0,
                                                              op0=ALU.mult, op1=ALU.add), reads=[R_], writes=[R_])
                        S.op("act", lambda e: e.activation(R_[:, 0:CK], R_[:, 0:CK], AF.Sqrt), reads=[R_], writes=[R_])
                        if heavy:
                            S.op("dve", lambda e: e.tensor_scalar(out=R_[:, 0:1], in0=R_[:, 0:1], scalar1=cst[:, O_IF:O_IF + 1],
                                                                  scalar2=None, op0=ALU.max), reads=[R_, cst], writes=[R_])
                        elif ck == 0:
                            S.op("dve", lambda e: e.memset(R_[:, 0:1], 1.0), reads=[R_], writes=[R_])
                        S.op("dve", lambda e: e.tensor_tensor(out=I_[:, 0:CK], in0=I_[:, 0:CK], in1=XC[:, 0:CK], op=ALU.mult),
                             reads=[I_, XC], writes=[I_])
                        S.op("dve", lambda e: e.tensor_tensor(out=I_[:, 0:CK], in0=I_[:, 0:CK], in1=R_[:, 0:CK], op=ALU.mult),
                             reads=[I_, R_], writes=[I_])
                        S.op("dve", lambda e, blk=blk: e.tensor_tensor_scan(
                            out=HH[:, 0:CK], data0=A_[:, 0:CK], data1=I_[:, 0:CK], initial=CAR[:, blk:blk + 1],
                            op0=ALU.mult, op1=ALU.add), reads=[A_, I_, CAR], writes=[HH])
                        S.op("dve", lambda e, blk=blk: e.tensor_copy(CAR[:, blk:blk + 1], HH[:, CK - 1:CK]), reads=[HH], writes=[CAR])
                        gg, gm = 15 + blk // 4, blk % 4
                        for half in range(2):
                            if light:
                                continue
                            pa, pb = inproj_half(gg, gm, half)
                            sl = sil[sil_i[0] % 2]
                            sil_i[0] += 1
                            evac_pair("act", pa, pb, sl, lambda e, o, i: e.activation(o, i, AF.Silu))
                            s_ = next_stg()
                            S.op("dve", lambda e, half=half, sl=sl, s_=s_: e.tensor_tensor(
                                out=s_[:], in0=HH[:, half * 1024:(half + 1) * 1024], in1=sl[:], op=ALU.mult),
                                reads=[HH, sl], writes=[s_])
                            S.dma("sp", G_d[16 + blk, :, half * 1024:(half + 1) * 1024], s_[:], s_, reads=[s_])
                    S.barrier()
                    S.retire([LWA, LWX])
                chk("lru")
                with contextlib.ExitStack() as sD:
                  if not light:
                      QM = sb(sD, "QM", [128, CK], BF16)
                      Et = [sb(sD, "Etm%d" % i, [128, 512], BF16) for i in range(3)]
                      YM = sb(sD, "YM", [128, CK], F32)
                      RS = sb(sD, "RS", [128, CK], F32)
                      et_i = 0
                      for hm in range(4):
                          for half in range(2):
                              pa, pb = inproj_half(10, hm, half)
                              for n, p_ in enumerate((pa, pb)):
                                  S.op("act", lambda e, n=n, p_=p_, half=half: e.activation(
                                      QM[:, half * 1024 + n * 512:half * 1024 + (n + 1) * 512], p_[:], AF.Copy, scale=128 ** -0.5),
                                      reads=[p_], writes=[QM])
                          for qt in range(4):
                              po, psm = PSB[0 + 2 * (qt % 2)], PSB[1 + 2 * (qt % 2)]
                              for mb in range(2):
                                  pss = PSB[4 + (et_i % 3)]
                                  et = Et[et_i % 3]
                                  et_i += 1
                                  S.op("pe", lambda e, mb=mb, hm=hm, qt=qt, pss=pss: e.matmul(
                                      pss[:], MK[:, hm, mb * 128:(mb + 1) * 128], QM[:, qt * 512:(qt + 1) * 512], start=True, stop=True),
                                      reads=[MK, QM], writes=[pss])
                                  S.op("act", lambda e, et=et, pss=pss: e.activation(et[:], pss[:], AF.Exp), reads=[pss], writes=[et])
                                  S.op("pe", lambda e, mb=mb, hm=hm, et=et, po=po: e.matmul(
                                      po[:], MV[:, mb, hm * 128:(hm + 1) * 128], et[:], start=(mb == 0), stop=(mb == 1)),
                                      reads=[MV, et], writes=[po], inc=False)
                                  S.op("pe", lambda e, mb=mb, et=et, psm=psm: e.matmul(
                                      psm[:], ones_b[:], et[:], start=(mb == 0), stop=(mb == 1)),
                                      reads=[ones_b, et], writes=[psm, po], inc=True)
                              S.op("dve", lambda e, qt=qt, psm=psm: e.reciprocal(RS[:, qt * 512:(qt + 1) * 512], psm[:]),
                                   reads=[psm], writes=[RS])
                              S.op("dve", lambda e, qt=qt, po=po: e.tensor_tensor(
                                  out=YM[:, qt * 512:(qt + 1) * 512], in0=po[:], in1=RS[:, qt * 512:(qt + 1) * 512], op=ALU.mult),
                                  reads=[po, RS], writes=[YM])
                          for half in range(2):
                              pa, pb = inproj_half(17, hm, half)
                              sl = sil[sil_i[0] % 2]
                              sil_i[0] += 1
                              evac_pair("act", pa, pb, sl, lambda e, o, i: e.activation(o, i, AF.Silu))
                              s_ = next_stg()
                              S.op("dve", lambda e, half=half, sl=sl, s_=s_: e.tensor_tensor(
                                  out=s_[:], in0=YM[:, half * 1024:(half + 1) * 1024], in1=sl[:], op=ALU.mult),
                                  reads=[YM, sl], writes=[s_])
                              S.dma("sp", G_d[24 + hm, :, half * 1024:(half + 1) * 1024], s_[:], s_, reads=[s_])
                      S.barrier()
                chk("mem")
                pre(11)
                pre(12)
                for h in range(8):
                    for half in range(2):
                        if light:
                            continue
                        pa, pb = inproj_half(11 + h // 4, h % 4, half)
                        s_ = next_stg()
                        evac_pair("act", pa, pb, s_, lambda e, o, i: e.activation(o, i, AF.Silu))
                        S.dma("sp", SGA_d[h, :, half * 1024:(half + 1) * 1024], s_[:], s_, reads=[s_])
                S.barrier()
                S.retire([H] + Wt + stg)

            chk("p1")
            if light:
                continue
            with contextlib.ExitStack() as p2:
                nkc = nchunk if heavy else ck + 1
                MSK = None
                if heavy:
                    PT = [sb(p2, "PT%d" % i, [128, 1024], BF16) for i in range(6)]
                    MSK = [[sb(p2, "MSK%d_%d" % (k_, i), [128, 1024], BF16) for i in range(6)] for k_ in range(nchunk)]
                    S.op("dve", lambda e: e.memset(PT[0][:], 1.0), writes=[PT[0]])
                    S.op("dve", lambda e: e.memset(PT[5][:], 0.0), writes=[PT[5]])
                    for r_ in range(4):
                        pt = PT[1 + r_]
                        S.op("dve", lambda e, pt=pt: e.memset(pt[:], 1.0), writes=[pt])
                        for c_ in range(2):
                            if r_ > 0:
                                S.op("dve", lambda e, pt=pt, c_=c_, r_=r_: e.memset(pt[:, c_ * 512:c_ * 512 + r_ * 128], 0.0),
                                     reads=[pt], writes=[pt])
                            S.op("dve", lambda e, pt=pt, c_=c_, r_=r_: e.tensor_copy(
                                pt[:, c_ * 512 + r_ * 128:c_ * 512 + (r_ + 1) * 128], tri[:]), reads=[pt, tri], writes=[pt])
                    for k_ in range(nchunk):
                        for i_ in range(6):
                            S.op("dve", lambda e, k_=k_, i_=i_: e.tensor_scalar(
                                out=MSK[k_][i_][:], in0=PT[i_][:], scalar1=cst[:, O_WS + k_:O_WS + k_ + 1],
                                scalar2=cst[:, O_AL + k_:O_AL + k_ + 1], op0=ALU.mult, op1=ALU.add),
                                reads=[PT[i_], cst], writes=[MSK[k_][i_]])
                KTb = [sb(p2, "KTb%d" % i, [128, nkc * CK], BF16) for i in range(2)]
                Vb = [sb(p2, "Vb%d" % i, [128, nkc * 16, 128], BF16) for i in range(2)]
                Qb = [sb(p2, "Qb%d" % i, [128, CK], BF16) for i in range(2)]
                SGb = [sb(p2, "SGb%d" % i, [128, CK], BF16) for i in range(2)]
                Et = [sb(p2, "Et%d" % i, [128, 1024], BF16) for i in range(3)]
                OS = [sb(p2, "OS%d" % i, [128, 512], F32) for i in range(4)]
                ta = [sb(p2, "ta%d" % i, [128, 512], F32) for i in range(4)]
                gst = [sb(p2, "gst%d" % i, [128, 512], BF16) for i in range(2)]

                def load_head(h):
                    i = h % 2
                    S.dma("sp", Qb[i][:], QT_d[h], Qb[i], writes=[Qb[i]])
                    S.dma("sp", SGb[i][:], SGA_d[h], SGb[i], writes=[SGb[i]])
                    S.dma("sp", KTb[i][:], KT_d[l, h, :, 0:nkc * CK], KTb[i], writes=[KTb[i]])
                    S.dma("sp", Vb[i][:], V_d[l, h, 0:nkc * CK, :].rearrange("(kb p) e -> p kb e", p=128), Vb[i], writes=[Vb[i]])

                load_head(0)
                et_i = 0
                g_i = 0
                pend_tail = []
                tail_i = [0]
                accb = (PSB[0], PSB[1], PSB[2], PSB[3])
                for h in range(8):
                    while pend_tail:
                        pend_tail.pop(0)()
                    if h + 1 < 8:
                        load_head(h + 1)
                    i = h % 2
                    Q, KT_, V_, SG = Qb[i], KTb[i], Vb[i], SGb[i]
                    for qt in range(4):
                        nkb = ck * 16 + qt * 4 + 4
                        blocks = []
                        if heavy:
                            for kb in range(nchunk * 16):
                                rel = (kb % 16) - 4 * qt
                                ty = 0 if rel < 0 else (5 if rel > 3 else 1 + rel)
                                blocks.append((kb, 0, False, MSK[kb // 16][ty]))
                        else:
                            for kb in range(nkb):
                                sub = kb - (ck * 16 + qt * 4)
                                n0 = sub * 128 if sub > 0 else 0
                                blocks.append((kb, n0, sub >= 0, None))
                        nb = len(blocks)

                        def emit_s(j):
                            kb, n0, dg, mk = blocks[j]
                            for c in range(2):
                                pss = PSB[4 + 2 * (j % 2) + c]
                                S.op("pe", lambda e, kb=kb, n0=n0, pss=pss, c=c: e.matmul(
                                    pss[:, n0:512], KT_[c * 64:(c + 1) * 64, kb * 128:(kb + 1) * 128],
                                    Q[c * 64:(c + 1) * 64, qt * 512 + n0:(qt + 1) * 512], start=True, stop=True),
                                    reads=[KT_, Q], writes=[pss])

                        emit_s(0)
                        if pend_tail:
                            pend_tail.pop(0)()
                        if nb > 1:
                            emit_s(1)
                        for j in range(nb):
                            kb, n0, dg, mk = blocks[j]
                            pa_, pb_ = PSB[4 + 2 * (j % 2)], PSB[5 + 2 * (j % 2)]
                            et = Et[et_i % 3]
                            et_i += 1
                            if n0 == 0:
                                pair_t = PSP[2 + (j % 2)]
                                S.op("act", lambda e, et=et, pair_t=pair_t: e.activation(et[:], pair_t[:], AF.Exp),
                                     reads=[pa_, pb_], writes=[et])
                            else:
                                for c, pss in enumerate((pa_, pb_)):
                                    S.op("act", lambda e, et=et, pss=pss, n0=n0, c=c: e.activation(
                                        et[:, c * 512 + n0:(c + 1) * 512], pss[:, n0:512], AF.Exp), reads=[pss], writes=[et])
                            if mk is not None:
                                S.op("dve", lambda e, et=et, mk=mk: e.tensor_tensor(out=et[:], in0=et[:], in1=mk[:], op=ALU.mult),
                                     reads=[et, mk], writes=[et])
                            if dg:
                                for c in range(2):
                                    S.op("dve", lambda e, et=et, n0=n0, c=c: e.tensor_tensor(
                                        out=et[:, c * 512 + n0:c * 512 + n0 + 128], in0=et[:, c * 512 + n0:c * 512 + n0 + 128],
                                        in1=tri[:], op=ALU.mult), reads=[et, tri], writes=[et])
                            if j + 2 < nb:
                                emit_s(j + 2)
                            last = (j == nb - 1)
                            for c in range(2):
                                po, psm = accb[2 * c], accb[2 * c + 1]
                                S.op("pe", lambda e, kb=kb, n0=n0, et=et, po=po, j=j, last=last, c=c: e.matmul(
                                    po[:, n0:512], V_[:, kb, :], et[:, c * 512 + n0:(c + 1) * 512], start=(j == 0), stop=last),
                                    reads=[V_, et], writes=[po], inc=False)
                                S.op("pe", lambda e, n0=n0, et=et, psm=psm, j=j, last=last, c=c: e.matmul(
                                    psm[:, n0:512], ones_b[:], et[:, c * 512 + n0:(c + 1) * 512], start=(j == 0), stop=last),
                                    reads=[ones_b, et], writes=[psm, po], inc=True)
                        for c in range(2):
                            po, psm = accb[2 * c], accb[2 * c + 1]
                            S.op("dve", lambda e, c=c, po=po: e.tensor_copy(OS[2 * c][:], po[:]), reads=[po], writes=[OS[2 * c]])
                            S.op("act", lambda e, c=c, psm=psm: e.activation(OS[2 * c + 1][:], psm[:], AF.Ln), reads=[psm],
                                 writes=[OS[2 * c + 1]])
                        for c in range(2):
                            S.op("act", lambda e, c=c: e.activation(OS[2 * c + 1][:], OS[2 * c + 1][:], AF.Exp, scale=-1.0),
                                 reads=[OS[2 * c + 1]], writes=[OS[2 * c + 1]])
                        t0, t1, t2, t3 = ta
                        S.op("dve", lambda e: e.tensor_tensor(out=t0[:], in0=OS[0][:], in1=OS[1][:], op=ALU.mult),
                             reads=[OS[0], OS[1]], writes=[t0])
                        S.op("dve", lambda e: e.tensor_tensor(out=t1[:], in0=OS[2][:], in1=OS[3][:], op=ALU.mult),
                             reads=[OS[2], OS[3]], writes=[t1])
                        S.op("dve", lambda e: e.scalar_tensor_tensor(out=t2[:], in0=t1[:], scalar=NEGLAM[:, 0:1], in1=t0[:],
                                                                     op0=ALU.mult, op1=ALU.add),
                             reads=[t1, t0, NEGLAM], writes=[t2])
                        S.op("dve", lambda e: e.tensor_tensor(out=t3[:], in0=t2[:], in1=t2[:], op=ALU.mult),
                             reads=[t2], writes=[t3])

                        def tail(h=h, qt=qt, SG=SG):
                            t0, t1, t2, t3 = ta
                            pq = PSB[7]
                            S.op("pe", lambda e: e.matmul(pq[:], ones_f[:], t3[:], start=True, stop=True),
                                 reads=[ones_f, t3], writes=[pq])
                            S.op("act", lambda e: e.activation(t0[:], pq[:], AF.Ln, bias=epsb[:, 0:1], scale=1.0 / 128),
                                 reads=[pq, epsb], writes=[t0])
                            S.op("act", lambda e: e.activation(t1[:], t0[:], AF.Exp, scale=-0.5), reads=[t0], writes=[t1])
                            S.op("dve", lambda e: e.tensor_tensor(out=t2[:], in0=t2[:], in1=t1[:], op=ALU.mult),
                                 reads=[t2, t1], writes=[t2])
                            gs = gst[tail_i[0] % 2]
                            tail_i[0] += 1
                            S.op("dve", lambda e, gs=gs, qt=qt: e.scalar_tensor_tensor(
                                out=gs[:], in0=t2[:], scalar=SGL[:, 0:1], in1=SG[:, qt * 512:(qt + 1) * 512],
                                op0=ALU.mult, op1=ALU.mult), reads=[t2, SGL, SG], writes=[gs])
                            S.dma("sp", G_d[h, :, qt * 512:(qt + 1) * 512], gs[:], gs, reads=[gs])

                        pend_tail.append(tail)
                    if False:
                        pass
                while pend_tail:
                    pend_tail.pop(0)()
                S.barrier()
                S.retire(KTb + Vb + Qb + SGb + gst)

            chk("p2")
            with contextlib.ExitStack() as p3:
                Hh = sb(p3, "Hh", [128, 16, 1024], BF16)
                Gh = sb(p3, "Gh", [128, 28, 1024], BF16)
                NB3 = 3
                Wb = [sb(p3, "Wb%d" % i, [128, 8, 128], BF16) for i in range(NB3)]
                Wm = [sb(p3, "Wm%d" % i, [128, 16, 128], BF16) for i in range(NB3)]
                sg = [sb(p3, "sg%d" % i, [128, 1024], F32) for i in range(2)]
                tm = [sb(p3, "tm%d" % i, [128, 1024], F32) for i in range(2)]
                acc = sb(p3, "acc", [128, 1024], F32)
                mst = [sb(p3, "mst%d" % i, [128, 1024], BF16) for i in range(2)]
                KOFF = (0, 8, 16, 24, 28)
                steps = [(n_, bi) for n_ in range(16) for bi in range(4)]

                def load_w3(si):
                    n_, bi = steps[si]
                    i = si % NB3
                    nk = KOFF[bi + 1] - KOFF[bi]
                    S.dma("pool", Wb[i][:, 0:nk, :],
                          w_branch[l][KOFF[bi] * 128:KOFF[bi + 1] * 128, :].rearrange("(k p) n -> p k n", p=128)[:, :, n_ * 128:(n_ + 1) * 128],
                          Wb[i], writes=[Wb[i]])
                    cb_ = 9216 + bi * 2048 + n_ * 128
                    S.dma("pool", Wm[i][:], w_in[l].rearrange("(c p) n -> p c n", p=128)[:, :, cb_:cb_ + 128],
                          Wm[i], writes=[Wm[i]])

                it3 = 0
                for half in range(2):
                    S.dma("sp", Hh[:], Hs_d.rearrange("c p n -> p c n")[:, :, half * 1024:(half + 1) * 1024], Hh, writes=[Hh])
                    S.dma("sp", Gh[:], G_d.rearrange("c p n -> p c n")[:, :, half * 1024:(half + 1) * 1024], Gh, writes=[Gh])
                    load_w3(0)
                    load_w3(1)
                    for si, (n_, bi) in enumerate(steps):
                        if si + 2 < len(steps):
                            load_w3(si + 2)
                        i = si % NB3
                        if True:
                            za, zb = PSB[(it3 % 2) * 2], PSB[(it3 % 2) * 2 + 1]
                            ma, mb_ = PSB[4 + (it3 % 2) * 2], PSB[5 + (it3 % 2) * 2]
                            it3 += 1
                            ks = list(range(KOFF[bi], KOFF[bi + 1]))
                            for n, p_ in enumerate((ma, mb_)):
                                for c in range(16):
                                    S.op("pe", lambda e, c=c, n=n, p_=p_, i=i: e.matmul(
                                        p_[:], Wm[i][:, c, :], Hh[:, c, n * 512:(n + 1) * 512], start=(c == 0), stop=(c == 15)),
                                        reads=[Wm[i], Hh], writes=[p_], inc=(c == 15))
                            for n, p_ in enumerate((za, zb)):
                                for kk, k in enumerate(ks):
                                    S.op("pe", lambda e, k=k, kk=kk, n=n, p_=p_, i=i, ks=ks: e.matmul(
                                        p_[:], Wb[i][:, kk, :], Gh[:, k, n * 512:(n + 1) * 512], start=(kk == 0),
                                        stop=(kk == len(ks) - 1)), reads=[Wb[i], Gh], writes=[p_], inc=(kk == len(ks) - 1))
                            sgt = sg[bi % 2]
                            for n, p_ in enumerate((ma, mb_)):
                                S.op("act", lambda e, n=n, p_=p_, sgt=sgt: e.activation(sgt[:, n * 512:(n + 1) * 512], p_[:], AF.Sigmoid),
                                     reads=[p_], writes=[sgt])
                            dst = acc if bi == 0 else tm[bi % 2]
                            for n, p_ in enumerate((za, zb)):
                                S.op("dve", lambda e, n=n, p_=p_, sgt=sgt, dst=dst: e.tensor_tensor(
                                    out=dst[:, n * 512:(n + 1) * 512], in0=p_[:], in1=sgt[:, n * 512:(n + 1) * 512], op=ALU.mult),
                                    reads=[p_, sgt], writes=[dst])
                            if bi > 0:
                                if bi < 3:
                                    S.op("dve", lambda e, dst=dst: e.tensor_tensor(out=acc[:], in0=acc[:], in1=dst[:], op=ALU.add),
                                         reads=[acc, dst], writes=[acc])
                                else:
                                    ms = mst[n_ % 2]
                                    S.op("dve", lambda e, dst=dst, ms=ms: e.tensor_tensor(out=ms[:], in0=acc[:], in1=dst[:], op=ALU.add),
                                         reads=[acc, dst], writes=[ms])
                                    S.dma("sp", MG_d[n_, :, half * 1024:(half + 1) * 1024], ms[:], ms, reads=[ms])
                S.barrier()
                S.retire([Hh, Gh] + Wb + Wm + mst)

            chk("p3a")
            with contextlib.ExitStack() as p4:
                MGh = sb(p4, "MGh", [128, 16, 1024], BF16)
                Wo = [sb(p4, "Wo%d" % i, [128, 16, 128], BF16) for i in range(2)]
                xin = [sb(p4, "xin%d" % i, [128, 1024], F32) for i in range(2)]
                xo = [sb(p4, "xo%d" % i, [128, 1024], F32) for i in range(2)]
                it4 = 0
                for half in range(2):
                    S.dma("sp", MGh[:], MG_d.rearrange("c p n -> p c n")[:, :, half * 1024:(half + 1) * 1024], MGh, writes=[MGh])
                    cols = slice(c0 + half * 1024, c0 + (half + 1) * 1024)
                    for n_ in range(16):
                        i = it4 % 2
                        S.dma("pool", Wo[i][:], w_out[l].rearrange("(k p) n -> p k n", p=128)[:, :, n_ * 128:(n_ + 1) * 128],
                              Wo[i], writes=[Wo[i]])
                        S.dma("sp", xin[i][:], (xown_d if heavy else xsrc)[n_ * 128:(n_ + 1) * 128, cols], xin[i], writes=[xin[i]])
                        ya, yb = PSB[(it4 % 4) * 2], PSB[(it4 % 4) * 2 + 1]
                        it4 += 1
                        for n, p_ in enumerate((ya, yb)):
                            for k in range(16):
                                S.op("pe", lambda e, k=k, n=n, p_=p_, i=i: e.matmul(
                                    p_[:], Wo[i][:, k, :], MGh[:, k, n * 512:(n + 1) * 512], start=(k == 0), stop=(k == 15)),
                                    reads=[Wo[i], MGh], writes=[p_], inc=(k == 15))
                            S.op("dve", lambda e, n=n, p_=p_, i=i: e.tensor_tensor(
                                out=xo[i][:, n * 512:(n + 1) * 512], in0=p_[:], in1=xin[i][:, n * 512:(n + 1) * 512], op=ALU.add),
                                reads=[p_, xin[i]], writes=[xo[i]])
                        S.dma("sp", (x2own_d if heavy else xdst)[n_ * 128:(n_ + 1) * 128, cols], xo[i][:], xo[i], reads=[xo[i]])
                S.barrier()
                S.retire([MGh] + Wo + xin + xo)

    except _Stop:
        pass
    with contextlib.ExitStack() as pf:
      if not stop_at:
          xts = [sb(pf, "fx%d" % i, [128, 16, 512], F32) for i in range(2)]
          ots = [sb(pf, "fo%d" % i, [128, 16, 512], F32) for i in range(2)]
          sqs = [sb(pf, "fsq%d" % i, [128, 512], F32) for i in range(3)]
          r1 = sb(pf, "fr1", [128, 512], F32)
          r2 = sb(pf, "fr2", [128, 512], F32)
          fsrc = x2own_d if DEDUP else x2T
          for tt in range((CK if DEDUP else S_) // 512):
              xt = xts[tt % 2]
              ot = ots[tt % 2]
              S.dma("sp", xt[:], fsrc.rearrange("(c p) n -> p c n", p=128)[:, :, tt * 512:(tt + 1) * 512], xt, writes=[xt])
              t1, t2 = r1, r2
              for c in range(16):
                  sq = sqs[c % 3]
                  S.op("act", lambda e, c=c, sq=sq, xt=xt: e.activation(sq[:], xt[:, c, :], AF.Square), reads=[xt], writes=[sq])
                  S.op("pe", lambda e, c=c, sq=sq: e.matmul(PSB[7][:], ones_f[:], sq[:], start=(c == 0), stop=(c == 15)),
                       reads=[ones_f, sq], writes=[PSB[7]])
              S.op("act", lambda e: e.activation(t1[:], PSB[7][:], AF.Sqrt, bias=epsb[:, 0:1], scale=1.0 / D),
                   reads=[PSB[7], epsb], writes=[t1])
              S.op("dve", lambda e: e.reciprocal(t2[:], t1[:]), reads=[t1], writes=[t2])
              for c in range(16):
                  S.op("dve", lambda e, c=c, xt=xt, ot=ot: e.scalar_tensor_tensor(
                      out=ot[:, c, :], in0=xt[:, c, :], scalar=cst[:, O_FG + c:O_FG + c + 1], in1=t2[:],
                      op0=ALU.mult, op1=ALU.mult), reads=[xt, cst, t2], writes=[ot])
              S.dma("sp", outT.rearrange("(c p) n -> p c n", p=128)[:, :, tt * 512:(tt + 1) * 512], ot[:], ot, reads=[ot])
          S.barrier()
    if not stop_at:
        es.close()
    return nc


def host_inputs(nchunk, b, jown, x, mem, norm_g, w_in, lam_vecs, subln_g, pool_w, pool_scale, conv_w, conv_b,
                lru_wa, lru_ba, lru_wx, lru_bx, lru_lambda, mem_norm_g, w_mem_kv, w_branch, w_out, final_g):
    S_ = nchunk * CK
    f = np.float32

    def pc(v, n):
        return np.ascontiguousarray(np.asarray(v, f).reshape(n, 128).T)

    cst = np.zeros((128, NCST), f)
    for l in range(2):
        LB = l * LW
        cst[:, LB + O_NG:LB + O_NG + 16] = pc(norm_g[l], 16)
        cst[:, LB + O_MG:LB + O_MG + 16] = pc(mem_norm_g[l], 16)
        cst[:, LB + O_PS:LB + O_PS + 8] = pc(pool_scale[l], 8)
        for k in range(4):
            cst[:, LB + O_CW + k * 8:LB + O_CW + k * 8 + 8] = pc(conv_w[l][k], 8)
        cst[:, LB + O_CB:LB + O_CB + 8] = pc(conv_b[l], 8)
        cst[:, LB + O_BA:LB + O_BA + 8] = pc(lru_ba[l], 8)
        cst[:, LB + O_BX:LB + O_BX + 8] = pc(lru_bx[l], 8)
        cst[:, LB + O_LL:LB + O_LL + 8] = pc(lru_lambda[l], 8)
        cst[:, LB + O_SG] = np.asarray(subln_g[l], f)
        cst[:, LB + O_LV:LB + O_LV + 256] = np.asarray(lam_vecs[l], f).reshape(1, 256)
    cst[:, O_FG:O_FG + 16] = pc(final_g, 16)
    for g in range(4):
        w = 2 << g
        for t in range(16):
            cst[:, O_IC + g * 16 + t] = 1.0 / min(t + 1, w)
    for k in range(4):
        cst[:, O_WS + k] = 1.0 if k == jown else 0.0
        cst[:, O_AL + k] = 1.0 if k < jown else 0.0
    cst[:, O_IF] = 1.0 if jown == 0 else 0.0
    for g in range(4):
        w = 2 << g
        for t in range(16):
            cst[:, O_IC2 + g * 16 + t] = (1.0 / min(t + 1, w)) if jown == 0 else 1.0 / w
    tri = np.triu(np.ones((128, 128), f))
    return {
        "xT": np.ascontiguousarray(np.asarray(x[b, :S_], f).T),
        "memT": np.ascontiguousarray(np.asarray(mem[b], f).T),
        "cst": cst, "tri": tri,
        "w_in": np.asarray(w_in, f), "pool_w": np.asarray(pool_w, f), "lru_wa": np.asarray(lru_wa, f),
        "lru_wx": np.asarray(lru_wx, f), "w_mem_kv": np.asarray(w_mem_kv, f), "w_branch": np.asarray(w_branch, f),
        "w_out": np.asarray(w_out, f),
    }


def run(nchunk, inputs, n_batch=2):
    nc = build(nchunk)
    maps = []
    per_b = 8 // n_batch
    for c in range(8):
        jown = (c % per_b) * nchunk // per_b
        maps.append(host_inputs(nchunk, min(c // per_b, n_batch - 1), jown, **inputs))
    res = run_bass_kernel_spmd(nc, maps, core_ids=list(range(8)))
    S_ = nchunk * CK
    out = np.zeros((n_batch, S_, D), np.float32)
    if nchunk == 1:
        q = S_ // per_b
        for c in range(8):
            b, j = c // per_b, c % per_b
            out[b, j * q:(j + 1) * q, :] = res.results[c]["outT"][:, j * q:(j + 1) * q].T
    else:
        sub = per_b // nchunk
        q = CK // sub
        for c in range(8):
            b, r = c // per_b, c % per_b
            jown, k = r * nchunk // per_b, r % sub
            out[b, jown * CK + k * q:jown * CK + (k + 1) * q, :] = res.results[c]["outT"][:, k * q:(k + 1) * q].T
    return out


def kernel(**inputs):
    return run(4, inputs)
```
